# Optimizing a Trainium2 kernel written in Bass

```python
import math
import jax, jax.numpy as jnp
from jax import lax
import numpy as np

D_MODEL = 4096
BATCH = 4
SEQ = 4096
DEPTH = 1

N_META = 16
EPS = 1e-6
D_SSM = 1024
SSM_GROUP = 16
N_GROUPS = D_SSM // SSM_GROUP
STATE = 64
DT_MIN = 1e-3
DT_MAX = 1e-1
N_HEADS = 16
HEAD_DIM = 64
V_DIM = 2 * HEAD_DIM
QK_W = N_HEADS * 2 * HEAD_DIM
D_ATT = N_HEADS * V_DIM
Q_BLOCK = 128
PAD = (-N_META) % Q_BLOCK
N_BUCKETS = 32
MAX_DISTANCE = 128
NEG_INF = -1e30
D_FF = -(-8 * D_MODEL // (3 * 256)) * 256
IN_SIZES = (D_SSM, QK_W, QK_W, D_ATT, D_MODEL, D_MODEL)
IN_SPLITS = tuple(int(s) for s in np.cumsum(IN_SIZES)[:-1])
D_IN = sum(IN_SIZES)

kernel_name = "hybrid_s5_diffattn_block"


def rms_norm(x, g):
    xf = x.astype(jnp.float32)
    y = xf * lax.rsqrt(jnp.mean(xf * xf, axis=-1, keepdims=True) + EPS)
    return (y * g.astype(jnp.float32)).astype(x.dtype)


def t5_bucket(n):
    max_exact = N_BUCKETS // 2
    nf = jnp.maximum(n, max_exact).astype(jnp.float32)
    log_b = max_exact + (jnp.log(nf / max_exact) / math.log(MAX_DISTANCE / max_exact)
                         * (N_BUCKETS - max_exact)).astype(jnp.int32)
    return jnp.where(n < max_exact, n, jnp.minimum(log_b, N_BUCKETS - 1))


def _ssm_combine(e1, e2):
    a1, b1 = e1
    a2, b2 = e2
    return a1 * a2, a2 * b1 + b2


def s5_branch(u, lam_re, lam_im, log_dt, b_re, b_im, c_re, c_im, d_skip, w_glu, b_glu):
    f32 = jnp.float32
    bsz, seq_len, _ = u.shape
    lam = lax.complex(lam_re.astype(f32), lam_im.astype(f32))
    dt = jnp.exp(log_dt.astype(f32))[:, None]
    a_bar = jnp.exp(lam * dt)
    b_bar = ((a_bar - 1.0) / lam)[..., None] * lax.complex(b_re.astype(f32), b_im.astype(f32))
    c_mat = lax.complex(c_re.astype(f32), c_im.astype(f32))
    uf = u.astype(f32)
    u_g = uf.reshape(bsz, seq_len, N_GROUPS, SSM_GROUP).astype(jnp.complex64)
    bu = jnp.einsum('gpc,blgc->lbgp', b_bar, u_g)
    a_seq = jnp.broadcast_to(a_bar, (seq_len, 1, N_GROUPS, STATE))
    _, states = lax.associative_scan(_ssm_combine, (a_seq, bu), axis=0)
    y = jnp.real(jnp.einsum('gcp,lbgp->blgc', c_mat, states)).reshape(bsz, seq_len, D_SSM)
    y = y + d_skip.astype(f32) * uf
    g = jax.nn.gelu(y)
    out = g * jax.nn.sigmoid(g @ w_glu.astype(f32) + b_glu.astype(f32))
    return out.astype(u.dtype)


def diff_attention(q, k, v, lam, rel_bias):
    bsz, seq_len = q.shape[:2]
    lp = seq_len + PAD
    n_blocks = lp // Q_BLOCK
    pad = ((0, 0), (PAD, 0), (0, 0), (0, 0), (0, 0))
    q = jnp.pad(q, pad)
    k = jnp.pad(k, pad)
    v = jnp.pad(v, pad[:-1])
    q_blocks = jnp.moveaxis(q.reshape(bsz, n_blocks, Q_BLOCK, N_HEADS, 2, HEAD_DIM), 1, 0)
    key_pos = jnp.arange(lp)
    bias_hd = rel_bias[t5_bucket(key_pos)].T.astype(jnp.float32)
    scale = HEAD_DIM ** -0.5

    def block(args):
        q_blk, blk = args
        q_pos = blk * Q_BLOCK + jnp.arange(Q_BLOCK)
        dist = q_pos[:, None] - key_pos[None, :]
        mask = (dist >= 0) & (key_pos[None, :] >= PAD)
        bias = bias_hd[:, jnp.maximum(dist, 0)]
        s = jnp.einsum('bqhmd,bkhmd->bhmqk', q_blk, k).astype(jnp.float32) * scale + bias[None, :, None]
        p = jax.nn.softmax(jnp.where(mask, s, NEG_INF), axis=-1)
        attn = p[:, :, 0] - lam * p[:, :, 1]
        return jnp.einsum('bhqk,bkhe->bqhe', attn.astype(v.dtype), v)

    out = lax.map(block, (q_blocks, jnp.arange(n_blocks)))
    out = jnp.moveaxis(out, 0, 1).reshape(bsz, lp, N_HEADS, V_DIM)
    return out[:, PAD:]


def setup_inputs(seed: int = 0) -> dict:
    key = jax.random.key(seed)
    ks = jax.random.split(key, 32)
    f32 = jnp.float32

    def nrm(k, shape, scale):
        return jax.random.normal(k, shape, f32) * scale

    def gain(k, shape):
        return 1.0 + 0.01 * jax.random.normal(k, shape, f32)

    lam_im = jnp.pi * jnp.arange(STATE, dtype=f32)[None, None, :] + 0.01 * jax.random.normal(ks[13], (DEPTH, N_GROUPS, STATE), f32)
    w_branch = jnp.concatenate([nrm(ks[22], (DEPTH, D_SSM, D_MODEL), D_SSM ** -0.5),
                                nrm(ks[23], (DEPTH, D_ATT, D_MODEL), D_ATT ** -0.5)], axis=1)
    return {
        "x": nrm(ks[0], (BATCH, SEQ, D_MODEL), 1.0),
        "meta_tokens": nrm(ks[1], (N_META, D_MODEL), 1.0),
        "rel_bias": nrm(ks[2], (N_BUCKETS, N_HEADS), 0.2),
        "ln1_g": gain(ks[3], (DEPTH, D_MODEL)),
        "w_in": nrm(ks[4], (DEPTH, D_MODEL, D_IN), D_MODEL ** -0.5),
        "q_norm_g": gain(ks[5], (DEPTH, HEAD_DIM)),
        "k_norm_g": gain(ks[6], (DEPTH, HEAD_DIM)),
        "lam_q1": nrm(ks[7], (DEPTH, HEAD_DIM), 0.1),
        "lam_k1": nrm(ks[8], (DEPTH, HEAD_DIM), 0.1),
        "lam_q2": nrm(ks[9], (DEPTH, HEAD_DIM), 0.1),
        "lam_k2": nrm(ks[10], (DEPTH, HEAD_DIM), 0.1),
        "subln_g": gain(ks[11], (DEPTH, V_DIM)),
        "lam_re": -0.5 + 0.01 * jax.random.normal(ks[12], (DEPTH, N_GROUPS, STATE), f32),
        "lam_im": lam_im,
        "log_dt": jax.random.uniform(ks[14], (DEPTH, N_GROUPS), f32, math.log(DT_MIN), math.log(DT_MAX)),
        "b_re": nrm(ks[15], (DEPTH, N_GROUPS, STATE, SSM_GROUP), (2 * SSM_GROUP) ** -0.5),
        "b_im": nrm(ks[16], (DEPTH, N_GROUPS, STATE, SSM_GROUP), (2 * SSM_GROUP) ** -0.5),
        "c_re": nrm(ks[17], (DEPTH, N_GROUPS, SSM_GROUP, STATE), STATE ** -0.5),
        "c_im": nrm(ks[18], (DEPTH, N_GROUPS, SSM_GROUP, STATE), STATE ** -0.5),
        "d_skip": nrm(ks[19], (DEPTH, D_SSM), 1.0),
        "w_glu": nrm(ks[20], (DEPTH, D_SSM, D_SSM), D_SSM ** -0.5),
        "b_glu": nrm(ks[21], (DEPTH, D_SSM), 0.01),
        "w_branch": w_branch,
        "w_o": nrm(ks[24], (DEPTH, D_MODEL, D_MODEL), D_MODEL ** -0.5),
        "ln2_g": gain(ks[25], (DEPTH, D_MODEL)),
        "w_gate_up": nrm(ks[26], (DEPTH, D_MODEL, 2 * D_FF), D_MODEL ** -0.5),
        "w_down": nrm(ks[27], (DEPTH, D_FF, D_MODEL), D_FF ** -0.5),
    }


def reference(x, meta_tokens, rel_bias, ln1_g, w_in, q_norm_g, k_norm_g, lam_q1, lam_k1, lam_q2, lam_k2,
              subln_g, lam_re, lam_im, log_dt, b_re, b_im, c_re, c_im, d_skip, w_glu, b_glu,
              w_branch, w_o, ln2_g, w_gate_up, w_down):
    f32 = jnp.float32
    bsz = x.shape[0]
    h = jnp.concatenate([jnp.broadcast_to(meta_tokens[None].astype(x.dtype), (bsz, N_META, D_MODEL)), x], axis=1)
    seq_len = h.shape[1]
    for l in range(DEPTH):
        lambda_init = 0.8 - 0.6 * math.exp(-0.3 * l)
        hn = rms_norm(h, ln1_g[l])
        proj = hn @ w_in[l]
        u, q, k, v, g_ssm, g_att = jnp.split(proj, IN_SPLITS, axis=-1)
        q = rms_norm(q.reshape(bsz, seq_len, N_HEADS, 2, HEAD_DIM), q_norm_g[l])
        k = rms_norm(k.reshape(bsz, seq_len, N_HEADS, 2, HEAD_DIM), k_norm_g[l])
        v = v.reshape(bsz, seq_len, N_HEADS, V_DIM)
        lam = (jnp.exp(jnp.sum(lam_q1[l].astype(f32) * lam_k1[l].astype(f32)))
               - jnp.exp(jnp.sum(lam_q2[l].astype(f32) * lam_k2[l].astype(f32))) + lambda_init)
        o = diff_attention(q, k, v, lam, rel_bias)
        y_att = (rms_norm(o, subln_g[l]) * (1.0 - lambda_init)).reshape(bsz, seq_len, D_ATT)
        y_ssm = s5_branch(u, lam_re[l], lam_im[l], log_dt[l], b_re[l], b_im[l], c_re[l], c_im[l],
                          d_skip[l], w_glu[l], b_glu[l])
        y_a = y_ssm @ w_branch[l, :D_SSM]
        y_b = y_att @ w_branch[l, D_SSM:]
        merged = jax.nn.sigmoid(g_ssm) * y_a + jax.nn.sigmoid(g_att) * y_b
        h = h + merged @ w_o[l]
        hn = rms_norm(h, ln2_g[l])
        gate, up = jnp.split(hn @ w_gate_up[l], 2, axis=-1)
        h = h + (jax.nn.silu(gate) * up) @ w_down[l]
    return h[:, N_META:]
```

```python
import math
import os
from contextlib import ExitStack

import numpy as np
import concourse.bass as bass
import concourse.mybir as mybir
from concourse.bass_utils import run_bass_kernel_spmd

F32 = mybir.dt.float32
BF16 = mybir.dt.bfloat16
AF = mybir.ActivationFunctionType
ALU = mybir.AluOpType
AX = mybir.AxisListType
NEG = -30000.0
EPS = 1e-6
PI = math.pi
TWO_PI = 2.0 * math.pi
N_META = 16
PADF = 112
FL = 384


class Cfg:
    def __init__(self, D=4096, DS=1024, H=16, DFF=11008, NPRE=16, NOWN=17, NSUB=2, SMAX=1152, depth_l=0):
        self.D, self.DS, self.H, self.DFF = D, DS, H, DFF
        self.NPRE, self.NOWN, self.NSUB, self.SMAX = NPRE, NOWN, NSUB, SMAX
        self.G = DS // 16
        self.QKW = H * 128
        self.DATT = H * 128
        self.DIN = DS + 2 * self.QKW + self.DATT + 2 * D
        self.cU, self.cQ = 0, DS
        self.cK = DS + self.QKW
        self.cV = DS + 2 * self.QKW
        self.cGS = self.cV + self.DATT
        self.cGA = self.cGS + D
        self.NCTX = NPRE + NOWN
        self.NT = self.NCTX * 128
        self.NO = NOWN * 128
        self.O0 = NPRE * 128
        self.KD = D // 128
        self.KS = DS // 128
        self.KF = DFF // 128
        self.MB = NSUB * 128
        self.lambda_init = 0.8 - 0.6 * math.exp(-0.3 * depth_l)


def tok_tiles(nblocks, maxb=4):
    nt = -(-nblocks // maxb)
    base, rem = divmod(nblocks, nt)
    out, s = [], 0
    for i in range(nt):
        w = base + (1 if i < rem else 0)
        out.append((s * 128, w * 128))
        s += w
    return out


def super_tiles(tiles, smax):
    out, cur, tot = [], [], 0
    for t in tiles:
        if cur and tot + t[1] > smax:
            out.append(cur)
            cur, tot = [], 0
        cur.append(t)
        tot += t[1]
    if cur:
        out.append(cur)
    return out


class Buf:
    __slots__ = ("name", "w", "r", "slot", "gen")

    def __init__(self, name):
        self.name, self.w, self.r, self.slot, self.gen = name, [], [], None, -1


class Prog:
    ENG = ("sync", "act", "pool", "dve", "pe")
    EPOCH = 30000
    DLIM = 3000

    def __init__(self):
        self.ops = {e: [] for e in self.ENG}
        self.cnt = {e: 0 for e in self.ENG}
        self.seen = {e: {} for e in self.ENG}
        self.nb = 0
        self.semkeys, self.semset = [], set()
        self.dlast = {}
        self.gen = 0
        self.slots = []
        self.nfree = 0

    def buf(self, name="b"):
        self.nb += 1
        return Buf("%s#%d" % (name, self.nb))

    def bufs(self, n, name="b"):
        return [self.buf(name) for _ in range(n)]

    def _sem(self, sk):
        if sk not in self.semset:
            self.semset.add(sk)
            self.semkeys.append(sk)

    def _need(self, eng, ev, waits):
        k, v = ev
        if k[0] == "e" and k[1] == "pe" and eng == "pe":
            return
        if self.seen[eng].get(k, 0) >= v:
            return
        if waits.get(k, 0) < v:
            waits[k] = v

    def _commit(self, eng, waits):
        wl = []
        for k, v in waits.items():
            self.seen[eng][k] = v
            if k[0] == "e":
                sk = ("e", k[1], (v - 1) // self.EPOCH)
                val = (v - 1) % self.EPOCH + 1
            else:
                sk, val = k, v
            self._sem(sk)
            wl.append((sk, val))
        return wl

    @staticmethod
    def _add(lst, ev):
        return [x for x in lst if x[0] != ev[0]] + [ev]

    def op(self, eng, name, kw, R=(), W=()):
        waits = {}
        for b in R:
            for ev in b.w:
                self._need(eng, ev, waits)
        for b in W:
            for ev in b.w:
                self._need(eng, ev, waits)
            for ev in b.r:
                self._need(eng, ev, waits)
        wl = self._commit(eng, waits)
        self.cnt[eng] += 1
        c = self.cnt[eng]
        ev = (("e", eng), c)
        sk = ("e", eng, (c - 1) // self.EPOCH)
        self._sem(sk)
        self.ops[eng].append((wl, name, kw, sk, 1))
        for b in W:
            b.w = [ev]
            b.r = []
        for b in R:
            if b not in W:
                b.r = self._add(b.r, ev)

    def dma(self, q, out, in_, sb, dr=None, store=False, partial=False, **kw):
        waits = {}
        if sb.gen != self.gen:
            if self.nfree >= len(self.slots):
                self.slots.append([0, 0])
            sb.slot = self.nfree
            self.nfree += 1
            sb.gen = self.gen
            sl = self.slots[sb.slot]
            if sl[1] >= self.DLIM:
                sl[0] += 1
                sl[1] = 0
        sl = self.slots[sb.slot]
        key = ("d", sb.slot, sl[0])
        if store:
            for ev in sb.w:
                self._need(q, ev, waits)
        else:
            if dr is not None:
                for ev in dr.w:
                    self._need(q, ev, waits)
            for ev in sb.w:
                if partial and ev[0] == key:
                    continue
                self._need(q, ev, waits)
            for ev in sb.r:
                self._need(q, ev, waits)
        wl = self._commit(q, waits)
        sl[1] += 1
        ev = (key, 16 * sl[1])
        self._sem(key)
        self.dlast[key] = 16 * sl[1]
        d = dict(out=out, in_=in_)
        d.update(kw)
        self.ops[q].append((wl, "dma_start", d, key, 16))
        if store:
            sb.r = self._add(sb.r, ev)
            if dr is not None:
                dr.w = self._add(dr.w, ev)
        else:
            if partial:
                sb.w = self._add(sb.w, ev)
            else:
                sb.w = [ev]
                sb.r = []
            if dr is not None:
                dr.r = self._add(dr.r, ev)

    def fence(self, q, b):
        waits = {}
        for ev in b.w:
            self._need(q, ev, waits)
        wl = self._commit(q, waits)
        if wl:
            self.ops[q].append((wl, None, None, None, 0))

    def barrier(self, allbufs=()):
        evs = [(("e", e), self.cnt[e]) for e in self.ENG if self.cnt[e] > 0]
        evs += [(k, v) for k, v in self.dlast.items()]
        for eng in self.ENG:
            waits = {}
            for ev in evs:
                self._need(eng, ev, waits)
            wl = self._commit(eng, waits)
            if wl:
                self.ops[eng].append((wl, None, None, None, 0))
        self.gen += 1
        self.nfree = 0

    def emit(self, nc, es):
        sems = {}
        for i, sk in enumerate(self.semkeys):
            sems[sk] = es.enter_context(nc.semaphore("s%d" % i))
        block = es.enter_context(nc.Block())
        if os.environ.get("KDUMP"):
            for e in self.ENG:
                ops = self.ops[e]
                idx = [i for i, o in enumerate(ops) if o[1] is None]
                st = idx[-int(os.environ.get("KDB", "3"))]
                print("ENGINE", e, "total", len(ops), "from", st)
                for o in ops[st:st + int(os.environ["KDUMP"])]:
                    print("   waits", o[0], "op", o[1], "inc", o[3], o[4])

        def run(engname):
            def f(e):
                for (wl, name, kw, sk, inc) in self.ops[engname]:
                    for (wk, val) in wl:
                        e.wait_ge(sems[wk], val)
                    if name is None:
                        continue
                    ins = getattr(e, name)(**kw)
                    ins.then_inc(sems[sk], inc)
            return f

        block.sync(run("sync"))
        block.scalar(run("act"))
        block.gpsimd(run("pool"))
        block.vector(run("dve"))
        block.tensor(run("pe"))


class Arena:
    def __init__(self, big, nwords):
        self.big, self.n, self.off = big, nwords, 0

    def f32(self, n):
        a = self.off
        self.off += n
        assert self.off <= self.n, ("arena overflow", self.off, self.n)
        return self.big[:, a:a + n]

    def bf16(self, n):
        w = (n + 1) // 2
        a = self.off
        self.off += w
        assert self.off <= self.n, ("arena overflow", self.off, self.n)
        return self.big[:, a:a + w].bitcast(BF16)[:, 0:n]


class Rot:
    def __init__(self, items):
        self.items, self.i = items, 0

    def next(self):
        it = self.items[self.i % len(self.items)]
        self.i += 1
        return it


class Builder:
    def __init__(self, cfg, debug=False, stop_after=99):
        self.c = cfg
        self.debug = debug
        self.stop_after = stop_after

    def act(self, out, in_, func, R, W, bias=None, scale=None, accum_out=None):
        kw = dict(out=out, in_=in_, func=func)
        if bias is not None:
            kw["bias"] = bias
        if scale is not None:
            kw["scale"] = scale
        if accum_out is not None:
            kw["accum_out"] = accum_out
        self.p.op("act", "activation", kw, R, W)

    def ts(self, eng, out, in0, s1, s2, op0, op1, R, W):
        kw = dict(out=out, in0=in0, scalar1=s1, scalar2=s2, op0=op0)
        if op1 is not None:
            kw["op1"] = op1
        self.p.op(eng, "tensor_scalar", kw, R, W)

    def tt(self, eng, out, in0, in1, op, R, W):
        self.p.op(eng, "tensor_tensor", dict(out=out, in0=in0, in1=in1, op=op), R, W)

    def stt(self, out, in0, scalar, in1, op0, op1, R, W):
        self.p.op("dve", "scalar_tensor_tensor", dict(out=out, in0=in0, scalar=scalar, in1=in1, op0=op0, op1=op1), R, W)

    def copy(self, eng, out, in_, R, W):
        if eng == "act":
            self.act(out, in_, AF.Copy, R, W)
        else:
            self.p.op(eng, "tensor_copy", dict(out=out, in_=in_), R, W)

    def memset(self, eng, ap, val, W):
        self.p.op(eng, "memset", dict(ap=ap, constant=val), (), W)

    def mm(self, out, lhsT, rhs, start, stop, R, W):
        self.p.op("pe", "matmul", dict(out=out, lhsT=lhsT, rhs=rhs, start=start, stop=stop), R, W)

    def tr(self, out, in_, R, W):
        self.p.op("pe", "transpose", dict(out=out, in_=in_, identity=self.ident), list(R) + [self.b_const], W)

    def cut(self):
        self.ncut = getattr(self, "ncut", 0) + 1
        return self.ncut >= int(os.environ.get("KCUT2", "999"))

    def flush_pe(self):
        for f in self.pe_def:
            f()
        self.pe_def = []

    def build(self):
        c = self.c
        nc = bass.Bass("TRN2", target_bir_lowering=False)
        self.nc = nc
        p = Prog()
        self.p = p
        self.pe_def = []
        IN = lambda name, shape: nc.dram_tensor(name, list(shape), F32, kind="ExternalInput").ap()
        skind = "ExternalOutput" if self.debug else "Internal"
        SC = lambda name, shape, dt: nc.dram_tensor(name, list(shape), dt, kind=skind).ap()
        d = {}
        d["hp"] = IN("hp", (c.NT, c.D))
        d["kvalid"] = IN("kvalid", (128, c.NCTX))
        d["ohc"] = IN("ohc", (32, FL))
        d["negmask"] = IN("negmask", (c.H, FL))
        d["rel_bias"] = IN("rel_bias", (32, c.H))
        d["ln1_g"] = IN("ln1_g", (c.D,))
        d["ln2_g"] = IN("ln2_g", (c.D,))
        d["w_in"] = IN("w_in", (c.D, c.DIN))
        for n in ("q_norm_g", "k_norm_g", "lam_q1", "lam_k1", "lam_q2", "lam_k2"):
            d[n] = IN(n, (64,))
        d["subln_g"] = IN("subln_g", (128,))
        d["lam_re"] = IN("lam_re", (c.G, 64))
        d["lam_im"] = IN("lam_im", (c.G, 64))
        d["log_dt"] = IN("log_dt", (c.G,))
        d["b_re"] = IN("b_re", (c.G, 64, 16))
        d["b_im"] = IN("b_im", (c.G, 64, 16))
        d["c_re"] = IN("c_re", (c.G * 16, 64))
        d["c_im"] = IN("c_im", (c.G * 16, 64))
        d["d_skip"] = IN("d_skip", (c.DS,))
        d["w_glu"] = IN("w_glu", (c.DS, c.DS))
        d["b_glu"] = IN("b_glu", (c.DS,))
        d["w_branch"] = IN("w_branch", (c.DS + c.DATT, c.D))
        d["w_o"] = IN("w_o", (c.D, c.D))
        d["w_gate_up"] = IN("w_gate_up", (c.D, 2 * c.DFF))
        d["w_down"] = IN("w_down", (c.DFF, c.D))
        self.d = d
        s = {}
        s["uT"] = SC("uT", (c.DS, c.NT), F32)
        s["qT"] = SC("qT", (c.H, 128, c.NO), BF16)
        s["kT"] = SC("kT", (c.H, 128, c.NT), BF16)
        s["v"] = SC("v", (c.NT, c.DATT), BF16)
        s["sgs"] = SC("sgs", (c.D, c.NO), BF16)
        s["sga"] = SC("sga", (c.D, c.NO), BF16)
        s["hT"] = SC("hT", (c.D, c.NO), F32)
        s["yssmT"] = SC("yssmT", (c.DS, c.NO), BF16)
        s["yattT"] = SC("yattT", (c.DATT, c.NO), BF16)
        s["h2T"] = SC("h2T", (c.D, c.NO), F32)
        s["a2T"] = SC("a2T", (c.D, c.NO), BF16)
        s["rstd2"] = SC("rstd2", (128, c.NO), F32)
        s["act3T"] = SC("act3T", (c.DFF, c.NO), BF16)
        s["rrep"] = nc.dram_tensor("rrep", [c.H, 128, FL], F32, kind="Internal").ap()
        self.s = s
        self.sb = {k: p.buf("dram_" + k) for k in s}
        self.out = nc.dram_tensor("out", [c.NO, c.D], F32, kind="ExternalOutput").ap()
        self.b_out = p.buf("dram_out")

        NW = 47616
        with ExitStack() as es:
            big = es.enter_context(nc.sbuf_tensor("arena", [128, NW], F32))
            self.A = Arena(big, NW)
            self.ps = [es.enter_context(nc.psum_tensor("ps%d" % i, [128, 512], F32)) for i in range(8)]
            self.psb = [p.buf("ps%d" % i) for i in range(8)]
            self.phase0()
            phases = [self.phase1, self.phase2, self.phase3, self.phase4, self.phase5, self.phase6]
            for i, ph in enumerate(phases):
                if i + 1 > self.stop_after:
                    break
                self.A.off = self.A0
                ph()
                self.flush_pe()
                p.barrier()
            if self.stop_after < 6:
                t = self.A.f32(128)
                b = p.buf("dummy")
                self.memset("dve", t, 0.0, [b])
                p.dma("sync", self.out[0:128, 0:128], t, b, self.b_out, store=True)
            p.barrier()
            print("ops:", {e: len(p.ops[e]) for e in p.ENG}, "sems:", len(p.semkeys), flush=True)
            p.emit(nc, es)
        return nc

    def phase0(self):
        c, p, A, d = self.c, self.p, self.A, self.d
        bc = p.buf("const")
        self.b_const = bc
        self.ident = A.f32(128)
        self.onesf = A.f32(128)
        self.onesb = A.bf16(128)
        self.blk = A.bf16(128)
        p.op("pool", "iota", dict(out=self.ident, pattern=[[1, 128]], base=0, channel_multiplier=-1,
                                  allow_small_or_imprecise_dtypes=True), (), [bc])
        p.op("dve", "tensor_single_scalar", dict(out=self.ident, in_=self.ident, scalar=0.0, op=ALU.is_equal), [bc], [bc])
        self.memset("dve", self.onesf, 1.0, [bc])
        self.memset("dve", self.onesb, 1.0, [bc])
        self.memset("dve", self.blk, 0.0, [bc])
        self.memset("dve", self.blk[0:64, 0:64], 1.0, [bc])
        self.memset("dve", self.blk[64:128, 64:128], 1.0, [bc])
        self.s1 = A.f32(1)
        self.s2 = A.f32(1)
        self.negpi = A.f32(1)
        for (ap, top, bot) in ((self.s1, -1.0, 1.0), (self.s2, 1.0, -1.0)):
            self.memset("dve", ap[0:64, :], top, [bc])
            self.memset("dve", ap[64:128, :], bot, [bc])
        self.memset("dve", self.negpi, -PI, [bc])
        self.epsc = A.f32(1)
        self.halfpi = A.f32(1)
        self.memset("dve", self.epsc, EPS, [bc])
        self.memset("dve", self.halfpi, PI / 2, [bc])
        self.g1col = A.f32(c.KD)
        self.g2col = A.f32(c.KD)
        bl = p.buf("cload")
        p.dma("sync", self.g1col, d["ln1_g"].rearrange("(c p) -> p c", p=128), bl, partial=True, allow_slow_non_contiguous=True)
        p.dma("sync", self.g2col, d["ln2_g"].rearrange("(c p) -> p c", p=128), bl, partial=True, allow_slow_non_contiguous=True)
        self.gq2 = A.f32(1)
        self.gk2 = A.f32(1)
        for (ap, nm) in ((self.gq2, "q_norm_g"), (self.gk2, "k_norm_g")):
            src = d[nm].rearrange("(p o) -> p o", o=1)
            p.dma("sync", ap[0:64, :], src, bl, partial=True, allow_slow_non_contiguous=True)
            p.dma("sync", ap[64:128, :], src, bl, partial=True, allow_slow_non_contiguous=True)
        if int(os.environ.get("KCUT", "99")) <= 1:
            self.A0 = A.off
            return
        self.slng = A.f32(1)
        p.dma("sync", self.slng, d["subln_g"].rearrange("(p o) -> p o", o=1), bl, partial=True, allow_slow_non_contiguous=True)
        self.kvalid = A.f32(c.NCTX)
        p.dma("sync", self.kvalid, d["kvalid"], bl, partial=True)
        self.rel31 = A.f32(c.H)
        p.dma("sync", self.rel31, d["rel_bias"][31, :].partition_broadcast(128), bl, partial=True)
        lam4 = A.f32(256)
        for i, nm in enumerate(("lam_q1", "lam_k1", "lam_q2", "lam_k2")):
            p.dma("sync", lam4[:, i * 64:(i + 1) * 64], d[nm].partition_broadcast(128), bl, partial=True)
        relb = A.f32(c.H)
        ohc = A.f32(FL)
        ngm = A.f32(FL)
        p.dma("sync", relb[0:32, :], d["rel_bias"], bl, partial=True)
        p.dma("sync", ohc[0:32, :], d["ohc"], bl, partial=True)
        p.dma("sync", ngm[0:c.H, :], d["negmask"], bl, partial=True)
        if int(os.environ.get("KCUT", "99")) <= 2:
            self.A0 = A.off
            return
        self.ts("dve", self.slng, self.slng, 1.0 - c.lambda_init, None, ALU.mult, None, [bl], [bl])
        lt = A.f32(128)
        l2 = A.f32(2)
        self.neglam = A.f32(1)
        self.tt("dve", lt[:, 0:64], lam4[:, 0:64], lam4[:, 64:128], ALU.mult, [bl], [bc])
        self.tt("dve", lt[:, 64:128], lam4[:, 128:192], lam4[:, 192:256], ALU.mult, [bl], [bc])
        p.op("dve", "tensor_reduce", dict(out=l2[:, 0:1], in_=lt[:, 0:64], axis=AX.X, op=ALU.add), [bc], [bc])
        p.op("dve", "tensor_reduce", dict(out=l2[:, 1:2], in_=lt[:, 64:128], axis=AX.X, op=ALU.add), [bc], [bc])
        self.act(l2, l2, AF.Exp, [bc], [bc])
        self.tt("dve", self.neglam, l2[:, 1:2], l2[:, 0:1], ALU.subtract, [bc], [bc])
        self.ts("dve", self.neglam, self.neglam, -c.lambda_init, None, ALU.add, None, [bc], [bc])
        if int(os.environ.get("KCUT", "99")) <= 3:
            self.A0 = A.off
            return
        frow = A.f32(FL)
        self.mm(self.ps[0][0:c.H, 0:FL], relb[0:32, 0:c.H], ohc[0:32, :], True, True, [bl], [self.psb[0]])
        self.tt("dve", frow[0:c.H, :], self.ps[0][0:c.H, 0:FL], ngm[0:c.H, :], ALU.add, [self.psb[0], bl], [bc])
        if int(os.environ.get("KCUT", "99")) <= 4:
            self.A0 = A.off
            return
        brr = p.buf("rrepst")
        bdr = self.p.buf("dram_rrep")
        src = frow[0:c.H, :].unsqueeze(1).broadcast_to([c.H, 128, FL])
        p.dma("sync", self.s["rrep"], src, bc, bdr, store=True)
        self.TD = A.f32(c.H * 128)
        self.TP = A.f32(c.H * 128)
        bt = p.buf("toep")
        rt = self.s["rrep"].tensor
        p.dma("sync", self.TD.rearrange("p (h q) -> p h q", q=128),
              bass.AP(tensor=rt, offset=127, ap=[[FL - 1, 128], [128 * FL, c.H], [1, 128]]), bt, bdr, partial=True)
        p.dma("sync", self.TP.rearrange("p (h q) -> p h q", q=128),
              bass.AP(tensor=rt, offset=255, ap=[[FL - 1, 128], [128 * FL, c.H], [1, 128]]), bt, bdr, partial=True)
        self.b_toep = bt
        self.b_cload = bl
        self.A0 = A.off
        p.barrier()

    def ctx_supers(self):
        c = self.c
        pre = [(t0, tw, False) for (t0, tw) in tok_tiles(c.NPRE)]
        own = [(c.O0 + t0, tw, True) for (t0, tw) in tok_tiles(c.NOWN)]
        sup = []
        for grp in super_tiles([(a, b) for (a, b, _) in pre], c.SMAX):
            sup.append((grp, False))
        for grp in super_tiles([(a, b) for (a, b, _) in own], c.SMAX):
            sup.append((grp, True))
        return sup

    def own_supers(self):
        c = self.c
        return super_tiles(tok_tiles(c.NOWN), c.SMAX)

    def phase1(self):
        c, p, A, d, s = self.c, self.p, self.A, self.d, self.s
        KD = c.KD
        for (grp, own) in self.ctx_supers():
            A.off = self.A0
            S0 = grp[0][0]
            S = sum(t[1] for t in grp)
            a1T = A.bf16(KD * S).rearrange("p (c t) -> p c t", t=S)
            b_a1 = p.buf("a1T")
            mark = A.off
            xin = [A.f32(c.D) for _ in range(2)]
            b_x = p.bufs(2, "xin")
            junk = A.bf16(c.D)
            b_j = p.buf("junk")
            hst2 = A.f32(c.D)
            hst = hst2.rearrange("p (c t) -> p c t", t=128)
            b_h = p.buf("hst")
            ssq = A.f32(2)
            rs = A.f32(2)
            b_s = p.bufs(2, "ssq")
            diag = [A.f32(128) for _ in range(2)]
            rrep = [A.f32(128) for _ in range(2)]
            b_r = p.bufs(2, "rrep")
            trb = Rot([5, 6, 7])
            for bi in range(S // 128):
                t0 = S0 + bi * 128
                j = bi % 2
                p.dma("sync", xin[j], d["hp"][t0:t0 + 128, :], b_x[j])
                self.act(junk, xin[j], AF.Square, [b_x[j]], [b_j, b_s[j]], accum_out=ssq[:, j:j + 1])
                self.act(rs[:, j:j + 1], ssq[:, j:j + 1], AF.Sqrt, [b_s[j], self.b_const], [b_r[j]], bias=self.epsc[:, 0:1], scale=1.0 / c.D)
                p.op("dve", "reciprocal", dict(out=rs[:, j:j + 1], in_=rs[:, j:j + 1]), [b_r[j]], [b_r[j]])
                self.ts("dve", diag[j], self.ident, rs[:, j:j + 1], None, ALU.mult, None, [b_r[j], self.b_const], [b_r[j]])
                self.mm(self.ps[4][:, 0:128], self.onesf, diag[j], True, True, [b_r[j], self.b_const], [self.psb[4]])
                self.copy("act", rrep[j], self.ps[4][:, 0:128], [self.psb[4]], [b_r[j]])
                for c0 in range(0, KD, 4):
                    n4 = min(4, KD - c0)
                    bk = trb.next()
                    for q in range(n4):
                        self.tr(self.ps[bk][:, q * 128:(q + 1) * 128], xin[j][:, (c0 + q) * 128:(c0 + q + 1) * 128],
                                [b_x[j]], [self.psb[bk]])
                    for q in range(n4):
                        self.stt(a1T[:, c0 + q, bi * 128:(bi + 1) * 128], self.ps[bk][:, q * 128:(q + 1) * 128],
                                 self.g1col[:, c0 + q:c0 + q + 1], rrep[j], ALU.mult, ALU.mult,
                                 [self.psb[bk], b_r[j], self.b_cload], [b_a1])
                    if own and not os.environ.get("KNOC"):
                        self.copy("dve", hst2[:, c0 * 128:(c0 + n4) * 128], self.ps[bk][:, 0:n4 * 128],
                                  [self.psb[bk]], [b_h])
                if own and not os.environ.get("KNOH"):
                    to = t0 - c.O0
                    p.dma("sync", s["hT"][:, to:to + 128].rearrange("(c p) t -> p c t", p=128), hst, b_h, self.sb["hT"], store=True)
            p.barrier()
            if self.cut():
                return
            A.off = mark
            wbuf = [A.bf16(KD * c.MB).rearrange("p (c n) -> p c n", n=c.MB) for _ in range(2)]
            b_w = p.bufs(2, "wbuf")
            NST = 3
            stf = Rot([(A.f32(512), p.buf("stf")) for _ in range(NST)])
            stb = Rot([(A.bf16(512), p.buf("stb")) for _ in range(NST)])
            sqt = Rot([(A.bf16(512), p.buf("sqt")) for _ in range(2)])
            rqt = Rot([(A.f32(512), p.buf("rqt")) for _ in range(2)])
            mainb = Rot([0, 1, 2, 3, 4])
            normb = Rot([5, 6])
            regions = [("u", c.cU, c.DS), ("q", c.cQ, c.QKW), ("k", c.cK, c.QKW), ("v", c.cV, c.DATT),
                       ("gs", c.cGS, c.D), ("ga", c.cGA, c.D)]
            blocks = []
            for (typ, cs, cw) in regions:
                if not own and typ not in ("u", "k", "v"):
                    continue
                if typ not in os.environ.get("KREG", "u,q,k,v,gs,ga").split(","):
                    continue
                for c0 in range(cs, cs + cw, c.MB):
                    blocks.append((typ, cs, c0))
            wi = 0
            for (typ, cs, c0) in blocks:
                wj = wi % 2
                wi += 1
                p.dma("pool", wbuf[wj], d["w_in"][:, c0:c0 + c.MB].rearrange("(c p) n -> p c n", p=128), b_w[wj])
                if typ == "v":
                    for tb in range(S // 128):
                        bk = mainb.next()
                        for kc in range(KD):
                            self.mm(self.ps[bk][:, 0:c.MB], a1T[:, kc, tb * 128:(tb + 1) * 128], wbuf[wj][:, kc, :],
                                    kc == 0, kc == KD - 1, [b_a1, b_w[wj]], [self.psb[bk]])
                        self.flush_pe()
                        (st, bs) = stb.next()
                        self.copy("act", st[:, 0:c.MB], self.ps[bk][:, 0:c.MB], [self.psb[bk]], [bs])
                        t0 = S0 + tb * 128
                        p.dma("sync", s["v"][t0:t0 + 128, c0 - cs:c0 - cs + c.MB], st[:, 0:c.MB], bs, self.sb["v"], store=True)
                    continue
                for sub in range(c.NSUB):
                    m = (c0 - cs) // 128 + sub
                    tl = 0
                    for (t0, tw) in grp:
                        bk = mainb.next()
                        for kc in range(KD):
                            self.mm(self.ps[bk][:, 0:tw], wbuf[wj][:, kc, sub * 128:(sub + 1) * 128], a1T[:, kc, tl:tl + tw],
                                    kc == 0, kc == KD - 1, [b_a1, b_w[wj]], [self.psb[bk]])
                        self.flush_pe()
                        pm = self.ps[bk][:, 0:tw]
                        if typ == "u":
                            (st, bs) = stf.next()
                            self.copy("act", st[:, 0:tw], pm, [self.psb[bk]], [bs])
                            p.dma("sync", s["uT"][m * 128:(m + 1) * 128, t0:t0 + tw], st[:, 0:tw], bs, self.sb["uT"], store=True)
                        elif typ in ("q", "k"):
                            (sq, bsq) = sqt.next()
                            (rq, brq) = rqt.next()
                            (st, bs) = stb.next()
                            nb = normb.next()
                            gcol = self.gq2 if typ == "q" else self.gk2
                            self.act(sq[:, 0:tw], pm, AF.Square, [self.psb[bk]], [bsq])

                            def deferred(nb=nb, sq=sq, bsq=bsq, tw=tw, rq=rq, brq=brq, st=st, bs=bs, pm=pm, bk=bk,
                                         gcol=gcol, typ=typ, m=m, t0=t0):
                                self.mm(self.ps[nb][:, 0:tw], self.blk, sq[:, 0:tw], True, True, [bsq, self.b_const], [self.psb[nb]])
                                self.act(rq[:, 0:tw], self.ps[nb][:, 0:tw], AF.Sqrt, [self.psb[nb], self.b_const], [brq],
                                         bias=self.epsc[:, 0:1], scale=1.0 / 64)
                                p.op("dve", "reciprocal", dict(out=rq[:, 0:tw], in_=rq[:, 0:tw]), [brq], [brq])
                                self.stt(st[:, 0:tw], pm, gcol[:, 0:1], rq[:, 0:tw], ALU.mult, ALU.mult,
                                         [self.psb[bk], brq, self.b_cload], [bs])
                                if typ == "q":
                                    to = t0 - c.O0
                                    p.dma("sync", s["qT"][m, :, to:to + tw], st[:, 0:tw], bs, self.sb["qT"], store=True)
                                else:
                                    p.dma("sync", s["kT"][m, :, t0:t0 + tw], st[:, 0:tw], bs, self.sb["kT"], store=True)
                            self.pe_def.append(deferred)
                        else:
                            (st, bs) = stb.next()
                            self.act(st[:, 0:tw], pm, AF.Sigmoid, [self.psb[bk]], [bs])
                            to = t0 - c.O0
                            dst = s["sgs"] if typ == "gs" else s["sga"]
                            p.dma("sync", dst[m * 128:(m + 1) * 128, to:to + tw], st[:, 0:tw], bs,
                                  self.sb["sgs" if typ == "gs" else "sga"], store=True)
                        tl += tw
            self.flush_pe()
            p.barrier()
            if self.cut():
                return

    def rsqrt_ps(self, out, in_, scale, Rb, Wb):
        self.act(out, in_, AF.Sqrt, list(Rb) + [self.b_const], [Wb], bias=self.epsc[:, 0:1], scale=scale)
        self.p.op("dve", "reciprocal", dict(out=out, in_=out), [Wb], [Wb])

    def reduce_angle(self, eng, out, a, ni, Ra, Wn, Wo):
        self.ts(eng, ni, a, 1.0 / TWO_PI, None, ALU.mult, None, Ra, [Wn])
        self.copy(eng, out, ni, [Wn], [Wo])
        self.stt(out, out, -TWO_PI, a, ALU.mult, ALU.add, list(Ra) + [Wo], [Wo])
        self.ts(eng, out, out, -PI, PI, ALU.max, ALU.min, [Wo], [Wo])

    def phase2(self):
        c, p, A, d, s = self.c, self.p, self.A, self.d, self.s
        G, KS = c.G, c.KS
        I32 = mybir.dt.int32
        bs = p.buf("ssm_setup")
        bl = p.buf("ssm_load")
        ZB = A.f32(G * 16)
        ZsB = A.f32(G * 16)
        rr = A.f32(G)
        thr = A.f32(G)
        dsk = A.f32(KS)
        bgl = A.f32(KS)
        maskc = A.f32(8)
        iota = A.f32(512)
        chunks = [(t0, tw, False) for (t0, tw) in tok_tiles(c.NPRE)] + [(c.O0 + t0, tw, True) for (t0, tw) in tok_tiles(c.NOWN)]
        NCH = len(chunks)
        offs = A.f32(NCH * G)
        gT = A.bf16(KS * c.NO).rearrange("p (c t) -> p c t", t=c.NO)
        b_gT = p.buf("gT")
        Blz = A.bf16(8 * 128).rearrange("p (g m) -> p g m", m=128)
        Blzs = A.bf16(8 * 128).rearrange("p (g m) -> p g m", m=128)
        Cl1 = A.bf16(8 * 128).rearrange("p (g m) -> p g m", m=128)
        Cl2 = A.bf16(8 * 128).rearrange("p (g m) -> p g m", m=128)
        mark = A.off
        lre = A.f32(G)
        lim = A.f32(G)
        dtr = A.f32(G)
        for (ap, nm) in ((lre, "lam_re"), (lim, "lam_im")):
            src = d[nm].rearrange("g p -> p g")
            p.dma("sync", ap[0:64, :], src, bl, partial=True, allow_slow_non_contiguous=True)
            p.dma("sync", ap[64:128, :], src, bl, partial=True, allow_slow_non_contiguous=True)
        p.dma("sync", dtr, d["log_dt"].partition_broadcast(128), bl, partial=True)
        p.dma("sync", dsk, d["d_skip"].rearrange("(k p) -> p k", p=128), bl, partial=True, allow_slow_non_contiguous=True)
        p.dma("sync", bgl, d["b_glu"].rearrange("(k p) -> p k", p=128), bl, partial=True, allow_slow_non_contiguous=True)
        X1 = A.f32(G * 16)
        X2 = A.f32(G * 16)
        X13 = X1.rearrange("p (g c) -> p g c", c=16)
        X23 = X2.rearrange("p (g c) -> p g c", c=16)
        bre = d["b_re"].rearrange("g p c -> p g c")
        bim = d["b_im"].rearrange("g p c -> p g c")
        p.dma("sync", X13[0:64], bre, bl, partial=True)
        p.dma("sync", X13[64:128], bim, bl, partial=True)
        p.dma("sync", X23[0:64], bim, bl, partial=True)
        p.dma("sync", X23[64:128], bre, bl, partial=True)
        if os.environ.get("KS2") == "1":
            p.barrier()
            return
        self.act(dtr, dtr, AF.Exp, [bl], [bs])
        tmp = A.f32(G)
        self.tt("dve", tmp, lre, dtr, ALU.mult, [bl, bs], [bs])
        self.act(rr, tmp, AF.Exp, [bs], [bs])
        th = A.f32(G)
        self.tt("dve", th, lim, dtr, ALU.mult, [bl, bs], [bs])
        if os.environ.get("KS2") == "2":
            p.barrier()
            return
        ni = A.f32(G).bitcast(I32)
        self.reduce_angle("dve", thr, th, ni, [bs], bs, bs)
        if os.environ.get("KS2") == "3":
            p.barrier()
            return
        sn = A.f32(G)
        cs = A.f32(G)
        ab = A.f32(G)
        self.act(sn, thr, AF.Sin, [bs], [bs])
        self.ts("dve", ab, thr, -1.0, None, ALU.mult, None, [bs], [bs])
        self.tt("dve", ab, ab, thr, ALU.max, [bs], [bs])
        self.act(cs, ab, AF.Sin, [bs, self.b_const], [bs], bias=self.halfpi[:, 0:1], scale=-1.0)
        ar1 = A.f32(G)
        ai = A.f32(G)
        self.tt("dve", ar1, rr, cs, ALU.mult, [bs], [bs])
        self.ts("dve", ar1, ar1, -1.0, None, ALU.add, None, [bs], [bs])
        self.tt("dve", ai, rr, sn, ALU.mult, [bs], [bs])
        if os.environ.get("KS2") == "4":
            p.barrier()
            return
        den = A.f32(G)
        t2 = A.f32(G)
        self.tt("dve", den, lre, lre, ALU.mult, [bl], [bs])
        self.tt("dve", t2, lim, lim, ALU.mult, [bl], [bs])
        self.tt("dve", den, den, t2, ALU.add, [bs], [bs])
        p.op("dve", "reciprocal", dict(out=den, in_=den), [bs], [bs])
        cre = A.f32(G)
        cim = A.f32(G)
        self.tt("dve", cre, ar1, lre, ALU.mult, [bs, bl], [bs])
        self.tt("dve", t2, ai, lim, ALU.mult, [bs, bl], [bs])
        self.tt("dve", cre, cre, t2, ALU.add, [bs], [bs])
        self.tt("dve", cre, cre, den, ALU.mult, [bs], [bs])
        self.tt("dve", cim, ai, lre, ALU.mult, [bs, bl], [bs])
        self.tt("dve", t2, ar1, lim, ALU.mult, [bs, bl], [bs])
        self.tt("dve", cim, cim, t2, ALU.subtract, [bs], [bs])
        self.tt("dve", cim, cim, den, ALU.mult, [bs], [bs])
        if os.environ.get("KS2") == "5":
            p.barrier()
            return
        s1c = A.f32(G)
        s2c = A.f32(G)
        self.ts("dve", s1c, cim, self.s1[:, 0:1], None, ALU.mult, None, [bs, self.b_const], [bs])
        self.ts("dve", s2c, cre, self.s2[:, 0:1], None, ALU.mult, None, [bs, self.b_const], [bs])
        T1 = A.f32(G * 16)
        T13 = T1.rearrange("p (g c) -> p g c", c=16)
        bc3 = lambda ap: ap.unsqueeze(2).broadcast_to([128, G, 16])
        ZB3 = ZB.rearrange("p (g c) -> p g c", c=16)
        ZsB3 = ZsB.rearrange("p (g c) -> p g c", c=16)
        for cc in range(16):
            self.tt("dve", ZB3[:, :, cc], X13[:, :, cc], cre, ALU.mult, [bs, bl], [bs])
            self.tt("dve", T13[:, :, cc], X23[:, :, cc], s1c, ALU.mult, [bs, bl], [bs])
        self.tt("dve", ZB, ZB, T1, ALU.add, [bs], [bs])
        for cc in range(16):
            self.tt("dve", ZsB3[:, :, cc], X23[:, :, cc], s2c, ALU.mult, [bs, bl], [bs])
            self.tt("dve", T13[:, :, cc], X13[:, :, cc], cim, ALU.mult, [bs, bl], [bs])
        self.tt("dve", ZsB, ZsB, T1, ALU.add, [bs], [bs])
        if os.environ.get("KS2") == "6":
            p.barrier()
            return
        for ci, (t0, tw, own) in enumerate(chunks):
            o = offs[:, ci * G:(ci + 1) * G]
            self.ts("dve", tmp, thr, float(t0), None, ALU.mult, None, [bs], [bs])
            self.reduce_angle("dve", o, tmp, ni, [bs], bs, bs)
        p.op("pool", "iota", dict(out=maskc, pattern=[[16, 8]], base=0, channel_multiplier=-1,
                                  allow_small_or_imprecise_dtypes=True), (), [bs])
        mk2 = A.f32(8)
        self.ts("dve", mk2, maskc, 0.0, None, ALU.is_le, None, [bs], [bs])
        self.ts("dve", maskc, maskc, -16.0, None, ALU.is_gt, None, [bs], [bs])
        self.tt("dve", maskc, maskc, mk2, ALU.mult, [bs], [bs])
        p.op("pool", "iota", dict(out=iota, pattern=[[1, 512]], base=0, channel_multiplier=0,
                                  allow_small_or_imprecise_dtypes=True), (), [bs])
        for t in (Cl1, Cl2):
            self.memset("dve", t, 0.0, [bs])
        p.barrier()
        if self.cut():
            return
        A.off = mark
        uTk = A.bf16(c.NT)
        uTf = A.f32(c.NO)
        b_uk = p.buf("uTk")
        b_uf = p.buf("uTf")
        CC = A.f32(128)
        CC2 = A.f32(128)
        b_cc = p.buf("CC")
        b_lhs = p.buf("lhs")
        R2 = lambda n, f, nm: Rot([(f(n), p.buf(nm)) for _ in range(2)])
        r_a = R2(512, A.f32, "a")
        r_ni = Rot([(A.f32(512).bitcast(I32), p.buf("ni")) for _ in range(2)])
        r_r = R2(512, A.f32, "r")
        r_ab = R2(512, A.f32, "ab")
        r_S = R2(512, A.f32, "S2")
        r_C = R2(512, A.f32, "C2")
        r_W1 = R2(512, A.f32, "W1")
        r_W2 = R2(512, A.f32, "W2")
        r_W = R2(512, A.f32, "W")
        r_w = R2(512, A.f32, "w")
        r_V1 = R2(512, A.bf16, "V1")
        r_V2 = R2(512, A.bf16, "V2")
        carry = A.f32(8)
        b_carry = p.bufs(8, "carry")
        r_yv = R2(512, A.f32, "yv")
        r_x2 = R2(512, A.f32, "x2")
        r_sg = R2(512, A.f32, "sg")
        zb = Rot([0, 1])
        zsb = Rot([2, 3])
        yb = Rot([4, 5])
        for k in range(KS):
            rows = slice(k * 128, (k + 1) * 128)
            p.dma("pool", uTk, s["uT"][rows, :], b_uk, self.sb["uT"])
            p.dma("sync", uTf, s["uT"][rows, c.O0:], b_uf, self.sb["uT"])
            p.dma("sync", CC[:, 0:64], d["c_re"][rows, :], b_cc, partial=True)
            p.dma("sync", CC[:, 64:128], d["c_im"][rows, :], b_cc, partial=True)
            p.dma("sync", CC2[:, 0:64], d["c_im"][rows, :], b_cc, partial=True)
            p.dma("sync", CC2[:, 64:128], d["c_re"][rows, :], b_cc, partial=True)
            self.tr(self.ps[6][:, 0:128], CC, [b_cc], [self.psb[6]])
            self.tr(self.ps[6][:, 128:256], CC2, [b_cc], [self.psb[6]])
            self.tr(self.ps[7][:, 0:128], ZB[:, k * 128:(k + 1) * 128], [bs], [self.psb[7]])
            self.tr(self.ps[7][:, 128:256], ZsB[:, k * 128:(k + 1) * 128], [bs], [self.psb[7]])
            for gi in range(8):
                cs_ = slice(gi * 16, (gi + 1) * 16)
                self.ts("dve", Cl1[:, gi, cs_], self.ps[6][:, gi * 16:(gi + 1) * 16], self.s2[:, 0:1], None, ALU.mult, None,
                        [self.psb[6], self.b_const], [b_lhs])
                self.ts("dve", Cl2[:, gi, cs_], self.ps[6][:, 128 + gi * 16:128 + (gi + 1) * 16], -1.0, None, ALU.mult, None,
                        [self.psb[6]], [b_lhs])
                self.ts("dve", Blz[:, gi, :], self.ps[7][:, 0:128], maskc[:, gi:gi + 1], None, ALU.mult, None,
                        [self.psb[7], bs], [b_lhs])
                self.ts("dve", Blzs[:, gi, :], self.ps[7][:, 128:256], maskc[:, gi:gi + 1], None, ALU.mult, None,
                        [self.psb[7], bs], [b_lhs])
            for ci, (t0, tw, own) in enumerate(chunks):
                ybk = yb.next() if own else None
                for gi in range(8):
                    g = k * 8 + gi
                    (a, b_a) = r_a.next()
                    (nii, b_ni) = r_ni.next()
                    (r, b_r) = r_r.next()
                    (ab_, b_ab) = r_ab.next()
                    (S2, b_S) = r_S.next()
                    (C2, b_C) = r_C.next()
                    self.ts("pool", a[:, 0:tw], iota[:, 0:tw], thr[:, g:g + 1], offs[:, ci * G + g:ci * G + g + 1],
                            ALU.mult, ALU.add, [bs], [b_a])
                    self.ts("pool", nii[:, 0:tw], a[:, 0:tw], 1.0 / TWO_PI, None, ALU.mult, None, [b_a], [b_ni])
                    self.copy("pool", r[:, 0:tw], nii[:, 0:tw], [b_ni], [b_r])
                    self.stt(r[:, 0:tw], r[:, 0:tw], -TWO_PI, a[:, 0:tw], ALU.mult, ALU.add, [b_a, b_r], [b_r])
                    self.ts("pool", r[:, 0:tw], r[:, 0:tw], -PI, PI, ALU.max, ALU.min, [b_r], [b_r])
                    self.act(S2[:, 0:tw], r[:, 0:tw], AF.Sin, [b_r], [b_S])
                    self.act(ab_[:, 0:tw], r[:, 0:tw], AF.Sin, [b_r], [b_ab], scale=0.5)
                    self.act(ab_[:, 0:tw], ab_[:, 0:tw], AF.Square, [b_ab], [b_ab])
                    self.ts("pool", C2[:, 0:tw], ab_[:, 0:tw], -2.0, 1.0, ALU.mult, ALU.add, [b_ab], [b_C])
                    z1, z2 = zb.next(), zsb.next()
                    self.mm(self.ps[z1][:, 0:tw], Blz[:, gi, :], uTk[:, t0:t0 + tw], True, True, [b_lhs, b_uk], [self.psb[z1]])
                    self.mm(self.ps[z2][:, 0:tw], Blzs[:, gi, :], uTk[:, t0:t0 + tw], True, True, [b_lhs, b_uk], [self.psb[z2]])
                    (W1, b_W1) = r_W1.next()
                    (W2, b_W2) = r_W2.next()
                    (W, b_W) = r_W.next()
                    (w, b_w) = r_w.next()
                    self.tt("dve", W1[:, 0:tw], self.ps[z1][:, 0:tw], C2[:, 0:tw], ALU.mult, [self.psb[z1], b_C], [b_W1])
                    self.tt("dve", W2[:, 0:tw], self.ps[z2][:, 0:tw], S2[:, 0:tw], ALU.mult, [self.psb[z2], b_S], [b_W2])
                    self.tt("pool", W[:, 0:tw], W1[:, 0:tw], W2[:, 0:tw], ALU.add, [b_W1, b_W2], [b_W])
                    init = 0.0 if ci == 0 else carry[:, gi:gi + 1]
                    Rl = [b_W, bs] + ([] if ci == 0 else [b_carry[gi]])
                    p.op("dve", "tensor_tensor_scan",
                         dict(out=w[:, 0:tw], data0=rr[:, g:g + 1].broadcast_to([128, tw]), data1=W[:, 0:tw], initial=init,
                              op0=ALU.mult, op1=ALU.add), Rl, [b_w])
                    if ci < NCH - 1:
                        self.copy("act", carry[:, gi:gi + 1], w[:, tw - 1:tw], [b_w], [b_carry[gi]])
                    if own:
                        (V1, b_V1) = r_V1.next()
                        (V2, b_V2) = r_V2.next()
                        self.tt("pool", V1[:, 0:tw], C2[:, 0:tw], w[:, 0:tw], ALU.mult, [b_C, b_w], [b_V1])
                        self.tt("pool", V2[:, 0:tw], S2[:, 0:tw], w[:, 0:tw], ALU.mult, [b_S, b_w], [b_V2])
                        self.mm(self.ps[ybk][:, 0:tw], Cl1[:, gi, :], V1[:, 0:tw], gi == 0, False, [b_lhs, b_V1], [self.psb[ybk]])
                        self.mm(self.ps[ybk][:, 0:tw], Cl2[:, gi, :], V2[:, 0:tw], False, gi == 7, [b_lhs, b_V2], [self.psb[ybk]])
                if own:
                    to = t0 - c.O0
                    (yv, b_yv) = r_yv.next()
                    (x2, b_x2) = r_x2.next()
                    (sg, b_sg) = r_sg.next()
                    self.stt(yv[:, 0:tw], uTf[:, to:to + tw], dsk[:, k:k + 1], self.ps[ybk][:, 0:tw], ALU.mult, ALU.add,
                             [b_uf, bl, self.psb[ybk]], [b_yv])
                    self.act(x2[:, 0:tw], yv[:, 0:tw], AF.Square, [b_yv], [b_x2])
                    self.ts("pool", x2[:, 0:tw], x2[:, 0:tw], 0.044715, 1.0, ALU.mult, ALU.add, [b_x2], [b_x2])
                    self.tt("pool", x2[:, 0:tw], x2[:, 0:tw], yv[:, 0:tw], ALU.mult, [b_x2, b_yv], [b_x2])
                    self.act(sg[:, 0:tw], x2[:, 0:tw], AF.Sigmoid, [b_x2], [b_sg], scale=1.5957691216057308)
                    self.tt("pool", gT[:, k, to:to + tw], yv[:, 0:tw], sg[:, 0:tw], ALU.mult, [b_yv, b_sg], [b_gT])
        p.barrier()
        A.off = mark
        wgl = A.bf16(KS * c.DS).rearrange("p (c n) -> p c n", n=c.DS)
        b_wg = p.buf("wglu")
        p.dma("pool", wgl, d["w_glu"].rearrange("(c p) n -> p c n", p=128), b_wg)
        r_sig = Rot([(A.f32(512), p.buf("sig")) for _ in range(2)])
        r_st = Rot([(A.bf16(512), p.buf("yst")) for _ in range(3)])
        mb = Rot([0, 1, 2, 3])
        for m in range(KS):
            for (to, tw) in tok_tiles(c.NOWN):
                bk = mb.next()
                for kc in range(KS):
                    self.mm(self.ps[bk][:, 0:tw], wgl[:, kc, m * 128:(m + 1) * 128], gT[:, kc, to:to + tw], kc == 0, kc == KS - 1,
                            [b_wg, b_gT], [self.psb[bk]])
                (sg, b_sg) = r_sig.next()
                (st, b_st) = r_st.next()
                self.act(sg[:, 0:tw], self.ps[bk][:, 0:tw], AF.Sigmoid, [self.psb[bk], bl], [b_sg], bias=bgl[:, m:m + 1])
                self.tt("pool", st[:, 0:tw], sg[:, 0:tw], gT[:, m, to:to + tw], ALU.mult, [b_sg, b_gT], [b_st])
                p.dma("sync", s["yssmT"][m * 128:(m + 1) * 128, to:to + tw], st[:, 0:tw], b_st, self.sb["yssmT"], store=True)

    def phase3(self):
        c, p, A, d, s = self.c, self.p, self.A, self.d, self.s
        H = c.H
        kTh = [A.bf16(c.NT) for _ in range(2)]
        qTh = [A.bf16(c.NO) for _ in range(2)]
        Vh = [A.bf16(c.NCTX * 128).rearrange("p (b e) -> p b e", e=128) for _ in range(2)]
        cbh = [A.f32(c.NCTX) for _ in range(2)]
        b_hd = p.bufs(2, "head")
        b_cb = p.bufs(2, "cbh")
        r_pt = Rot([(A.bf16(512), p.buf("pt")) for _ in range(4)])
        r_tmp = Rot([(A.f32(128), p.buf("tmp")) for _ in range(4)])
        r_rs = Rot([(A.f32(512), p.buf("rs")) for _ in range(2)])
        r_o = Rot([(A.f32(512), p.buf("o")) for _ in range(2)])
        r_o2 = Rot([(A.f32(512), p.buf("o2")) for _ in range(2)])
        r_sq = Rot([(A.bf16(512), p.buf("sq")) for _ in range(2)])
        r_rn = Rot([(A.f32(512), p.buf("rn")) for _ in range(2)])
        r_st = Rot([(A.bf16(512), p.buf("st")) for _ in range(2)])
        sbank = Rot([0, 1, 2, 3])
        TD3 = self.TD.rearrange("p (h q) -> p h q", q=128)
        TP3 = self.TP.rearrange("p (h q) -> p h q", q=128)
        for h in range(H):
            j = h % 2
            p.dma("sync", kTh[j], s["kT"][h], b_hd[j], self.sb["kT"], partial=False)
            p.dma("sync", qTh[j], s["qT"][h], b_hd[j], self.sb["qT"], partial=True)
            p.dma("sync", Vh[j], s["v"][:, h * 128:(h + 1) * 128].rearrange("(b p) e -> p b e", p=128), b_hd[j], self.sb["v"], partial=True)
            self.ts("dve", cbh[j], self.kvalid, self.rel31[:, h:h + 1], None, ALU.add, None, [self.b_cload], [b_cb[j]])
            for (to, tw) in tok_tiles(c.NOWN):
                nqb = tw // 128
                qb0 = c.NPRE + to // 128
                nkb = qb0 + nqb
                obk = [4, 5]
                smk = [6, 7]
                for kb in range(nkb):
                    col0 = max(0, kb - qb0) * 128
                    ncols = tw - col0
                    pts = []
                    for m in range(2):
                        sb_ = sbank.next()
                        pr = slice(m * 64, (m + 1) * 64)
                        self.mm(self.ps[sb_][:, 0:ncols], kTh[j][pr, kb * 128:(kb + 1) * 128], qTh[j][pr, to + col0:to + tw],
                                True, True, [b_hd[j]], [self.psb[sb_]])
                        (pt, b_pt) = r_pt.next()
                        near = []
                        qbs = max(qb0, kb)
                        ci = 0
                        for i in range(ncols // 128):
                            qb = qbs + i
                            if qb - kb <= 1:
                                near.append((i, qb - kb))
                            else:
                                break
                        far0 = len(near) * 128
                        lastb = None
                        for (i, dd) in near:
                            (tmp, b_tmp) = r_tmp.next()
                            T3 = TD3 if dd == 0 else TP3
                            self.stt(tmp, self.ps[sb_][:, i * 128:(i + 1) * 128], 0.125, T3[:, h, :], ALU.mult, ALU.add,
                                     [self.psb[sb_], self.b_toep], [b_tmp])
                            self.act(pt[:, i * 128:(i + 1) * 128], tmp, AF.Exp, [b_tmp, self.b_cload], [b_pt],
                                     bias=self.kvalid[:, kb:kb + 1])
                            lastb = b_tmp
                        if far0 < ncols:
                            Rl = [self.psb[sb_], b_cb[j]] + ([lastb] if lastb is not None else [])
                            self.act(pt[:, far0:ncols], self.ps[sb_][:, far0:ncols], AF.Exp, Rl, [b_pt],
                                     bias=cbh[j][:, kb:kb + 1], scale=0.125)
                        pts.append((pt, b_pt))
                    self.flush_pe()

                    def pv(pts=pts, kb=kb, col0=col0, ncols=ncols, tw=tw, j=j, nkb=nkb, obk=obk, smk=smk):
                        for m in range(2):
                            (pt, b_pt) = pts[m]
                            self.mm(self.ps[obk[m]][:, col0:tw], Vh[j][:, kb, :], pt[:, 0:ncols], kb == 0, kb == nkb - 1,
                                    [b_hd[j], b_pt], [self.psb[obk[m]]])
                            self.mm(self.ps[smk[m]][:, col0:tw], self.onesb, pt[:, 0:ncols], kb == 0, kb == nkb - 1,
                                    [self.b_const, b_pt], [self.psb[smk[m]]])
                    self.pe_def.append(pv)
                self.flush_pe()
                rsl = []
                for m in range(2):
                    (rs, b_rs) = r_rs.next()
                    self.ts("dve", rs[:, 0:tw], self.ps[smk[m]][:, 0:tw], 1e-30, None, ALU.max, None, [self.psb[smk[m]]], [b_rs])
                    p.op("dve", "reciprocal", dict(out=rs[:, 0:tw], in_=rs[:, 0:tw]), [b_rs], [b_rs])
                    rsl.append((rs, b_rs))
                (o1, b_o1) = r_o.next()
                (o2, b_o2) = r_o2.next()
                self.tt("dve", o1[:, 0:tw], self.ps[obk[0]][:, 0:tw], rsl[0][0][:, 0:tw], ALU.mult, [self.psb[obk[0]], rsl[0][1]], [b_o1])
                self.tt("dve", o2[:, 0:tw], self.ps[obk[1]][:, 0:tw], rsl[1][0][:, 0:tw], ALU.mult, [self.psb[obk[1]], rsl[1][1]], [b_o2])
                self.stt(o1[:, 0:tw], o2[:, 0:tw], self.neglam[:, 0:1], o1[:, 0:tw], ALU.mult, ALU.add, [b_o2, self.b_const], [b_o1])
                (sq, b_sq) = r_sq.next()
                (rn, b_rn) = r_rn.next()
                (st, b_st) = r_st.next()
                self.act(sq[:, 0:tw], o1[:, 0:tw], AF.Square, [b_o1], [b_sq])
                nb = sbank.next()
                self.mm(self.ps[nb][:, 0:tw], self.onesb, sq[:, 0:tw], True, True, [self.b_const, b_sq], [self.psb[nb]])
                self.rsqrt_ps(rn[:, 0:tw], self.ps[nb][:, 0:tw], 1.0 / 128, [self.psb[nb]], b_rn)
                self.stt(st[:, 0:tw], o1[:, 0:tw], self.slng[:, 0:1], rn[:, 0:tw], ALU.mult, ALU.mult, [b_o1, b_rn, self.b_cload], [b_st])
                p.dma("sync", s["yattT"][h * 128:(h + 1) * 128, to:to + tw], st[:, 0:tw], b_st, self.sb["yattT"], store=True)

    def phase4(self):
        c, p, A, d, s = self.c, self.p, self.A, self.d, self.s
        KD, KS = c.KD, c.KS
        KA = c.DATT // 128
        KB = KS + KA
        base = A.off
        for grp in self.own_supers():
            A.off = base
            S0 = grp[0][0]
            S = sum(t[1] for t in grp)
            mg = A.bf16(KD * S).rearrange("p (c t) -> p c t", t=S)
            b_mg = p.buf("merged")
            mark = A.off
            ybT = A.bf16(KB * S).rearrange("p (c t) -> p c t", t=S)
            b_yb = p.buf("ybT")
            p.dma("sync", ybT[:, 0:KS, :], s["yssmT"][:, S0:S0 + S].rearrange("(c p) t -> p c t", p=128), b_yb, self.sb["yssmT"], partial=True)
            p.dma("sync", ybT[:, KS:KB, :], s["yattT"][:, S0:S0 + S].rearrange("(c p) t -> p c t", p=128), b_yb, self.sb["yattT"], partial=True)
            wbr = [A.bf16(KB * 128).rearrange("p (c n) -> p c n", n=128) for _ in range(2)]
            b_w = p.bufs(2, "wbr")
            r_g1 = Rot([(A.bf16(512), p.buf("g1")) for _ in range(3)])
            r_g2 = Rot([(A.bf16(512), p.buf("g2")) for _ in range(3)])
            r_t1 = Rot([(A.f32(512), p.buf("t1")) for _ in range(2)])
            r_t2 = Rot([(A.f32(512), p.buf("t2")) for _ in range(2)])
            ba = Rot([0, 1, 2])
            bb = Rot([3, 4, 5])
            for m in range(KD):
                wj = m % 2
                p.dma("pool", wbr[wj], d["w_branch"][:, m * 128:(m + 1) * 128].rearrange("(c p) n -> p c n", p=128), b_w[wj])
                tl = 0
                for (to, tw) in grp:
                    ka, kb_ = ba.next(), bb.next()
                    for kc in range(KS):
                        self.mm(self.ps[ka][:, 0:tw], wbr[wj][:, kc, :], ybT[:, kc, tl:tl + tw], kc == 0, kc == KS - 1,
                                [b_w[wj], b_yb], [self.psb[ka]])
                    for kc in range(KA):
                        self.mm(self.ps[kb_][:, 0:tw], wbr[wj][:, KS + kc, :], ybT[:, KS + kc, tl:tl + tw], kc == 0, kc == KA - 1,
                                [b_w[wj], b_yb], [self.psb[kb_]])
                    (g1, b_g1) = r_g1.next()
                    (g2, b_g2) = r_g2.next()
                    (t1, b_t1) = r_t1.next()
                    (t2, b_t2) = r_t2.next()
                    p.dma("sync", g1[:, 0:tw], s["sgs"][m * 128:(m + 1) * 128, to:to + tw], b_g1, self.sb["sgs"])
                    p.dma("sync", g2[:, 0:tw], s["sga"][m * 128:(m + 1) * 128, to:to + tw], b_g2, self.sb["sga"])
                    self.tt("dve", t1[:, 0:tw], self.ps[ka][:, 0:tw], g1[:, 0:tw], ALU.mult, [self.psb[ka], b_g1], [b_t1])
                    self.tt("dve", t2[:, 0:tw], self.ps[kb_][:, 0:tw], g2[:, 0:tw], ALU.mult, [self.psb[kb_], b_g2], [b_t2])
                    self.tt("pool", mg[:, m, tl:tl + tw], t1[:, 0:tw], t2[:, 0:tw], ALU.add, [b_t1, b_t2], [b_mg])
                    tl += tw
            p.barrier()
            A.off = mark
            wo = [A.bf16(KD * c.MB).rearrange("p (c n) -> p c n", n=c.MB) for _ in range(2)]
            b_wo = p.bufs(2, "wo")
            acc = A.f32(S)
            b_acc = p.buf("acc")
            self.memset("dve", acc, 0.0, [b_acc])
            r_hb = Rot([(A.f32(512), p.buf("hb")) for _ in range(3)])
            r_h2 = Rot([(A.f32(512), p.buf("h2")) for _ in range(3)])
            r_a2 = Rot([(A.bf16(512), p.buf("a2")) for _ in range(3)])
            r_sq = Rot([(A.f32(512), p.buf("sq")) for _ in range(2)])
            mb = Rot([0, 1, 2, 3, 4])
            wi = 0
            for c0 in range(0, c.D, c.MB):
                wj = wi % 2
                wi += 1
                p.dma("pool", wo[wj], d["w_o"][:, c0:c0 + c.MB].rearrange("(c p) n -> p c n", p=128), b_wo[wj])
                for sub in range(c.NSUB):
                    m = c0 // 128 + sub
                    tl = 0
                    for (to, tw) in grp:
                        bk = mb.next()
                        for kc in range(KD):
                            self.mm(self.ps[bk][:, 0:tw], wo[wj][:, kc, sub * 128:(sub + 1) * 128], mg[:, kc, tl:tl + tw],
                                    kc == 0, kc == KD - 1, [b_wo[wj], b_mg], [self.psb[bk]])
                        (hb, b_hb) = r_hb.next()
                        (h2, b_h2) = r_h2.next()
                        (a2, b_a2) = r_a2.next()
                        (sq, b_sq) = r_sq.next()
                        p.dma("sync", hb[:, 0:tw], s["hT"][m * 128:(m + 1) * 128, to:to + tw], b_hb, self.sb["hT"])
                        self.tt("dve", h2[:, 0:tw], self.ps[bk][:, 0:tw], hb[:, 0:tw], ALU.add, [self.psb[bk], b_hb], [b_h2])
                        p.dma("sync", s["h2T"][m * 128:(m + 1) * 128, to:to + tw], h2[:, 0:tw], b_h2, self.sb["h2T"], store=True)
                        self.ts("pool", a2[:, 0:tw], h2[:, 0:tw], self.g2col[:, m:m + 1], None, ALU.mult, None, [b_h2, self.b_cload], [b_a2])
                        p.dma("sync", s["a2T"][m * 128:(m + 1) * 128, to:to + tw], a2[:, 0:tw], b_a2, self.sb["a2T"], store=True)
                        self.act(sq[:, 0:tw], h2[:, 0:tw], AF.Square, [b_h2], [b_sq])
                        self.tt("pool", acc[:, tl:tl + tw], acc[:, tl:tl + tw], sq[:, 0:tw], ALU.add, [b_sq], [b_acc])
                        tl += tw
            tl = 0
            for (to, tw) in grp:
                bk = mb.next()
                (h2, b_h2) = r_h2.next()
                self.mm(self.ps[bk][:, 0:tw], self.onesf, acc[:, tl:tl + tw], True, True, [self.b_const, b_acc], [self.psb[bk]])
                self.rsqrt_ps(h2[:, 0:tw], self.ps[bk][:, 0:tw], 1.0 / c.D, [self.psb[bk]], b_h2)
                p.dma("sync", s["rstd2"][:, to:to + tw], h2[:, 0:tw], b_h2, self.sb["rstd2"], store=True)
                tl += tw
            p.barrier()

    def phase5(self):
        c, p, A, d, s = self.c, self.p, self.A, self.d, self.s
        KD = c.KD
        base = A.off
        for grp in self.own_supers():
            A.off = base
            S0 = grp[0][0]
            S = sum(t[1] for t in grp)
            a2 = A.bf16(KD * S).rearrange("p (c t) -> p c t", t=S)
            b_a2 = p.buf("a2")
            rst = A.f32(S)
            b_rst = p.buf("rst")
            p.dma("sync", a2, s["a2T"][:, S0:S0 + S].rearrange("(c p) t -> p c t", p=128), b_a2, self.sb["a2T"])
            p.dma("sync", rst, s["rstd2"][:, S0:S0 + S], b_rst, self.sb["rstd2"])
            wg = [A.bf16(KD * c.MB).rearrange("p (c n) -> p c n", n=c.MB) for _ in range(2)]
            wu = [A.bf16(KD * c.MB).rearrange("p (c n) -> p c n", n=c.MB) for _ in range(2)]
            b_wg = p.bufs(2, "wg")
            b_wu = p.bufs(2, "wu")
            r_t1 = Rot([(A.f32(512), p.buf("t1")) for _ in range(2)])
            r_sg = Rot([(A.f32(512), p.buf("sg")) for _ in range(2)])
            r_t3 = Rot([(A.f32(512), p.buf("t3")) for _ in range(2)])
            r_st = Rot([(A.bf16(512), p.buf("st")) for _ in range(3)])
            bg = Rot([0, 1, 2, 3])
            bu = Rot([4, 5, 6, 7])
            wi = 0
            for c0 in range(0, c.DFF, c.MB):
                wj = wi % 2
                wi += 1
                p.dma("pool", wg[wj], d["w_gate_up"][:, c0:c0 + c.MB].rearrange("(c p) n -> p c n", p=128), b_wg[wj])
                p.dma("pool", wu[wj], d["w_gate_up"][:, c.DFF + c0:c.DFF + c0 + c.MB].rearrange("(c p) n -> p c n", p=128), b_wu[wj])
                for sub in range(c.NSUB):
                    m = c0 // 128 + sub
                    tl = 0
                    for (to, tw) in grp:
                        kg, ku = bg.next(), bu.next()
                        for kc in range(KD):
                            self.mm(self.ps[kg][:, 0:tw], wg[wj][:, kc, sub * 128:(sub + 1) * 128], a2[:, kc, tl:tl + tw],
                                    kc == 0, kc == KD - 1, [b_wg[wj], b_a2], [self.psb[kg]])
                        for kc in range(KD):
                            self.mm(self.ps[ku][:, 0:tw], wu[wj][:, kc, sub * 128:(sub + 1) * 128], a2[:, kc, tl:tl + tw],
                                    kc == 0, kc == KD - 1, [b_wu[wj], b_a2], [self.psb[ku]])
                        (t1, b_t1) = r_t1.next()
                        (sg, b_sg) = r_sg.next()
                        (t3, b_t3) = r_t3.next()
                        (st, b_st) = r_st.next()
                        self.tt("dve", t1[:, 0:tw], self.ps[kg][:, 0:tw], rst[:, tl:tl + tw], ALU.mult, [self.psb[kg], b_rst], [b_t1])
                        self.act(sg[:, 0:tw], t1[:, 0:tw], AF.Silu, [b_t1], [b_sg])
                        self.tt("dve", t3[:, 0:tw], self.ps[ku][:, 0:tw], rst[:, tl:tl + tw], ALU.mult, [self.psb[ku], b_rst], [b_t3])
                        self.tt("pool", st[:, 0:tw], sg[:, 0:tw], t3[:, 0:tw], ALU.mult, [b_sg, b_t3], [b_st])
                        p.dma("sync", s["act3T"][m * 128:(m + 1) * 128, to:to + tw], st[:, 0:tw], b_st, self.sb["act3T"], store=True)
                        tl += tw
            p.barrier()

    def phase6(self):
        c, p, A, d, s = self.c, self.p, self.A, self.d, self.s
        KF, KD = c.KF, c.KD
        PC = 32
        pieces = [(c0, min(KF, c0 + PC)) for c0 in range(0, KF, PC)]
        base = A.off
        for (to, tw) in tok_tiles(c.NOWN):
            A.off = base
            a3src = s["act3T"][:, to:to + tw].rearrange("(c p) t -> p c t", p=128)
            a3p, b_a3 = [], []
            for (c0, c1) in pieces:
                t = A.bf16((c1 - c0) * tw).rearrange("p (c t) -> p c t", t=tw)
                b = p.buf("a3")
                p.dma("sync", t, a3src[:, c0:c1, :], b, self.sb["act3T"])
                p.fence("sync", b)
                a3p.append(t)
                b_a3.append(b)
            r_wd = Rot([(A.bf16(PC * 128).rearrange("p (c n) -> p c n", n=128), p.buf("wd")) for _ in range(3)])
            r_hb = Rot([(A.f32(512), p.buf("hb")) for _ in range(3)])
            r_h3 = Rot([(A.f32(512), p.buf("h3")) for _ in range(2)])
            r_os = Rot([(A.f32(512), p.buf("ost")) for _ in range(3)])
            mb = Rot([0, 1, 2, 3])
            tb = Rot([4, 5, 6, 7])
            nb = tw // 128
            for m in range(KD):
                wsrc = d["w_down"][:, m * 128:(m + 1) * 128].rearrange("(c p) n -> p c n", p=128)
                bk = mb.next()
                for pi, (c0, c1) in enumerate(pieces):
                    (wd, b_wd) = r_wd.next()
                    p.dma("pool", wd[:, 0:c1 - c0, :], wsrc[:, c0:c1, :], b_wd)
                    for kc in range(c0, c1):
                        self.mm(self.ps[bk][:, 0:tw], wd[:, kc - c0, :], a3p[pi][:, kc - c0, :], kc == 0, kc == KF - 1,
                                [b_wd, b_a3[pi]], [self.psb[bk]])
                self.flush_pe()
                (hb, b_hb) = r_hb.next()
                (h3, b_h3) = r_h3.next()
                (os_, b_os) = r_os.next()
                p.dma("sync", hb[:, 0:tw], s["h2T"][m * 128:(m + 1) * 128, to:to + tw], b_hb, self.sb["h2T"])
                self.tt("dve", h3[:, 0:tw], self.ps[bk][:, 0:tw], hb[:, 0:tw], ALU.add, [self.psb[bk], b_hb], [b_h3])

                def fin(h3=h3, b_h3=b_h3, os_=os_, b_os=b_os, m=m, to=to, tw=tw, nb=nb):
                    tk = tb.next()
                    for i in range(nb):
                        self.tr(self.ps[tk][:, i * 128:(i + 1) * 128], h3[:, i * 128:(i + 1) * 128], [b_h3], [self.psb[tk]])
                    self.copy("dve", os_[:, 0:tw], self.ps[tk][:, 0:tw], [self.psb[tk]], [b_os])
                    p.dma("sync", self.out[to:to + tw, m * 128:(m + 1) * 128].rearrange("(i p) f -> p i f", p=128),
                          os_[:, 0:tw].rearrange("p (i f) -> p i f", f=128), b_os, self.b_out, store=True)
                self.pe_def.append(fin)
            self.flush_pe()
            p.barrier()


def t5_bucket_np(n):
    n = np.asarray(n)
    nf = np.maximum(n, 16).astype(np.float32)
    log_b = 16 + (np.log(nf / np.float32(16)).astype(np.float32) / np.float32(math.log(128 / 16)) * np.float32(16)).astype(np.int32)
    return np.where(n < 16, n, np.minimum(log_b, 31))


def static_consts(cfg):
    ohc = np.zeros((32, FL), np.float32)
    dist = np.arange(FL) - 127
    bk = t5_bucket_np(np.maximum(dist, 0))
    for j in range(FL):
        if dist[j] >= 0:
            ohc[bk[j], j] = 1.0
    negmask = np.zeros((cfg.H, FL), np.float32)
    negmask[:, :127] = NEG
    return ohc, negmask


def make_in_maps(cfg, inputs, n_batch):
    c = cfg
    f = lambda a: np.ascontiguousarray(np.asarray(a, dtype=np.float32))
    x = np.asarray(inputs["x"], dtype=np.float32)
    meta = f(inputs["meta_tokens"])
    ohc, negmask = static_consts(c)
    shared = {
        "ohc": ohc, "negmask": negmask, "rel_bias": f(inputs["rel_bias"]),
        "ln1_g": f(inputs["ln1_g"][0]), "ln2_g": f(inputs["ln2_g"][0]), "w_in": f(inputs["w_in"][0]),
        "subln_g": f(inputs["subln_g"][0]),
        "lam_re": f(inputs["lam_re"][0]), "lam_im": f(inputs["lam_im"][0]), "log_dt": f(inputs["log_dt"][0]),
        "b_re": f(inputs["b_re"][0]), "b_im": f(inputs["b_im"][0]),
        "c_re": f(inputs["c_re"][0]).reshape(c.G * 16, 64), "c_im": f(inputs["c_im"][0]).reshape(c.G * 16, 64),
        "d_skip": f(inputs["d_skip"][0]), "w_glu": f(inputs["w_glu"][0]), "b_glu": f(inputs["b_glu"][0]),
        "w_branch": f(inputs["w_branch"][0]), "w_o": f(inputs["w_o"][0]),
        "w_gate_up": f(inputs["w_gate_up"][0]), "w_down": f(inputs["w_down"][0]),
    }
    for n in ("q_norm_g", "k_norm_g", "lam_q1", "lam_k1", "lam_q2", "lam_k2"):
        shared[n] = f(inputs[n][0])
    NB = c.NCTX
    maps = []
    for b in range(n_batch):
        hpad = np.zeros((NB * 128, c.D), np.float32)
        hpad[PADF:PADF + N_META] = meta
        hpad[PADF + N_META:] = x[b]
        for half in range(2):
            hp = np.zeros((c.NT, c.D), np.float32)
            kv = np.full((c.NT,), NEG, np.float32)
            if half == 0:
                hp[c.O0:] = hpad[:c.NO]
                kv[c.O0 + PADF:] = 0.0
            else:
                hp[:] = hpad
                kv[PADF:] = 0.0
            m = dict(shared)
            m["hp"] = hp
            m["kvalid"] = np.ascontiguousarray(kv.reshape(c.NCTX, 128).T)
            maps.append(m)
    return maps


_NC_CACHE = {}


def run_cfg(cfg, inputs, n_batch, debug=False, stop_after=99, trace=False):
    key = (id(cfg), debug, stop_after)
    if key not in _NC_CACHE:
        _NC_CACHE[key] = Builder(cfg, debug=debug, stop_after=stop_after).build()
    nc = _NC_CACHE[key]
    maps = make_in_maps(cfg, inputs, n_batch)
    res = run_bass_kernel_spmd(nc, maps, core_ids=list(range(2 * n_batch)), trace=trace)
    return res


def assemble(cfg, res, n_batch, seq):
    c = cfg
    out = np.zeros((n_batch, seq, c.D), np.float32)
    n0 = c.NO - 128
    for b in range(n_batch):
        o0 = res.results[2 * b]["out"]
        o1 = res.results[2 * b + 1]["out"]
        out[b, :n0] = o0[128:]
        out[b, n0:] = o1[c.NO - (seq - n0):]
    return out


FULL = Cfg()


def kernel(**inputs):
    res = run_cfg(FULL, inputs, 4)
    return assemble(FULL, res, 4, 4096)
```

```python
import math
import os
from contextlib import ExitStack

import numpy as np
import concourse.bass as bass
import concourse.mybir as mybir
from concourse.bass_utils import run_bass_kernel_spmd

F32 = mybir.dt.float32
BF16 = mybir.dt.bfloat16
AF = mybir.ActivationFunctionType
ALU = mybir.AluOpType
AX = mybir.AxisListType
NEG = -30000.0
EPS = 1e-6
PI = math.pi
TWO_PI = 2.0 * math.pi
N_META = 16
PADF = 112
FL = 384


class Cfg:
    def __init__(self, D=4096, DS=1024, H=16, DFF=11008, NPRE=16, NOWN=17, NSUB=2, SMAX=1152, depth_l=0):
        self.D, self.DS, self.H, self.DFF = D, DS, H, DFF
        self.NPRE, self.NOWN, self.NSUB, self.SMAX = NPRE, NOWN, NSUB, SMAX
        self.G = DS // 16
        self.QKW = H * 128
        self.DATT = H * 128
        self.DIN = DS + 2 * self.QKW + self.DATT + 2 * D
        self.cU, self.cQ = 0, DS
        self.cK = DS + self.QKW
        self.cV = DS + 2 * self.QKW
        self.cGS = self.cV + self.DATT
        self.cGA = self.cGS + D
        self.NCTX = NPRE + NOWN
        self.NT = self.NCTX * 128
        self.NO = NOWN * 128
        self.O0 = NPRE * 128
        self.KD = D // 128
        self.KS = DS // 128
        self.KF = DFF // 128
        self.MB = NSUB * 128
        self.lambda_init = 0.8 - 0.6 * math.exp(-0.3 * depth_l)


def tok_tiles(nblocks, maxb=4):
    nt = -(-nblocks // maxb)
    base, rem = divmod(nblocks, nt)
    out, s = [], 0
    for i in range(nt):
        w = base + (1 if i < rem else 0)
        out.append((s * 128, w * 128))
        s += w
    return out


def super_tiles(tiles, smax):
    out, cur, tot = [], [], 0
    for t in tiles:
        if cur and tot + t[1] > smax:
            out.append(cur)
            cur, tot = [], 0
        cur.append(t)
        tot += t[1]
    if cur:
        out.append(cur)
    return out


class Buf:
    __slots__ = ("name", "w", "r", "slot", "gen")

    def __init__(self, name):
        self.name, self.w, self.r, self.slot, self.gen = name, [], [], None, -1


class Prog:
    ENG = ("sync", "act", "pool", "dve", "pe")
    EPOCH = 30000
    DLIM = 3000

    def __init__(self):
        self.ops = {e: [] for e in self.ENG}
        self.cnt = {e: 0 for e in self.ENG}
        self.seen = {e: {} for e in self.ENG}
        self.nb = 0
        self.semkeys, self.semset = [], set()
        self.dlast = {}
        self.gen = 0
        self.slots = []
        self.nfree = 0

    def buf(self, name="b"):
        self.nb += 1
        return Buf("%s#%d" % (name, self.nb))

    def bufs(self, n, name="b"):
        return [self.buf(name) for _ in range(n)]

    def _sem(self, sk):
        if sk not in self.semset:
            self.semset.add(sk)
            self.semkeys.append(sk)

    def _need(self, eng, ev, waits):
        k, v = ev
        if k[0] == "e" and k[1] == "pe" and eng == "pe":
            return
        if self.seen[eng].get(k, 0) >= v:
            return
        if waits.get(k, 0) < v:
            waits[k] = v

    def _commit(self, eng, waits):
        wl = []
        for k, v in waits.items():
            self.seen[eng][k] = v
            if k[0] == "e":
                sk = ("e", k[1], (v - 1) // self.EPOCH)
                val = (v - 1) % self.EPOCH + 1
            else:
                sk, val = k, v
            self._sem(sk)
            wl.append((sk, val))
        return wl

    @staticmethod
    def _add(lst, ev):
        return [x for x in lst if x[0] != ev[0]] + [ev]

    def op(self, eng, name, kw, R=(), W=()):
        waits = {}
        for b in R:
            for ev in b.w:
                self._need(eng, ev, waits)
        for b in W:
            for ev in b.w:
                self._need(eng, ev, waits)
            for ev in b.r:
                self._need(eng, ev, waits)
        wl = self._commit(eng, waits)
        self.cnt[eng] += 1
        c = self.cnt[eng]
        ev = (("e", eng), c)
        sk = ("e", eng, (c - 1) // self.EPOCH)
        self._sem(sk)
        self.ops[eng].append((wl, name, kw, sk, 1))
        for b in W:
            b.w = [ev]
            b.r = []
        for b in R:
            if b not in W:
                b.r = self._add(b.r, ev)

    def dma(self, q, out, in_, sb, dr=None, store=False, partial=False, **kw):
        waits = {}
        if sb.gen != self.gen:
            if self.nfree >= len(self.slots):
                self.slots.append([0, 0])
            sb.slot = self.nfree
            self.nfree += 1
            sb.gen = self.gen
            sl = self.slots[sb.slot]
            if sl[1] >= self.DLIM:
                sl[0] += 1
                sl[1] = 0
        sl = self.slots[sb.slot]
        key = ("d", sb.slot, sl[0])
        if store:
            for ev in sb.w:
                self._need(q, ev, waits)
        else:
            if dr is not None:
                for ev in dr.w:
                    self._need(q, ev, waits)
            for ev in sb.w:
                if partial and ev[0] == key:
                    continue
                self._need(q, ev, waits)
            for ev in sb.r:
                self._need(q, ev, waits)
        wl = self._commit(q, waits)
        sl[1] += 1
        ev = (key, 16 * sl[1])
        self._sem(key)
        self.dlast[key] = 16 * sl[1]
        d = dict(out=out, in_=in_)
        d.update(kw)
        self.ops[q].append((wl, "dma_start", d, key, 16))
        if store:
            sb.r = self._add(sb.r, ev)
            if dr is not None:
                dr.w = self._add(dr.w, ev)
        else:
            if partial:
                sb.w = self._add(sb.w, ev)
            else:
                sb.w = [ev]
                sb.r = []
            if dr is not None:
                dr.r = self._add(dr.r, ev)

    def fence(self, q, b):
        waits = {}
        for ev in b.w:
            self._need(q, ev, waits)
        wl = self._commit(q, waits)
        if wl:
            self.ops[q].append((wl, None, None, None, 0))

    def barrier(self, allbufs=()):
        evs = [(("e", e), self.cnt[e]) for e in self.ENG if self.cnt[e] > 0]
        evs += [(k, v) for k, v in self.dlast.items()]
        for eng in self.ENG:
            waits = {}
            for ev in evs:
                self._need(eng, ev, waits)
            wl = self._commit(eng, waits)
            if wl:
                self.ops[eng].append((wl, None, None, None, 0))
        self.gen += 1
        self.nfree = 0

    def emit(self, nc, es):
        sems = {}
        for i, sk in enumerate(self.semkeys):
            sems[sk] = es.enter_context(nc.semaphore("s%d" % i))
        block = es.enter_context(nc.Block())
        if os.environ.get("KDUMP"):
            for e in self.ENG:
                ops = self.ops[e]
                idx = [i for i, o in enumerate(ops) if o[1] is None]
                st = idx[-int(os.environ.get("KDB", "3"))]
                print("ENGINE", e, "total", len(ops), "from", st)
                for o in ops[st:st + int(os.environ["KDUMP"])]:
                    print("   waits", o[0], "op", o[1], "inc", o[3], o[4])

        def run(engname):
            def f(e):
                for (wl, name, kw, sk, inc) in self.ops[engname]:
                    for (wk, val) in wl:
                        e.wait_ge(sems[wk], val)
                    if name is None:
                        continue
                    ins = getattr(e, name)(**kw)
                    ins.then_inc(sems[sk], inc)
            return f

        block.sync(run("sync"))
        block.scalar(run("act"))
        block.gpsimd(run("pool"))
        block.vector(run("dve"))
        block.tensor(run("pe"))


class Arena:
    def __init__(self, big, nwords):
        self.big, self.n, self.off = big, nwords, 0

    def f32(self, n):
        a = self.off
        self.off += n
        assert self.off <= self.n, ("arena overflow", self.off, self.n)
        return self.big[:, a:a + n]

    def bf16(self, n):
        w = (n + 1) // 2
        a = self.off
        self.off += w
        assert self.off <= self.n, ("arena overflow", self.off, self.n)
        return self.big[:, a:a + w].bitcast(BF16)[:, 0:n]


class Rot:
    def __init__(self, items):
        self.items, self.i = items, 0

    def next(self):
        it = self.items[self.i % len(self.items)]
        self.i += 1
        return it


class Builder:
    def __init__(self, cfg, debug=False, stop_after=99):
        self.c = cfg
        self.debug = debug
        self.stop_after = stop_after

    def act(self, out, in_, func, R, W, bias=None, scale=None, accum_out=None):
        kw = dict(out=out, in_=in_, func=func)
        if bias is not None:
            kw["bias"] = bias
        if scale is not None:
            kw["scale"] = scale
        if accum_out is not None:
            kw["accum_out"] = accum_out
        self.p.op("act", "activation", kw, R, W)

    def ts(self, eng, out, in0, s1, s2, op0, op1, R, W):
        kw = dict(out=out, in0=in0, scalar1=s1, scalar2=s2, op0=op0)
        if op1 is not None:
            kw["op1"] = op1
        self.p.op(eng, "tensor_scalar", kw, R, W)

    def tt(self, eng, out, in0, in1, op, R, W):
        self.p.op(eng, "tensor_tensor", dict(out=out, in0=in0, in1=in1, op=op), R, W)

    def stt(self, out, in0, scalar, in1, op0, op1, R, W):
        self.p.op("dve", "scalar_tensor_tensor", dict(out=out, in0=in0, scalar=scalar, in1=in1, op0=op0, op1=op1), R, W)

    def copy(self, eng, out, in_, R, W):
        if eng == "act":
            self.act(out, in_, AF.Copy, R, W)
        else:
            self.p.op(eng, "tensor_copy", dict(out=out, in_=in_), R, W)

    def memset(self, eng, ap, val, W):
        self.p.op(eng, "memset", dict(ap=ap, constant=val), (), W)

    def mm(self, out, lhsT, rhs, start, stop, R, W):
        self.p.op("pe", "matmul", dict(out=out, lhsT=lhsT, rhs=rhs, start=start, stop=stop), R, W)

    def tr(self, out, in_, R, W):
        self.p.op("pe", "transpose", dict(out=out, in_=in_, identity=self.ident), list(R) + [self.b_const], W)

    def cut(self):
        self.ncut = getattr(self, "ncut", 0) + 1
        return self.ncut >= int(os.environ.get("KCUT2", "999"))

    def flush_pe(self):
        for f in self.pe_def:
            f()
        self.pe_def = []

    def build(self):
        c = self.c
        nc = bass.Bass("TRN2", target_bir_lowering=False)
        self.nc = nc
        p = Prog()
        self.p = p
        self.pe_def = []
        IN = lambda name, shape: nc.dram_tensor(name, list(shape), F32, kind="ExternalInput").ap()
        skind = "ExternalOutput" if self.debug else "Internal"
        SC = lambda name, shape, dt: nc.dram_tensor(name, list(shape), dt, kind=skind).ap()
        d = {}
        d["hp"] = IN("hp", (c.NT, c.D))
        d["kvalid"] = IN("kvalid", (128, c.NCTX))
        d["ohc"] = IN("ohc", (32, FL))
        d["negmask"] = IN("negmask", (c.H, FL))
        d["rel_bias"] = IN("rel_bias", (32, c.H))
        d["ln1_g"] = IN("ln1_g", (c.D,))
        d["ln2_g"] = IN("ln2_g", (c.D,))
        d["w_in"] = IN("w_in", (c.D, c.DIN))
        for n in ("q_norm_g", "k_norm_g", "lam_q1", "lam_k1", "lam_q2", "lam_k2"):
            d[n] = IN(n, (64,))
        d["subln_g"] = IN("subln_g", (128,))
        d["lam_re"] = IN("lam_re", (c.G, 64))
        d["lam_im"] = IN("lam_im", (c.G, 64))
        d["log_dt"] = IN("log_dt", (c.G,))
        d["b_re"] = IN("b_re", (c.G, 64, 16))
        d["b_im"] = IN("b_im", (c.G, 64, 16))
        d["c_re"] = IN("c_re", (c.G * 16, 64))
        d["c_im"] = IN("c_im", (c.G * 16, 64))
        d["d_skip"] = IN("d_skip", (c.DS,))
        d["w_glu"] = IN("w_glu", (c.DS, c.DS))
        d["b_glu"] = IN("b_glu", (c.DS,))
        d["w_branch"] = IN("w_branch", (c.DS + c.DATT, c.D))
        d["w_o"] = IN("w_o", (c.D, c.D))
        d["w_gate_up"] = IN("w_gate_up", (c.D, 2 * c.DFF))
        d["w_down"] = IN("w_down", (c.DFF, c.D))
        self.d = d
        s = {}
        s["uT"] = SC("uT", (c.DS, c.NT), F32)
        s["qT"] = SC("qT", (c.H, 128, c.NO), BF16)
        s["kT"] = SC("kT", (c.H, 128, c.NT), BF16)
        s["v"] = SC("v", (c.NT, c.DATT), BF16)
        s["sgs"] = SC("sgs", (c.D, c.NO), BF16)
        s["sga"] = SC("sga", (c.D, c.NO), BF16)
        s["hT"] = SC("hT", (c.D, c.NO), F32)
        s["yssmT"] = SC("yssmT", (c.DS, c.NO), BF16)
        s["yattT"] = SC("yattT", (c.DATT, c.NO), BF16)
        s["h2T"] = SC("h2T", (c.D, c.NO), F32)
        s["a2T"] = SC("a2T", (c.D, c.NO), BF16)
        s["rstd2"] = SC("rstd2", (128, c.NO), F32)
        s["act3T"] = SC("act3T", (c.DFF, c.NO), BF16)
        s["rrep"] = nc.dram_tensor("rrep", [c.H, 128, FL], F32, kind="Internal").ap()
        self.s = s
        self.sb = {k: p.buf("dram_" + k) for k in s}
        self.out = nc.dram_tensor("out", [c.NO, c.D], F32, kind="ExternalOutput").ap()
        self.b_out = p.buf("dram_out")

        NW = 47616
        with ExitStack() as es:
            big = es.enter_context(nc.sbuf_tensor("arena", [128, NW], F32))
            self.A = Arena(big, NW)
            self.ps = [es.enter_context(nc.psum_tensor("ps%d" % i, [128, 512], F32)) for i in range(8)]
            self.psb = [p.buf("ps%d" % i) for i in range(8)]
            self.phase0()
            phases = [self.phase1, self.phase2, self.phase3, self.phase4, self.phase5, self.phase6]
            for i, ph in enumerate(phases):
                if i + 1 > self.stop_after:
                    break
                self.A.off = self.A0
                ph()
                self.flush_pe()
                p.barrier()
            if self.stop_after < 6:
                t = self.A.f32(128)
                b = p.buf("dummy")
                self.memset("dve", t, 0.0, [b])
                p.dma("sync", self.out[0:128, 0:128], t, b, self.b_out, store=True)
            p.barrier()
            print("ops:", {e: len(p.ops[e]) for e in p.ENG}, "sems:", len(p.semkeys), flush=True)
            p.emit(nc, es)
        return nc

    def phase0(self):
        c, p, A, d = self.c, self.p, self.A, self.d
        bc = p.buf("const")
        self.b_const = bc
        self.ident = A.f32(128)
        self.onesf = A.f32(128)
        self.onesb = A.bf16(128)
        self.blk = A.bf16(128)
        p.op("pool", "iota", dict(out=self.ident, pattern=[[1, 128]], base=0, channel_multiplier=-1,
                                  allow_small_or_imprecise_dtypes=True), (), [bc])
        p.op("dve", "tensor_single_scalar", dict(out=self.ident, in_=self.ident, scalar=0.0, op=ALU.is_equal), [bc], [bc])
        self.memset("dve", self.onesf, 1.0, [bc])
        self.memset("dve", self.onesb, 1.0, [bc])
        self.memset("dve", self.blk, 0.0, [bc])
        self.memset("dve", self.blk[0:64, 0:64], 1.0, [bc])
        self.memset("dve", self.blk[64:128, 64:128], 1.0, [bc])
        self.s1 = A.f32(1)
        self.s2 = A.f32(1)
        self.negpi = A.f32(1)
        for (ap, top, bot) in ((self.s1, -1.0, 1.0), (self.s2, 1.0, -1.0)):
            self.memset("dve", ap[0:64, :], top, [bc])
            self.memset("dve", ap[64:128, :], bot, [bc])
        self.memset("dve", self.negpi, -PI, [bc])
        self.epsc = A.f32(1)
        self.halfpi = A.f32(1)
        self.memset("dve", self.epsc, EPS, [bc])
        self.memset("dve", self.halfpi, PI / 2, [bc])
        self.g1col = A.f32(c.KD)
        self.g2col = A.f32(c.KD)
        bl = p.buf("cload")
        p.dma("sync", self.g1col, d["ln1_g"].rearrange("(c p) -> p c", p=128), bl, partial=True, allow_slow_non_contiguous=True)
        p.dma("sync", self.g2col, d["ln2_g"].rearrange("(c p) -> p c", p=128), bl, partial=True, allow_slow_non_contiguous=True)
        self.gq2 = A.f32(1)
        self.gk2 = A.f32(1)
        for (ap, nm) in ((self.gq2, "q_norm_g"), (self.gk2, "k_norm_g")):
            src = d[nm].rearrange("(p o) -> p o", o=1)
            p.dma("sync", ap[0:64, :], src, bl, partial=True, allow_slow_non_contiguous=True)
            p.dma("sync", ap[64:128, :], src, bl, partial=True, allow_slow_non_contiguous=True)
        if int(os.environ.get("KCUT", "99")) <= 1:
            self.A0 = A.off
            return
        self.slng = A.f32(1)
        p.dma("sync", self.slng, d["subln_g"].rearrange("(p o) -> p o", o=1), bl, partial=True, allow_slow_non_contiguous=True)
        self.kvalid = A.f32(c.NCTX)
        p.dma("sync", self.kvalid, d["kvalid"], bl, partial=True)
        self.rel31 = A.f32(c.H)
        p.dma("sync", self.rel31, d["rel_bias"][31, :].partition_broadcast(128), bl, partial=True)
        lam4 = A.f32(256)
        for i, nm in enumerate(("lam_q1", "lam_k1", "lam_q2", "lam_k2")):
            p.dma("sync", lam4[:, i * 64:(i + 1) * 64], d[nm].partition_broadcast(128), bl, partial=True)
        relb = A.f32(c.H)
        ohc = A.f32(FL)
        ngm = A.f32(FL)
        p.dma("sync", relb[0:32, :], d["rel_bias"], bl, partial=True)
        p.dma("sync", ohc[0:32, :], d["ohc"], bl, partial=True)
        p.dma("sync", ngm[0:c.H, :], d["negmask"], bl, partial=True)
        if int(os.environ.get("KCUT", "99")) <= 2:
            self.A0 = A.off
            return
        self.ts("dve", self.slng, self.slng, 1.0 - c.lambda_init, None, ALU.mult, None, [bl], [bl])
        lt = A.f32(128)
        l2 = A.f32(2)
        self.neglam = A.f32(1)
        self.tt("dve", lt[:, 0:64], lam4[:, 0:64], lam4[:, 64:128], ALU.mult, [bl], [bc])
        self.tt("dve", lt[:, 64:128], lam4[:, 128:192], lam4[:, 192:256], ALU.mult, [bl], [bc])
        p.op("dve", "tensor_reduce", dict(out=l2[:, 0:1], in_=lt[:, 0:64], axis=AX.X, op=ALU.add), [bc], [bc])
        p.op("dve", "tensor_reduce", dict(out=l2[:, 1:2], in_=lt[:, 64:128], axis=AX.X, op=ALU.add), [bc], [bc])
        self.act(l2, l2, AF.Exp, [bc], [bc])
        self.tt("dve", self.neglam, l2[:, 1:2], l2[:, 0:1], ALU.subtract, [bc], [bc])
        self.ts("dve", self.neglam, self.neglam, -c.lambda_init, None, ALU.add, None, [bc], [bc])
        if int(os.environ.get("KCUT", "99")) <= 3:
            self.A0 = A.off
            return
        frow = A.f32(FL)
        self.mm(self.ps[0][0:c.H, 0:FL], relb[0:32, 0:c.H], ohc[0:32, :], True, True, [bl], [self.psb[0]])
        self.tt("dve", frow[0:c.H, :], self.ps[0][0:c.H, 0:FL], ngm[0:c.H, :], ALU.add, [self.psb[0], bl], [bc])
        if int(os.environ.get("KCUT", "99")) <= 4:
            self.A0 = A.off
            return
        brr = p.buf("rrepst")
        bdr = self.p.buf("dram_rrep")
        src = frow[0:c.H, :].unsqueeze(1).broadcast_to([c.H, 128, FL])
        p.dma("sync", self.s["rrep"], src, bc, bdr, store=True)
        self.TD = A.f32(c.H * 128)
        self.TP = A.f32(c.H * 128)
        bt = p.buf("toep")
        rt = self.s["rrep"].tensor
        p.dma("sync", self.TD.rearrange("p (h q) -> p h q", q=128),
              bass.AP(tensor=rt, offset=127, ap=[[FL - 1, 128], [128 * FL, c.H], [1, 128]]), bt, bdr, partial=True)
        p.dma("sync", self.TP.rearrange("p (h q) -> p h q", q=128),
              bass.AP(tensor=rt, offset=255, ap=[[FL - 1, 128], [128 * FL, c.H], [1, 128]]), bt, bdr, partial=True)
        self.b_toep = bt
        self.b_cload = bl
        self.A0 = A.off
        p.barrier()

    def ctx_supers(self):
        c = self.c
        pre = [(t0, tw, False) for (t0, tw) in tok_tiles(c.NPRE)]
        own = [(c.O0 + t0, tw, True) for (t0, tw) in tok_tiles(c.NOWN)]
        sup = []
        for grp in super_tiles([(a, b) for (a, b, _) in pre], c.SMAX):
            sup.append((grp, False))
        for grp in super_tiles([(a, b) for (a, b, _) in own], c.SMAX):
            sup.append((grp, True))
        return sup

    def own_supers(self):
        c = self.c
        return super_tiles(tok_tiles(c.NOWN), c.SMAX)

    def phase1(self):
        c, p, A, d, s = self.c, self.p, self.A, self.d, self.s
        KD = c.KD
        for (grp, own) in self.ctx_supers():
            A.off = self.A0
            S0 = grp[0][0]
            S = sum(t[1] for t in grp)
            a1T = A.bf16(KD * S).rearrange("p (c t) -> p c t", t=S)
            b_a1 = p.buf("a1T")
            mark = A.off
            xin = [A.f32(c.D) for _ in range(2)]
            b_x = p.bufs(2, "xin")
            junk = A.bf16(c.D)
            b_j = p.buf("junk")
            hst2 = A.f32(c.D)
            hst = hst2.rearrange("p (c t) -> p c t", t=128)
            b_h = p.buf("hst")
            ssq = A.f32(2)
            rs = A.f32(2)
            b_s = p.bufs(2, "ssq")
            diag = [A.f32(128) for _ in range(2)]
            rrep = [A.f32(128) for _ in range(2)]
            b_r = p.bufs(2, "rrep")
            trb = Rot([5, 6, 7])
            for bi in range(S // 128):
                t0 = S0 + bi * 128
                j = bi % 2
                p.dma("sync", xin[j], d["hp"][t0:t0 + 128, :], b_x[j])
                self.act(junk, xin[j], AF.Square, [b_x[j]], [b_j, b_s[j]], accum_out=ssq[:, j:j + 1])
                self.act(rs[:, j:j + 1], ssq[:, j:j + 1], AF.Sqrt, [b_s[j], self.b_const], [b_r[j]], bias=self.epsc[:, 0:1], scale=1.0 / c.D)
                p.op("dve", "reciprocal", dict(out=rs[:, j:j + 1], in_=rs[:, j:j + 1]), [b_r[j]], [b_r[j]])
                self.ts("dve", diag[j], self.ident, rs[:, j:j + 1], None, ALU.mult, None, [b_r[j], self.b_const], [b_r[j]])
                self.mm(self.ps[4][:, 0:128], self.onesf, diag[j], True, True, [b_r[j], self.b_const], [self.psb[4]])
                self.copy("act", rrep[j], self.ps[4][:, 0:128], [self.psb[4]], [b_r[j]])
                for c0 in range(0, KD, 4):
                    n4 = min(4, KD - c0)
                    bk = trb.next()
                    for q in range(n4):
                        self.tr(self.ps[bk][:, q * 128:(q + 1) * 128], xin[j][:, (c0 + q) * 128:(c0 + q + 1) * 128],
                                [b_x[j]], [self.psb[bk]])
                    for q in range(n4):
                        self.stt(a1T[:, c0 + q, bi * 128:(bi + 1) * 128], self.ps[bk][:, q * 128:(q + 1) * 128],
                                 self.g1col[:, c0 + q:c0 + q + 1], rrep[j], ALU.mult, ALU.mult,
                                 [self.psb[bk], b_r[j], self.b_cload], [b_a1])
                    if own and not os.environ.get("KNOC"):
                        self.copy("dve", hst2[:, c0 * 128:(c0 + n4) * 128], self.ps[bk][:, 0:n4 * 128],
                                  [self.psb[bk]], [b_h])
                if own and not os.environ.get("KNOH"):
                    to = t0 - c.O0
                    p.dma("sync", s["hT"][:, to:to + 128].rearrange("(c p) t -> p c t", p=128), hst, b_h, self.sb["hT"], store=True)
            p.barrier()
            if self.cut():
                return
            A.off = mark
            wbuf = [A.bf16(KD * c.MB).rearrange("p (c n) -> p c n", n=c.MB) for _ in range(2)]
            b_w = p.bufs(2, "wbuf")
            NST = 3
            stf = Rot([(A.f32(512), p.buf("stf")) for _ in range(NST)])
            stb = Rot([(A.bf16(512), p.buf("stb")) for _ in range(NST)])
            sqt = Rot([(A.bf16(512), p.buf("sqt")) for _ in range(2)])
            rqt = Rot([(A.f32(512), p.buf("rqt")) for _ in range(2)])
            mainb = Rot([0, 1, 2, 3, 4])
            normb = Rot([5, 6])
            regions = [("u", c.cU, c.DS), ("q", c.cQ, c.QKW), ("k", c.cK, c.QKW), ("v", c.cV, c.DATT),
                       ("gs", c.cGS, c.D), ("ga", c.cGA, c.D)]
            blocks = []
            for (typ, cs, cw) in regions:
                if not own and typ not in ("u", "k", "v"):
                    continue
                if typ not in os.environ.get("KREG", "u,q,k,v,gs,ga").split(","):
                    continue
                for c0 in range(cs, cs + cw, c.MB):
                    blocks.append((typ, cs, c0))
            wi = 0
            for (typ, cs, c0) in blocks:
                wj = wi % 2
                wi += 1
                p.dma("pool", wbuf[wj], d["w_in"][:, c0:c0 + c.MB].rearrange("(c p) n -> p c n", p=128), b_w[wj])
                if typ == "v":
                    for tb in range(S // 128):
                        bk = mainb.next()
                        for kc in range(KD):
                            self.mm(self.ps[bk][:, 0:c.MB], a1T[:, kc, tb * 128:(tb + 1) * 128], wbuf[wj][:, kc, :],
                                    kc == 0, kc == KD - 1, [b_a1, b_w[wj]], [self.psb[bk]])
                        self.flush_pe()
                        (st, bs) = stb.next()
                        self.copy("act", st[:, 0:c.MB], self.ps[bk][:, 0:c.MB], [self.psb[bk]], [bs])
                        t0 = S0 + tb * 128
                        p.dma("sync", s["v"][t0:t0 + 128, c0 - cs:c0 - cs + c.MB], st[:, 0:c.MB], bs, self.sb["v"], store=True)
                    continue
                for sub in range(c.NSUB):
                    m = (c0 - cs) // 128 + sub
                    tl = 0
                    for (t0, tw) in grp:
                        bk = mainb.next()
                        for kc in range(KD):
                            self.mm(self.ps[bk][:, 0:tw], wbuf[wj][:, kc, sub * 128:(sub + 1) * 128], a1T[:, kc, tl:tl + tw],
                                    kc == 0, kc == KD - 1, [b_a1, b_w[wj]], [self.psb[bk]])
                        self.flush_pe()
                        pm = self.ps[bk][:, 0:tw]
                        if typ == "u":
                            (st, bs) = stf.next()
                            self.copy("act", st[:, 0:tw], pm, [self.psb[bk]], [bs])
                            p.dma("sync", s["uT"][m * 128:(m + 1) * 128, t0:t0 + tw], st[:, 0:tw], bs, self.sb["uT"], store=True)
                        elif typ in ("q", "k"):
                            (sq, bsq) = sqt.next()
                            (rq, brq) = rqt.next()
                            (st, bs) = stb.next()
                            nb = normb.next()
                            gcol = self.gq2 if typ == "q" else self.gk2
                            self.act(sq[:, 0:tw], pm, AF.Square, [self.psb[bk]], [bsq])

                            def deferred(nb=nb, sq=sq, bsq=bsq, tw=tw, rq=rq, brq=brq, st=st, bs=bs, pm=pm, bk=bk,
                                         gcol=gcol, typ=typ, m=m, t0=t0):
                                self.mm(self.ps[nb][:, 0:tw], self.blk, sq[:, 0:tw], True, True, [bsq, self.b_const], [self.psb[nb]])
                                self.act(rq[:, 0:tw], self.ps[nb][:, 0:tw], AF.Sqrt, [self.psb[nb], self.b_const], [brq],
                                         bias=self.epsc[:, 0:1], scale=1.0 / 64)
                                p.op("dve", "reciprocal", dict(out=rq[:, 0:tw], in_=rq[:, 0:tw]), [brq], [brq])
                                self.stt(st[:, 0:tw], pm, gcol[:, 0:1], rq[:, 0:tw], ALU.mult, ALU.mult,
                                         [self.psb[bk], brq, self.b_cload], [bs])
                                if typ == "q":
                                    to = t0 - c.O0
                                    p.dma("sync", s["qT"][m, :, to:to + tw], st[:, 0:tw], bs, self.sb["qT"], store=True)
                                else:
                                    p.dma("sync", s["kT"][m, :, t0:t0 + tw], st[:, 0:tw], bs, self.sb["kT"], store=True)
                            self.pe_def.append(deferred)
                        else:
                            (st, bs) = stb.next()
                            self.act(st[:, 0:tw], pm, AF.Sigmoid, [self.psb[bk]], [bs])
                            to = t0 - c.O0
                            dst = s["sgs"] if typ == "gs" else s["sga"]
                            p.dma("sync", dst[m * 128:(m + 1) * 128, to:to + tw], st[:, 0:tw], bs,
                                  self.sb["sgs" if typ == "gs" else "sga"], store=True)
                        tl += tw
            self.flush_pe()
            p.barrier()
            if self.cut():
                return

    def rsqrt_ps(self, out, in_, scale, Rb, Wb):
        self.act(out, in_, AF.Sqrt, list(Rb) + [self.b_const], [Wb], bias=self.epsc[:, 0:1], scale=scale)
        self.p.op("dve", "reciprocal", dict(out=out, in_=out), [Wb], [Wb])

    def reduce_angle(self, eng, out, a, ni, Ra, Wn, Wo):
        self.ts(eng, ni, a, 1.0 / TWO_PI, None, ALU.mult, None, Ra, [Wn])
        self.copy(eng, out, ni, [Wn], [Wo])
        self.stt(out, out, -TWO_PI, a, ALU.mult, ALU.add, list(Ra) + [Wo], [Wo])
        self.ts(eng, out, out, -PI, PI, ALU.max, ALU.min, [Wo], [Wo])

    def phase2(self):
        c, p, A, d, s = self.c, self.p, self.A, self.d, self.s
        G, KS = c.G, c.KS
        I32 = mybir.dt.int32
        bs = p.buf("ssm_setup")
        bl = p.buf("ssm_load")
        ZB = A.f32(G * 16)
        ZsB = A.f32(G * 16)
        rr = A.f32(G)
        thr = A.f32(G)
        A0thrT = A.f32(G)
        dsk = A.f32(KS)
        bgl = A.f32(KS)
        maskc = A.f32(8)
        iota = A.f32(512)
        chunks = [(t0, tw, False) for (t0, tw) in tok_tiles(c.NPRE)] + [(c.O0 + t0, tw, True) for (t0, tw) in tok_tiles(c.NOWN)]
        NCH = len(chunks)
        offs = A.f32(NCH * G)
        gT = A.bf16(KS * c.NO).rearrange("p (c t) -> p c t", t=c.NO)
        b_gT = p.buf("gT")
        Blz = A.bf16(8 * 128).rearrange("p (g m) -> p g m", m=128)
        Blzs = A.bf16(8 * 128).rearrange("p (g m) -> p g m", m=128)
        Cl1 = A.bf16(8 * 128).rearrange("p (g m) -> p g m", m=128)
        Cl2 = A.bf16(8 * 128).rearrange("p (g m) -> p g m", m=128)
        mark = A.off
        lre = A.f32(G)
        lim = A.f32(G)
        dtr = A.f32(G)
        for (ap, nm) in ((lre, "lam_re"), (lim, "lam_im")):
            src = d[nm].rearrange("g p -> p g")
            p.dma("sync", ap[0:64, :], src, bl, partial=True, allow_slow_non_contiguous=True)
            p.dma("sync", ap[64:128, :], src, bl, partial=True, allow_slow_non_contiguous=True)
        p.dma("sync", dtr, d["log_dt"].partition_broadcast(128), bl, partial=True)
        p.dma("sync", dsk, d["d_skip"].rearrange("(k p) -> p k", p=128), bl, partial=True, allow_slow_non_contiguous=True)
        p.dma("sync", bgl, d["b_glu"].rearrange("(k p) -> p k", p=128), bl, partial=True, allow_slow_non_contiguous=True)
        X1 = A.f32(G * 16)
        X2 = A.f32(G * 16)
        X13 = X1.rearrange("p (g c) -> p g c", c=16)
        X23 = X2.rearrange("p (g c) -> p g c", c=16)
        bre = d["b_re"].rearrange("g p c -> p g c")
        bim = d["b_im"].rearrange("g p c -> p g c")
        p.dma("sync", X13[0:64], bre, bl, partial=True)
        p.dma("sync", X13[64:128], bim, bl, partial=True)
        p.dma("sync", X23[0:64], bim, bl, partial=True)
        p.dma("sync", X23[64:128], bre, bl, partial=True)
        if os.environ.get("KS2") == "1":
            p.barrier()
            return
        self.act(dtr, dtr, AF.Exp, [bl], [bs])
        tmp = A.f32(G)
        self.tt("dve", tmp, lre, dtr, ALU.mult, [bl, bs], [bs])
        self.act(rr, tmp, AF.Exp, [bs], [bs])
        th = A.f32(G)
        self.tt("dve", th, lim, dtr, ALU.mult, [bl, bs], [bs])
        if os.environ.get("KS2") == "2":
            p.barrier()
            return
        ni = A.f32(G).bitcast(I32)
        self.reduce_angle("dve", thr, th, ni, [bs], bs, bs)
        if os.environ.get("KS2") == "3":
            p.barrier()
            return
        sn = A.f32(G)
        cs = A.f32(G)
        ab = A.f32(G)
        self.act(sn, thr, AF.Sin, [bs], [bs])
        self.ts("dve", ab, thr, -1.0, None, ALU.mult, None, [bs], [bs])
        self.tt("dve", ab, ab, thr, ALU.max, [bs], [bs])
        self.act(cs, ab, AF.Sin, [bs, self.b_const], [bs], bias=self.halfpi[:, 0:1], scale=-1.0)
        ar1 = A.f32(G)
        ai = A.f32(G)
        self.tt("dve", ar1, rr, cs, ALU.mult, [bs], [bs])
        self.ts("dve", ar1, ar1, -1.0, None, ALU.add, None, [bs], [bs])
        self.tt("dve", ai, rr, sn, ALU.mult, [bs], [bs])
        if os.environ.get("KS2") == "4":
            p.barrier()
            return
        den = A.f32(G)
        t2 = A.f32(G)
        self.tt("dve", den, lre, lre, ALU.mult, [bl], [bs])
        self.tt("dve", t2, lim, lim, ALU.mult, [bl], [bs])
        self.tt("dve", den, den, t2, ALU.add, [bs], [bs])
        p.op("dve", "reciprocal", dict(out=den, in_=den), [bs], [bs])
        cre = A.f32(G)
        cim = A.f32(G)
        self.tt("dve", cre, ar1, lre, ALU.mult, [bs, bl], [bs])
        self.tt("dve", t2, ai, lim, ALU.mult, [bs, bl], [bs])
        self.tt("dve", cre, cre, t2, ALU.add, [bs], [bs])
        self.tt("dve", cre, cre, den, ALU.mult, [bs], [bs])
        self.tt("dve", cim, ai, lre, ALU.mult, [bs, bl], [bs])
        self.tt("dve", t2, ar1, lim, ALU.mult, [bs, bl], [bs])
        self.tt("dve", cim, cim, t2, ALU.subtract, [bs], [bs])
        self.tt("dve", cim, cim, den, ALU.mult, [bs], [bs])
        if os.environ.get("KS2") == "5":
            p.barrier()
            return
        s1c = A.f32(G)
        s2c = A.f32(G)
        self.ts("dve", s1c, cim, self.s1[:, 0:1], None, ALU.mult, None, [bs, self.b_const], [bs])
        self.ts("dve", s2c, cre, self.s2[:, 0:1], None, ALU.mult, None, [bs, self.b_const], [bs])
        T1 = A.f32(G * 16)
        T13 = T1.rearrange("p (g c) -> p g c", c=16)
        bc3 = lambda ap: ap.unsqueeze(2).broadcast_to([128, G, 16])
        ZB3 = ZB.rearrange("p (g c) -> p g c", c=16)
        ZsB3 = ZsB.rearrange("p (g c) -> p g c", c=16)
        for cc in range(16):
            self.tt("dve", ZB3[:, :, cc], X13[:, :, cc], cre, ALU.mult, [bs, bl], [bs])
            self.tt("dve", T13[:, :, cc], X23[:, :, cc], s1c, ALU.mult, [bs, bl], [bs])
        self.tt("dve", ZB, ZB, T1, ALU.add, [bs], [bs])
        for cc in range(16):
            self.tt("dve", ZsB3[:, :, cc], X23[:, :, cc], s2c, ALU.mult, [bs, bl], [bs])
            self.tt("dve", T13[:, :, cc], X13[:, :, cc], cim, ALU.mult, [bs, bl], [bs])
        self.tt("dve", ZsB, ZsB, T1, ALU.add, [bs], [bs])
        if os.environ.get("KS2") == "6":
            p.barrier()
            return
        for ci, (t0, tw, own) in enumerate(chunks):
            o = offs[:, ci * G:(ci + 1) * G]
            self.ts("dve", tmp, thr, float(t0), None, ALU.mult, None, [bs], [bs])
            self.reduce_angle("dve", o, tmp, ni, [bs], bs, bs)
        thrT = A0thrT
        self.ts("dve", thrT, thr, 1.0 / TWO_PI, None, ALU.mult, None, [bs], [bs])
        self.ts("dve", offs, offs, 1.0 / TWO_PI, None, ALU.mult, None, [bs], [bs])
        p.op("pool", "iota", dict(out=maskc, pattern=[[16, 8]], base=0, channel_multiplier=-1,
                                  allow_small_or_imprecise_dtypes=True), (), [bs])
        mk2 = A.f32(8)
        self.ts("dve", mk2, maskc, 0.0, None, ALU.is_le, None, [bs], [bs])
        self.ts("dve", maskc, maskc, -16.0, None, ALU.is_gt, None, [bs], [bs])
        self.tt("dve", maskc, maskc, mk2, ALU.mult, [bs], [bs])
        p.op("pool", "iota", dict(out=iota, pattern=[[1, 512]], base=0, channel_multiplier=0,
                                  allow_small_or_imprecise_dtypes=True), (), [bs])
        for t in (Cl1, Cl2):
            self.memset("dve", t, 0.0, [bs])
        p.barrier()
        if self.cut():
            return
        A.off = mark
        uTk = A.bf16(c.NT)
        uTf = A.f32(c.NO)
        b_uk = p.buf("uTk")
        b_uf = p.buf("uTf")
        CC = A.f32(128)
        CC2 = A.f32(128)
        b_cc = p.buf("CC")
        b_lhs = p.buf("lhs")
        R2 = lambda n, f, nm: Rot([(f(n), p.buf(nm)) for _ in range(2)])
        r_a = R2(512, A.f32, "a")
        r_ni = Rot([(A.f32(512).bitcast(I32), p.buf("ni")) for _ in range(2)])
        r_r = R2(512, A.f32, "r")
        r_ab = Rot([(A.f32(512), p.buf("ab")) for _ in range(4)])
        r_S = Rot([(A.f32(512), p.buf("S2")) for _ in range(5)])
        r_C = Rot([(A.f32(512), p.buf("C2")) for _ in range(4)])
        r_f = Rot([(A.f32(512), p.buf("f")) for _ in range(3)])
        r_W1 = R2(512, A.f32, "W1")
        r_W2 = R2(512, A.f32, "W2")
        r_W = Rot([(A.f32(512), p.buf("W")) for _ in range(3)])
        r_w = R2(512, A.f32, "w")
        r_V1 = R2(512, A.bf16, "V1")
        r_V2 = R2(512, A.bf16, "V2")
        carry = A.f32(8)
        b_carry = p.bufs(8, "carry")
        r_yv = R2(512, A.f32, "yv")
        r_x2 = R2(512, A.f32, "x2")
        r_sg = R2(512, A.f32, "sg")
        zb = Rot([0, 1])
        zsb = Rot([2, 3])
        yb = Rot([4, 5])
        for k in range(KS):
            rows = slice(k * 128, (k + 1) * 128)
            p.dma("pool", uTk, s["uT"][rows, :], b_uk, self.sb["uT"])
            p.dma("sync", uTf, s["uT"][rows, c.O0:], b_uf, self.sb["uT"])
            p.dma("sync", CC[:, 0:64], d["c_re"][rows, :], b_cc, partial=True)
            p.dma("sync", CC[:, 64:128], d["c_im"][rows, :], b_cc, partial=True)
            p.dma("sync", CC2[:, 0:64], d["c_im"][rows, :], b_cc, partial=True)
            p.dma("sync", CC2[:, 64:128], d["c_re"][rows, :], b_cc, partial=True)
            self.tr(self.ps[6][:, 0:128], CC, [b_cc], [self.psb[6]])
            self.tr(self.ps[6][:, 128:256], CC2, [b_cc], [self.psb[6]])
            self.tr(self.ps[7][:, 0:128], ZB[:, k * 128:(k + 1) * 128], [bs], [self.psb[7]])
            self.tr(self.ps[7][:, 128:256], ZsB[:, k * 128:(k + 1) * 128], [bs], [self.psb[7]])
            for gi in range(8):
                cs_ = slice(gi * 16, (gi + 1) * 16)
                self.ts("dve", Cl1[:, gi, cs_], self.ps[6][:, gi * 16:(gi + 1) * 16], self.s2[:, 0:1], None, ALU.mult, None,
                        [self.psb[6], self.b_const], [b_lhs])
                self.ts("dve", Cl2[:, gi, cs_], self.ps[6][:, 128 + gi * 16:128 + (gi + 1) * 16], -1.0, None, ALU.mult, None,
                        [self.psb[6]], [b_lhs])
                self.ts("dve", Blz[:, gi, :], self.ps[7][:, 0:128], maskc[:, gi:gi + 1], None, ALU.mult, None,
                        [self.psb[7], bs], [b_lhs])
                self.ts("dve", Blzs[:, gi, :], self.ps[7][:, 128:256], maskc[:, gi:gi + 1], None, ALU.mult, None,
                        [self.psb[7], bs], [b_lhs])
            iters = []
            for ci, (t0, tw, own) in enumerate(chunks):
                for gi in range(8):
                    iters.append((ci, t0, tw, own, gi))
            NI = len(iters)
            ybk_of = {}
            st = [dict() for _ in range(NI)]
            MAGIC = 12582912.0

            def a1(i):
                (ci, t0, tw, own, gi) = iters[i]
                g = k * 8 + gi
                (a, b_a) = r_a.next()
                (nn, b_n) = r_r.next()
                (f, b_f) = r_f.next()
                (ab_, b_ab) = r_ab.next()
                (S2, b_S) = r_S.next()
                self.ts("dve", a[:, 0:tw], iota[:, 0:tw], thrT[:, g:g + 1], offs[:, ci * G + g:ci * G + g + 1],
                        ALU.mult, ALU.add, [bs], [b_a])
                self.ts("dve", nn[:, 0:tw], a[:, 0:tw], MAGIC, None, ALU.add, None, [b_a], [b_n])
                self.ts("dve", nn[:, 0:tw], nn[:, 0:tw], MAGIC, None, ALU.subtract, None, [b_n], [b_n])
                self.tt("dve", f[:, 0:tw], a[:, 0:tw], nn[:, 0:tw], ALU.subtract, [b_a, b_n], [b_f])
                self.act(S2[:, 0:tw], f[:, 0:tw], AF.Sin, [b_f], [b_S], scale=TWO_PI)
                self.act(ab_[:, 0:tw], f[:, 0:tw], AF.Sin, [b_f], [b_ab], scale=PI)
                self.act(ab_[:, 0:tw], ab_[:, 0:tw], AF.Square, [b_ab], [b_ab])
                st[i].update(S2=S2, b_S=b_S, ab=ab_, b_ab=b_ab)

            def zmm(i):
                (ci, t0, tw, own, gi) = iters[i]
                z1, z2 = zb.next(), zsb.next()
                self.mm(self.ps[z1][:, 0:tw], Blz[:, gi, :], uTk[:, t0:t0 + tw], True, True, [b_lhs, b_uk], [self.psb[z1]])
                self.mm(self.ps[z2][:, 0:tw], Blzs[:, gi, :], uTk[:, t0:t0 + tw], True, True, [b_lhs, b_uk], [self.psb[z2]])
                st[i].update(z1=z1, z2=z2)

            def c2(i):
                (ci, t0, tw, own, gi) = iters[i]
                (C2, b_C) = r_C.next()
                self.ts("dve", C2[:, 0:tw], st[i]["ab"][:, 0:tw], -2.0, 1.0, ALU.mult, ALU.add, [st[i]["b_ab"]], [b_C])
                st[i].update(C2=C2, b_C=b_C)

            def wstage(i):
                (ci, t0, tw, own, gi) = iters[i]
                d_ = st[i]
                (W1, b_W1) = r_W1.next()
                (W2, b_W2) = r_W2.next()
                (W, b_W) = r_W.next()
                self.tt("dve", W1[:, 0:tw], self.ps[d_["z1"]][:, 0:tw], d_["C2"][:, 0:tw], ALU.mult, [self.psb[d_["z1"]], d_["b_C"]], [b_W1])
                self.tt("dve", W2[:, 0:tw], self.ps[d_["z2"]][:, 0:tw], d_["S2"][:, 0:tw], ALU.mult, [self.psb[d_["z2"]], d_["b_S"]], [b_W2])
                self.tt("pool", W[:, 0:tw], W1[:, 0:tw], W2[:, 0:tw], ALU.add, [b_W1, b_W2], [b_W])
                d_.update(W=W, b_W=b_W)

            def scanstage(i):
                (ci, t0, tw, own, gi) = iters[i]
                d_ = st[i]
                g = k * 8 + gi
                S2, b_S, C2, b_C, W, b_W = d_["S2"], d_["b_S"], d_["C2"], d_["b_C"], d_["W"], d_["b_W"]
                if own and gi == 0:
                    ybk_of[ci] = yb.next()
                ybk = ybk_of.get(ci)
                (w, b_w) = r_w.next()
                init = 0.0 if ci == 0 else carry[:, gi:gi + 1]
                Rl = [b_W, bs] + ([] if ci == 0 else [b_carry[gi]])
                p.op("dve", "tensor_tensor_scan",
                     dict(out=w[:, 0:tw], data0=rr[:, g:g + 1].broadcast_to([128, tw]), data1=W[:, 0:tw], initial=init,
                          op0=ALU.mult, op1=ALU.add), Rl, [b_w])
                if ci < NCH - 1:
                    self.copy("act", carry[:, gi:gi + 1], w[:, tw - 1:tw], [b_w], [b_carry[gi]])
                if own:
                    (V1, b_V1) = r_V1.next()
                    (V2, b_V2) = r_V2.next()
                    self.tt("pool", V1[:, 0:tw], C2[:, 0:tw], w[:, 0:tw], ALU.mult, [b_C, b_w], [b_V1])
                    self.tt("pool", V2[:, 0:tw], S2[:, 0:tw], w[:, 0:tw], ALU.mult, [b_S, b_w], [b_V2])
                    self.mm(self.ps[ybk][:, 0:tw], Cl1[:, gi, :], V1[:, 0:tw], gi == 0, False, [b_lhs, b_V1], [self.psb[ybk]])
                    self.mm(self.ps[ybk][:, 0:tw], Cl2[:, gi, :], V2[:, 0:tw], False, gi == 7, [b_lhs, b_V2], [self.psb[ybk]])
                if own and gi == 7:
                    to = t0 - c.O0
                    (yv, b_yv) = r_yv.next()
                    (x2, b_x2) = r_x2.next()
                    (sg, b_sg) = r_sg.next()
                    self.stt(yv[:, 0:tw], uTf[:, to:to + tw], dsk[:, k:k + 1], self.ps[ybk][:, 0:tw], ALU.mult, ALU.add,
                             [b_uf, bl, self.psb[ybk]], [b_yv])
                    self.act(x2[:, 0:tw], yv[:, 0:tw], AF.Square, [b_yv], [b_x2])
                    self.ts("dve", x2[:, 0:tw], x2[:, 0:tw], 0.044715, 1.0, ALU.mult, ALU.add, [b_x2], [b_x2])
                    self.tt("pool", x2[:, 0:tw], x2[:, 0:tw], yv[:, 0:tw], ALU.mult, [b_x2, b_yv], [b_x2])
                    self.act(sg[:, 0:tw], x2[:, 0:tw], AF.Sigmoid, [b_x2], [b_sg], scale=1.5957691216057308)
                    self.tt("pool", gT[:, k, to:to + tw], yv[:, 0:tw], sg[:, 0:tw], ALU.mult, [b_yv, b_sg], [b_gT])
                st[i].clear()

            a1(0)
            if NI > 1:
                a1(1)
            zmm(0)
            c2(0)
            for i in range(NI + 1):
                if i + 2 < NI:
                    a1(i + 2)
                if i + 1 < NI:
                    zmm(i + 1)
                if i < NI:
                    wstage(i)
                if i + 1 < NI:
                    c2(i + 1)
                if 0 <= i - 1 < NI:
                    scanstage(i - 1)
        p.barrier()
        A.off = mark
        wgl = A.bf16(KS * c.DS).rearrange("p (c n) -> p c n", n=c.DS)
        b_wg = p.buf("wglu")
        p.dma("pool", wgl, d["w_glu"].rearrange("(c p) n -> p c n", p=128), b_wg)
        r_sig = Rot([(A.f32(512), p.buf("sig")) for _ in range(2)])
        r_st = Rot([(A.bf16(512), p.buf("yst")) for _ in range(3)])
        mb = Rot([0, 1, 2, 3])
        for m in range(KS):
            for (to, tw) in tok_tiles(c.NOWN):
                bk = mb.next()
                for kc in range(KS):
                    self.mm(self.ps[bk][:, 0:tw], wgl[:, kc, m * 128:(m + 1) * 128], gT[:, kc, to:to + tw], kc == 0, kc == KS - 1,
                            [b_wg, b_gT], [self.psb[bk]])
                (sg, b_sg) = r_sig.next()
                (st, b_st) = r_st.next()
                self.act(sg[:, 0:tw], self.ps[bk][:, 0:tw], AF.Sigmoid, [self.psb[bk], bl], [b_sg], bias=bgl[:, m:m + 1])
                self.tt("pool", st[:, 0:tw], sg[:, 0:tw], gT[:, m, to:to + tw], ALU.mult, [b_sg, b_gT], [b_st])
                p.dma("sync", s["yssmT"][m * 128:(m + 1) * 128, to:to + tw], st[:, 0:tw], b_st, self.sb["yssmT"], store=True)

    def phase3(self):
        c, p, A, d, s = self.c, self.p, self.A, self.d, self.s
        H = c.H
        kTh = [A.bf16(c.NT) for _ in range(2)]
        qTh = [A.bf16(c.NO) for _ in range(2)]
        Vh = [A.bf16(c.NCTX * 128).rearrange("p (b e) -> p b e", e=128) for _ in range(2)]
        cbh = [A.f32(c.NCTX) for _ in range(2)]
        b_hd = p.bufs(2, "head")
        b_cb = p.bufs(2, "cbh")
        r_pt = Rot([(A.bf16(512), p.buf("pt")) for _ in range(4)])
        r_tmp = Rot([(A.f32(128), p.buf("tmp")) for _ in range(4)])
        r_rs = Rot([(A.f32(512), p.buf("rs")) for _ in range(2)])
        r_o = Rot([(A.f32(512), p.buf("o")) for _ in range(2)])
        r_o2 = Rot([(A.f32(512), p.buf("o2")) for _ in range(2)])
        r_sq = Rot([(A.bf16(512), p.buf("sq")) for _ in range(2)])
        r_rn = Rot([(A.f32(512), p.buf("rn")) for _ in range(2)])
        r_st = Rot([(A.bf16(512), p.buf("st")) for _ in range(2)])
        sbank = Rot([0, 1, 2, 3])
        TD3 = self.TD.rearrange("p (h q) -> p h q", q=128)
        TP3 = self.TP.rearrange("p (h q) -> p h q", q=128)
        for h in range(H):
            j = h % 2
            p.dma("sync", kTh[j], s["kT"][h], b_hd[j], self.sb["kT"], partial=False)
            p.dma("sync", qTh[j], s["qT"][h], b_hd[j], self.sb["qT"], partial=True)
            p.dma("sync", Vh[j], s["v"][:, h * 128:(h + 1) * 128].rearrange("(b p) e -> p b e", p=128), b_hd[j], self.sb["v"], partial=True)
            self.ts("dve", cbh[j], self.kvalid, self.rel31[:, h:h + 1], None, ALU.add, None, [self.b_cload], [b_cb[j]])
            for (to, tw) in tok_tiles(c.NOWN):
                nqb = tw // 128
                qb0 = c.NPRE + to // 128
                nkb = qb0 + nqb
                obk = [4, 5]
                smk = [6, 7]
                for kb in range(nkb):
                    col0 = max(0, kb - qb0) * 128
                    ncols = tw - col0
                    pts = []
                    for m in range(2):
                        sb_ = sbank.next()
                        pr = slice(m * 64, (m + 1) * 64)
                        self.mm(self.ps[sb_][:, 0:ncols], kTh[j][pr, kb * 128:(kb + 1) * 128], qTh[j][pr, to + col0:to + tw],
                                True, True, [b_hd[j]], [self.psb[sb_]])
                        (pt, b_pt) = r_pt.next()
                        near = []
                        qbs = max(qb0, kb)
                        ci = 0
                        for i in range(ncols // 128):
                            qb = qbs + i
                            if qb - kb <= 1:
                                near.append((i, qb - kb))
                            else:
                                break
                        far0 = len(near) * 128
                        lastb = None
                        for (i, dd) in near:
                            (tmp, b_tmp) = r_tmp.next()
                            T3 = TD3 if dd == 0 else TP3
                            self.stt(tmp, self.ps[sb_][:, i * 128:(i + 1) * 128], 0.125, T3[:, h, :], ALU.mult, ALU.add,
                                     [self.psb[sb_], self.b_toep], [b_tmp])
                            self.act(pt[:, i * 128:(i + 1) * 128], tmp, AF.Exp, [b_tmp, self.b_cload], [b_pt],
                                     bias=self.kvalid[:, kb:kb + 1])
                            lastb = b_tmp
                        if far0 < ncols:
                            Rl = [self.psb[sb_], b_cb[j]] + ([lastb] if lastb is not None else [])
                            self.act(pt[:, far0:ncols], self.ps[sb_][:, far0:ncols], AF.Exp, Rl, [b_pt],
                                     bias=cbh[j][:, kb:kb + 1], scale=0.125)
                        pts.append((pt, b_pt))
                    self.flush_pe()

                    def pv(pts=pts, kb=kb, col0=col0, ncols=ncols, tw=tw, j=j, nkb=nkb, obk=obk, smk=smk):
                        for m in range(2):
                            (pt, b_pt) = pts[m]
                            self.mm(self.ps[obk[m]][:, col0:tw], Vh[j][:, kb, :], pt[:, 0:ncols], kb == 0, kb == nkb - 1,
                                    [b_hd[j], b_pt], [self.psb[obk[m]]])
                            self.mm(self.ps[smk[m]][:, col0:tw], self.onesb, pt[:, 0:ncols], kb == 0, kb == nkb - 1,
                                    [self.b_const, b_pt], [self.psb[smk[m]]])
                    self.pe_def.append(pv)
                self.flush_pe()
                rsl = []
                for m in range(2):
                    (rs, b_rs) = r_rs.next()
                    self.ts("dve", rs[:, 0:tw], self.ps[smk[m]][:, 0:tw], 1e-30, None, ALU.max, None, [self.psb[smk[m]]], [b_rs])
                    p.op("dve", "reciprocal", dict(out=rs[:, 0:tw], in_=rs[:, 0:tw]), [b_rs], [b_rs])
                    rsl.append((rs, b_rs))
                (o1, b_o1) = r_o.next()
                (o2, b_o2) = r_o2.next()
                self.tt("dve", o1[:, 0:tw], self.ps[obk[0]][:, 0:tw], rsl[0][0][:, 0:tw], ALU.mult, [self.psb[obk[0]], rsl[0][1]], [b_o1])
                self.tt("dve", o2[:, 0:tw], self.ps[obk[1]][:, 0:tw], rsl[1][0][:, 0:tw], ALU.mult, [self.psb[obk[1]], rsl[1][1]], [b_o2])
                self.stt(o1[:, 0:tw], o2[:, 0:tw], self.neglam[:, 0:1], o1[:, 0:tw], ALU.mult, ALU.add, [b_o2, self.b_const], [b_o1])
                (sq, b_sq) = r_sq.next()
                (rn, b_rn) = r_rn.next()
                (st, b_st) = r_st.next()
                self.act(sq[:, 0:tw], o1[:, 0:tw], AF.Square, [b_o1], [b_sq])
                nb = sbank.next()
                self.mm(self.ps[nb][:, 0:tw], self.onesb, sq[:, 0:tw], True, True, [self.b_const, b_sq], [self.psb[nb]])
                self.rsqrt_ps(rn[:, 0:tw], self.ps[nb][:, 0:tw], 1.0 / 128, [self.psb[nb]], b_rn)
                self.stt(st[:, 0:tw], o1[:, 0:tw], self.slng[:, 0:1], rn[:, 0:tw], ALU.mult, ALU.mult, [b_o1, b_rn, self.b_cload], [b_st])
                p.dma("sync", s["yattT"][h * 128:(h + 1) * 128, to:to + tw], st[:, 0:tw], b_st, self.sb["yattT"], store=True)

    def phase4(self):
        c, p, A, d, s = self.c, self.p, self.A, self.d, self.s
        KD, KS = c.KD, c.KS
        KA = c.DATT // 128
        KB = KS + KA
        base = A.off
        for grp in self.own_supers():
            A.off = base
            S0 = grp[0][0]
            S = sum(t[1] for t in grp)
            mg = A.bf16(KD * S).rearrange("p (c t) -> p c t", t=S)
            b_mg = p.buf("merged")
            mark = A.off
            ybT = A.bf16(KB * S).rearrange("p (c t) -> p c t", t=S)
            b_yb = p.buf("ybT")
            p.dma("sync", ybT[:, 0:KS, :], s["yssmT"][:, S0:S0 + S].rearrange("(c p) t -> p c t", p=128), b_yb, self.sb["yssmT"], partial=True)
            p.dma("sync", ybT[:, KS:KB, :], s["yattT"][:, S0:S0 + S].rearrange("(c p) t -> p c t", p=128), b_yb, self.sb["yattT"], partial=True)
            wbr = [A.bf16(KB * 128).rearrange("p (c n) -> p c n", n=128) for _ in range(2)]
            b_w = p.bufs(2, "wbr")
            r_g1 = Rot([(A.bf16(512), p.buf("g1")) for _ in range(3)])
            r_g2 = Rot([(A.bf16(512), p.buf("g2")) for _ in range(3)])
            r_t1 = Rot([(A.f32(512), p.buf("t1")) for _ in range(2)])
            r_t2 = Rot([(A.f32(512), p.buf("t2")) for _ in range(2)])
            ba = Rot([0, 1, 2])
            bb = Rot([3, 4, 5])
            for m in range(KD):
                wj = m % 2
                p.dma("pool", wbr[wj], d["w_branch"][:, m * 128:(m + 1) * 128].rearrange("(c p) n -> p c n", p=128), b_w[wj])
                tl = 0
                for (to, tw) in grp:
                    ka, kb_ = ba.next(), bb.next()
                    for kc in range(KS):
                        self.mm(self.ps[ka][:, 0:tw], wbr[wj][:, kc, :], ybT[:, kc, tl:tl + tw], kc == 0, kc == KS - 1,
                                [b_w[wj], b_yb], [self.psb[ka]])
                    for kc in range(KA):
                        self.mm(self.ps[kb_][:, 0:tw], wbr[wj][:, KS + kc, :], ybT[:, KS + kc, tl:tl + tw], kc == 0, kc == KA - 1,
                                [b_w[wj], b_yb], [self.psb[kb_]])
                    (g1, b_g1) = r_g1.next()
                    (g2, b_g2) = r_g2.next()
                    (t1, b_t1) = r_t1.next()
                    (t2, b_t2) = r_t2.next()
                    p.dma("sync", g1[:, 0:tw], s["sgs"][m * 128:(m + 1) * 128, to:to + tw], b_g1, self.sb["sgs"])
                    p.dma("sync", g2[:, 0:tw], s["sga"][m * 128:(m + 1) * 128, to:to + tw], b_g2, self.sb["sga"])
                    self.tt("dve", t1[:, 0:tw], self.ps[ka][:, 0:tw], g1[:, 0:tw], ALU.mult, [self.psb[ka], b_g1], [b_t1])
                    self.tt("dve", t2[:, 0:tw], self.ps[kb_][:, 0:tw], g2[:, 0:tw], ALU.mult, [self.psb[kb_], b_g2], [b_t2])
                    self.tt("pool", mg[:, m, tl:tl + tw], t1[:, 0:tw], t2[:, 0:tw], ALU.add, [b_t1, b_t2], [b_mg])
                    tl += tw
            p.barrier()
            A.off = mark
            wo = [A.bf16(KD * c.MB).rearrange("p (c n) -> p c n", n=c.MB) for _ in range(2)]
            b_wo = p.bufs(2, "wo")
            acc = A.f32(S)
            b_acc = p.buf("acc")
            self.memset("dve", acc, 0.0, [b_acc])
            r_hb = Rot([(A.f32(512), p.buf("hb")) for _ in range(3)])
            r_h2 = Rot([(A.f32(512), p.buf("h2")) for _ in range(3)])
            r_a2 = Rot([(A.bf16(512), p.buf("a2")) for _ in range(3)])
            r_sq = Rot([(A.f32(512), p.buf("sq")) for _ in range(2)])
            mb = Rot([0, 1, 2, 3, 4])
            wi = 0
            for c0 in range(0, c.D, c.MB):
                wj = wi % 2
                wi += 1
                p.dma("pool", wo[wj], d["w_o"][:, c0:c0 + c.MB].rearrange("(c p) n -> p c n", p=128), b_wo[wj])
                for sub in range(c.NSUB):
                    m = c0 // 128 + sub
                    tl = 0
                    for (to, tw) in grp:
                        bk = mb.next()
                        for kc in range(KD):
                            self.mm(self.ps[bk][:, 0:tw], wo[wj][:, kc, sub * 128:(sub + 1) * 128], mg[:, kc, tl:tl + tw],
                                    kc == 0, kc == KD - 1, [b_wo[wj], b_mg], [self.psb[bk]])
                        (hb, b_hb) = r_hb.next()
                        (h2, b_h2) = r_h2.next()
                        (a2, b_a2) = r_a2.next()
                        (sq, b_sq) = r_sq.next()
                        p.dma("sync", hb[:, 0:tw], s["hT"][m * 128:(m + 1) * 128, to:to + tw], b_hb, self.sb["hT"])
                        self.tt("dve", h2[:, 0:tw], self.ps[bk][:, 0:tw], hb[:, 0:tw], ALU.add, [self.psb[bk], b_hb], [b_h2])
                        p.dma("sync", s["h2T"][m * 128:(m + 1) * 128, to:to + tw], h2[:, 0:tw], b_h2, self.sb["h2T"], store=True)
                        self.act(a2[:, 0:tw], h2[:, 0:tw], AF.Copy, [b_h2, self.b_cload], [b_a2], scale=self.g2col[:, m:m + 1])
                        p.dma("sync", s["a2T"][m * 128:(m + 1) * 128, to:to + tw], a2[:, 0:tw], b_a2, self.sb["a2T"], store=True)
                        self.act(sq[:, 0:tw], h2[:, 0:tw], AF.Square, [b_h2], [b_sq])
                        self.tt("pool", acc[:, tl:tl + tw], acc[:, tl:tl + tw], sq[:, 0:tw], ALU.add, [b_sq], [b_acc])
                        tl += tw
            tl = 0
            for (to, tw) in grp:
                bk = mb.next()
                (h2, b_h2) = r_h2.next()
                self.mm(self.ps[bk][:, 0:tw], self.onesf, acc[:, tl:tl + tw], True, True, [self.b_const, b_acc], [self.psb[bk]])
                self.rsqrt_ps(h2[:, 0:tw], self.ps[bk][:, 0:tw], 1.0 / c.D, [self.psb[bk]], b_h2)
                p.dma("sync", s["rstd2"][:, to:to + tw], h2[:, 0:tw], b_h2, self.sb["rstd2"], store=True)
                tl += tw
            p.barrier()

    def phase5(self):
        c, p, A, d, s = self.c, self.p, self.A, self.d, self.s
        KD = c.KD
        base = A.off
        for grp in self.own_supers():
            A.off = base
            S0 = grp[0][0]
            S = sum(t[1] for t in grp)
            a2 = A.bf16(KD * S).rearrange("p (c t) -> p c t", t=S)
            b_a2 = p.buf("a2")
            rst = A.f32(S)
            b_rst = p.buf("rst")
            p.dma("sync", a2, s["a2T"][:, S0:S0 + S].rearrange("(c p) t -> p c t", p=128), b_a2, self.sb["a2T"])
            p.dma("sync", rst, s["rstd2"][:, S0:S0 + S], b_rst, self.sb["rstd2"])
            wg = [A.bf16(KD * c.MB).rearrange("p (c n) -> p c n", n=c.MB) for _ in range(2)]
            wu = [A.bf16(KD * c.MB).rearrange("p (c n) -> p c n", n=c.MB) for _ in range(2)]
            b_wg = p.bufs(2, "wg")
            b_wu = p.bufs(2, "wu")
            r_t1 = Rot([(A.f32(512), p.buf("t1")) for _ in range(2)])
            r_sg = Rot([(A.f32(512), p.buf("sg")) for _ in range(2)])
            r_t3 = Rot([(A.f32(512), p.buf("t3")) for _ in range(2)])
            r_st = Rot([(A.bf16(512), p.buf("st")) for _ in range(3)])
            bg = Rot([0, 1, 2, 3])
            bu = Rot([4, 5, 6, 7])
            wi = 0
            for c0 in range(0, c.DFF, c.MB):
                wj = wi % 2
                wi += 1
                p.dma("pool", wg[wj], d["w_gate_up"][:, c0:c0 + c.MB].rearrange("(c p) n -> p c n", p=128), b_wg[wj])
                p.dma("pool", wu[wj], d["w_gate_up"][:, c.DFF + c0:c.DFF + c0 + c.MB].rearrange("(c p) n -> p c n", p=128), b_wu[wj])
                for sub in range(c.NSUB):
                    m = c0 // 128 + sub
                    tl = 0
                    for (to, tw) in grp:
                        kg, ku = bg.next(), bu.next()
                        for kc in range(KD):
                            self.mm(self.ps[kg][:, 0:tw], wg[wj][:, kc, sub * 128:(sub + 1) * 128], a2[:, kc, tl:tl + tw],
                                    kc == 0, kc == KD - 1, [b_wg[wj], b_a2], [self.psb[kg]])
                        for kc in range(KD):
                            self.mm(self.ps[ku][:, 0:tw], wu[wj][:, kc, sub * 128:(sub + 1) * 128], a2[:, kc, tl:tl + tw],
                                    kc == 0, kc == KD - 1, [b_wu[wj], b_a2], [self.psb[ku]])
                        (t1, b_t1) = r_t1.next()
                        (sg, b_sg) = r_sg.next()
                        (t3, b_t3) = r_t3.next()
                        (st, b_st) = r_st.next()
                        self.tt("dve", t1[:, 0:tw], self.ps[kg][:, 0:tw], rst[:, tl:tl + tw], ALU.mult, [self.psb[kg], b_rst], [b_t1])
                        self.act(sg[:, 0:tw], t1[:, 0:tw], AF.Silu, [b_t1], [b_sg])
                        self.tt("dve", t3[:, 0:tw], self.ps[ku][:, 0:tw], rst[:, tl:tl + tw], ALU.mult, [self.psb[ku], b_rst], [b_t3])
                        self.tt("pool", st[:, 0:tw], sg[:, 0:tw], t3[:, 0:tw], ALU.mult, [b_sg, b_t3], [b_st])
                        p.dma("sync", s["act3T"][m * 128:(m + 1) * 128, to:to + tw], st[:, 0:tw], b_st, self.sb["act3T"], store=True)
                        tl += tw
            p.barrier()

    def phase6(self):
        c, p, A, d, s = self.c, self.p, self.A, self.d, self.s
        KF, KD = c.KF, c.KD
        PC = 32
        pieces = [(c0, min(KF, c0 + PC)) for c0 in range(0, KF, PC)]
        base = A.off
        for (to, tw) in tok_tiles(c.NOWN):
            A.off = base
            a3src = s["act3T"][:, to:to + tw].rearrange("(c p) t -> p c t", p=128)
            a3p, b_a3 = [], []
            for (c0, c1) in pieces:
                t = A.bf16((c1 - c0) * tw).rearrange("p (c t) -> p c t", t=tw)
                b = p.buf("a3")
                p.dma("sync", t, a3src[:, c0:c1, :], b, self.sb["act3T"])
                p.fence("sync", b)
                a3p.append(t)
                b_a3.append(b)
            r_wd = Rot([(A.bf16(PC * 128).rearrange("p (c n) -> p c n", n=128), p.buf("wd")) for _ in range(3)])
            r_hb = Rot([(A.f32(512), p.buf("hb")) for _ in range(3)])
            r_h3 = Rot([(A.f32(512), p.buf("h3")) for _ in range(2)])
            r_os = Rot([(A.f32(512), p.buf("ost")) for _ in range(3)])
            mb = Rot([0, 1, 2, 3])
            tb = Rot([4, 5, 6, 7])
            nb = tw // 128
            for m in range(KD):
                wsrc = d["w_down"][:, m * 128:(m + 1) * 128].rearrange("(c p) n -> p c n", p=128)
                bk = mb.next()
                for pi, (c0, c1) in enumerate(pieces):
                    (wd, b_wd) = r_wd.next()
                    p.dma("pool", wd[:, 0:c1 - c0, :], wsrc[:, c0:c1, :], b_wd)
                    for kc in range(c0, c1):
                        self.mm(self.ps[bk][:, 0:tw], wd[:, kc - c0, :], a3p[pi][:, kc - c0, :], kc == 0, kc == KF - 1,
                                [b_wd, b_a3[pi]], [self.psb[bk]])
                self.flush_pe()
                (hb, b_hb) = r_hb.next()
                (h3, b_h3) = r_h3.next()
                (os_, b_os) = r_os.next()
                p.dma("sync", hb[:, 0:tw], s["h2T"][m * 128:(m + 1) * 128, to:to + tw], b_hb, self.sb["h2T"])
                self.tt("dve", h3[:, 0:tw], self.ps[bk][:, 0:tw], hb[:, 0:tw], ALU.add, [self.psb[bk], b_hb], [b_h3])

                def fin(h3=h3, b_h3=b_h3, os_=os_, b_os=b_os, m=m, to=to, tw=tw, nb=nb):
                    tk = tb.next()
                    for i in range(nb):
                        self.tr(self.ps[tk][:, i * 128:(i + 1) * 128], h3[:, i * 128:(i + 1) * 128], [b_h3], [self.psb[tk]])
                    self.copy("dve", os_[:, 0:tw], self.ps[tk][:, 0:tw], [self.psb[tk]], [b_os])
                    p.dma("sync", self.out[to:to + tw, m * 128:(m + 1) * 128].rearrange("(i p) f -> p i f", p=128),
                          os_[:, 0:tw].rearrange("p (i f) -> p i f", f=128), b_os, self.b_out, store=True)
                self.pe_def.append(fin)
            self.flush_pe()
            p.barrier()


def t5_bucket_np(n):
    n = np.asarray(n)
    nf = np.maximum(n, 16).astype(np.float32)
    log_b = 16 + (np.log(nf / np.float32(16)).astype(np.float32) / np.float32(math.log(128 / 16)) * np.float32(16)).astype(np.int32)
    return np.where(n < 16, n, np.minimum(log_b, 31))


def static_consts(cfg):
    ohc = np.zeros((32, FL), np.float32)
    dist = np.arange(FL) - 127
    bk = t5_bucket_np(np.maximum(dist, 0))
    for j in range(FL):
        if dist[j] >= 0:
            ohc[bk[j], j] = 1.0
    negmask = np.zeros((cfg.H, FL), np.float32)
    negmask[:, :127] = NEG
    return ohc, negmask


def make_in_maps(cfg, inputs, n_batch):
    c = cfg
    f = lambda a: np.ascontiguousarray(np.asarray(a, dtype=np.float32))
    x = np.asarray(inputs["x"], dtype=np.float32)
    meta = f(inputs["meta_tokens"])
    ohc, negmask = static_consts(c)
    shared = {
        "ohc": ohc, "negmask": negmask, "rel_bias": f(inputs["rel_bias"]),
        "ln1_g": f(inputs["ln1_g"][0]), "ln2_g": f(inputs["ln2_g"][0]), "w_in": f(inputs["w_in"][0]),
        "subln_g": f(inputs["subln_g"][0]),
        "lam_re": f(inputs["lam_re"][0]), "lam_im": f(inputs["lam_im"][0]), "log_dt": f(inputs["log_dt"][0]),
        "b_re": f(inputs["b_re"][0]), "b_im": f(inputs["b_im"][0]),
        "c_re": f(inputs["c_re"][0]).reshape(c.G * 16, 64), "c_im": f(inputs["c_im"][0]).reshape(c.G * 16, 64),
        "d_skip": f(inputs["d_skip"][0]), "w_glu": f(inputs["w_glu"][0]), "b_glu": f(inputs["b_glu"][0]),
        "w_branch": f(inputs["w_branch"][0]), "w_o": f(inputs["w_o"][0]),
        "w_gate_up": f(inputs["w_gate_up"][0]), "w_down": f(inputs["w_down"][0]),
    }
    for n in ("q_norm_g", "k_norm_g", "lam_q1", "lam_k1", "lam_q2", "lam_k2"):
        shared[n] = f(inputs[n][0])
    NB = c.NCTX
    maps = []
    for b in range(n_batch):
        hpad = np.zeros((NB * 128, c.D), np.float32)
        hpad[PADF:PADF + N_META] = meta
        hpad[PADF + N_META:] = x[b]
        for half in range(2):
            hp = np.zeros((c.NT, c.D), np.float32)
            kv = np.full((c.NT,), NEG, np.float32)
            if half == 0:
                hp[c.O0:] = hpad[:c.NO]
                kv[c.O0 + PADF:] = 0.0
            else:
                hp[:] = hpad
                kv[PADF:] = 0.0
            m = dict(shared)
            m["hp"] = hp
            m["kvalid"] = np.ascontiguousarray(kv.reshape(c.NCTX, 128).T)
            maps.append(m)
    return maps


_NC_CACHE = {}


def run_cfg(cfg, inputs, n_batch, debug=False, stop_after=99, trace=False):
    key = (id(cfg), debug, stop_after)
    if key not in _NC_CACHE:
        _NC_CACHE[key] = Builder(cfg, debug=debug, stop_after=stop_after).build()
    nc = _NC_CACHE[key]
    maps = make_in_maps(cfg, inputs, n_batch)
    res = run_bass_kernel_spmd(nc, maps, core_ids=list(range(2 * n_batch)), trace=trace)
    return res


def assemble(cfg, res, n_batch, seq):
    c = cfg
    out = np.zeros((n_batch, seq, c.D), np.float32)
    n0 = c.NO - 128
    for b in range(n_batch):
        o0 = res.results[2 * b]["out"]
        o1 = res.results[2 * b + 1]["out"]
        out[b, :n0] = o0[128:]
        out[b, n0:] = o1[c.NO - (seq - n0):]
    return out


FULL = Cfg()


def kernel(**inputs):
    res = run_cfg(FULL, inputs, 4)
    return assemble(FULL, res, 4, 4096)
```

```python
import math
import os
from contextlib import ExitStack

import numpy as np
import concourse.bass as bass
import concourse.mybir as mybir
from concourse.bass_utils import run_bass_kernel_spmd

F32 = mybir.dt.float32
BF16 = mybir.dt.bfloat16
AF = mybir.ActivationFunctionType
ALU = mybir.AluOpType
AX = mybir.AxisListType
NEG = -30000.0
EPS = 1e-6
PI = math.pi
TWO_PI = 2.0 * math.pi
N_META = 16
PADF = 112
FL = 384


class Cfg:
    def __init__(self, D=4096, DS=1024, H=16, DFF=11008, NPRE=16, NOWN=17, NSUB=2, SMAX=1152, depth_l=0):
        self.D, self.DS, self.H, self.DFF = D, DS, H, DFF
        self.NPRE, self.NOWN, self.NSUB, self.SMAX = NPRE, NOWN, NSUB, SMAX
        self.G = DS // 16
        self.QKW = H * 128
        self.DATT = H * 128
        self.DIN = DS + 2 * self.QKW + self.DATT + 2 * D
        self.cU, self.cQ = 0, DS
        self.cK = DS + self.QKW
        self.cV = DS + 2 * self.QKW
        self.cGS = self.cV + self.DATT
        self.cGA = self.cGS + D
        self.NCTX = NPRE + NOWN
        self.NT = self.NCTX * 128
        self.NO = NOWN * 128
        self.O0 = NPRE * 128
        self.KD = D // 128
        self.KS = DS // 128
        self.KF = DFF // 128
        self.MB = NSUB * 128
        self.lambda_init = 0.8 - 0.6 * math.exp(-0.3 * depth_l)


def tok_tiles(nblocks, maxb=4):
    nt = -(-nblocks // maxb)
    base, rem = divmod(nblocks, nt)
    out, s = [], 0
    for i in range(nt):
        w = base + (1 if i < rem else 0)
        out.append((s * 128, w * 128))
        s += w
    return out


def super_tiles(tiles, smax):
    out, cur, tot = [], [], 0
    for t in tiles:
        if cur and tot + t[1] > smax:
            out.append(cur)
            cur, tot = [], 0
        cur.append(t)
        tot += t[1]
    if cur:
        out.append(cur)
    return out


class Buf:
    __slots__ = ("name", "w", "r", "slot", "gen")

    def __init__(self, name):
        self.name, self.w, self.r, self.slot, self.gen = name, [], [], None, -1


class Prog:
    ENG = ("sync", "act", "pool", "dve", "pe")
    EPOCH = 30000
    DLIM = 3000

    def __init__(self):
        self.ops = {e: [] for e in self.ENG}
        self.cnt = {e: 0 for e in self.ENG}
        self.seen = {e: {} for e in self.ENG}
        self.nb = 0
        self.semkeys, self.semset = [], set()
        self.dlast = {}
        self.gen = 0
        self.slots = []
        self.nfree = 0

    def buf(self, name="b"):
        self.nb += 1
        return Buf("%s#%d" % (name, self.nb))

    def bufs(self, n, name="b"):
        return [self.buf(name) for _ in range(n)]

    def _sem(self, sk):
        if sk not in self.semset:
            self.semset.add(sk)
            self.semkeys.append(sk)

    def _need(self, eng, ev, waits):
        k, v = ev
        if k[0] == "e" and k[1] == "pe" and eng == "pe":
            return
        if self.seen[eng].get(k, 0) >= v:
            return
        if waits.get(k, 0) < v:
            waits[k] = v

    def _commit(self, eng, waits):
        wl = []
        for k, v in waits.items():
            self.seen[eng][k] = v
            if k[0] == "e":
                sk = ("e", k[1], (v - 1) // self.EPOCH)
                val = (v - 1) % self.EPOCH + 1
            else:
                sk, val = k, v
            self._sem(sk)
            wl.append((sk, val))
        return wl

    @staticmethod
    def _add(lst, ev):
        return [x for x in lst if x[0] != ev[0]] + [ev]

    def op(self, eng, name, kw, R=(), W=()):
        waits = {}
        for b in R:
            for ev in b.w:
                self._need(eng, ev, waits)
        for b in W:
            for ev in b.w:
                self._need(eng, ev, waits)
            for ev in b.r:
                self._need(eng, ev, waits)
        wl = self._commit(eng, waits)
        self.cnt[eng] += 1
        c = self.cnt[eng]
        ev = (("e", eng), c)
        sk = ("e", eng, (c - 1) // self.EPOCH)
        self._sem(sk)
        self.ops[eng].append((wl, name, kw, sk, 1))
        for b in W:
            b.w = [ev]
            b.r = []
        for b in R:
            if b not in W:
                b.r = self._add(b.r, ev)

    def dma(self, q, out, in_, sb, dr=None, store=False, partial=False, **kw):
        waits = {}
        if sb.gen != self.gen:
            if self.nfree >= len(self.slots):
                self.slots.append([0, 0])
            sb.slot = self.nfree
            self.nfree += 1
            sb.gen = self.gen
            sl = self.slots[sb.slot]
            if sl[1] >= self.DLIM:
                sl[0] += 1
                sl[1] = 0
        sl = self.slots[sb.slot]
        key = ("d", sb.slot, sl[0])
        if store:
            for ev in sb.w:
                self._need(q, ev, waits)
        else:
            if dr is not None:
                for ev in dr.w:
                    self._need(q, ev, waits)
            for ev in sb.w:
                if partial and ev[0] == key:
                    continue
                self._need(q, ev, waits)
            for ev in sb.r:
                self._need(q, ev, waits)
        wl = self._commit(q, waits)
        sl[1] += 1
        ev = (key, 16 * sl[1])
        self._sem(key)
        self.dlast[key] = 16 * sl[1]
        d = dict(out=out, in_=in_)
        d.update(kw)
        self.ops[q].append((wl, "dma_start", d, key, 16))
        if store:
            sb.r = self._add(sb.r, ev)
            if dr is not None:
                dr.w = self._add(dr.w, ev)
        else:
            if partial:
                sb.w = self._add(sb.w, ev)
            else:
                sb.w = [ev]
                sb.r = []
            if dr is not None:
                dr.r = self._add(dr.r, ev)

    def fence(self, q, b):
        waits = {}
        for ev in b.w:
            self._need(q, ev, waits)
        wl = self._commit(q, waits)
        if wl:
            self.ops[q].append((wl, None, None, None, 0))

    def barrier(self, allbufs=()):
        evs = [(("e", e), self.cnt[e]) for e in self.ENG if self.cnt[e] > 0]
        evs += [(k, v) for k, v in self.dlast.items()]
        for eng in self.ENG:
            waits = {}
            for ev in evs:
                self._need(eng, ev, waits)
            wl = self._commit(eng, waits)
            if wl:
                self.ops[eng].append((wl, None, None, None, 0))
        self.gen += 1
        self.nfree = 0

    def emit(self, nc, es):
        sems = {}
        for i, sk in enumerate(self.semkeys):
            sems[sk] = es.enter_context(nc.semaphore("s%d" % i))
        block = es.enter_context(nc.Block())
        if os.environ.get("KDUMP"):
            for e in self.ENG:
                ops = self.ops[e]
                idx = [i for i, o in enumerate(ops) if o[1] is None]
                st = idx[-int(os.environ.get("KDB", "3"))]
                print("ENGINE", e, "total", len(ops), "from", st)
                for o in ops[st:st + int(os.environ["KDUMP"])]:
                    print("   waits", o[0], "op", o[1], "inc", o[3], o[4])

        def run(engname):
            def f(e):
                for (wl, name, kw, sk, inc) in self.ops[engname]:
                    for (wk, val) in wl:
                        e.wait_ge(sems[wk], val)
                    if name is None:
                        continue
                    ins = getattr(e, name)(**kw)
                    ins.then_inc(sems[sk], inc)
            return f

        block.sync(run("sync"))
        block.scalar(run("act"))
        block.gpsimd(run("pool"))
        block.vector(run("dve"))
        block.tensor(run("pe"))


class Arena:
    def __init__(self, big, nwords):
        self.big, self.n, self.off = big, nwords, 0

    def f32(self, n):
        a = self.off
        self.off += n
        assert self.off <= self.n, ("arena overflow", self.off, self.n)
        return self.big[:, a:a + n]

    def bf16(self, n):
        w = (n + 1) // 2
        a = self.off
        self.off += w
        assert self.off <= self.n, ("arena overflow", self.off, self.n)
        return self.big[:, a:a + w].bitcast(BF16)[:, 0:n]


class Rot:
    def __init__(self, items):
        self.items, self.i = items, 0

    def next(self):
        it = self.items[self.i % len(self.items)]
        self.i += 1
        return it


class Builder:
    def __init__(self, cfg, debug=False, stop_after=99):
        self.c = cfg
        self.debug = debug
        self.stop_after = stop_after

    def act(self, out, in_, func, R, W, bias=None, scale=None, accum_out=None):
        kw = dict(out=out, in_=in_, func=func)
        if bias is not None:
            kw["bias"] = bias
        if scale is not None:
            kw["scale"] = scale
        if accum_out is not None:
            kw["accum_out"] = accum_out
        self.p.op("act", "activation", kw, R, W)

    def ts(self, eng, out, in0, s1, s2, op0, op1, R, W):
        kw = dict(out=out, in0=in0, scalar1=s1, scalar2=s2, op0=op0)
        if op1 is not None:
            kw["op1"] = op1
        self.p.op(eng, "tensor_scalar", kw, R, W)

    def tt(self, eng, out, in0, in1, op, R, W):
        self.p.op(eng, "tensor_tensor", dict(out=out, in0=in0, in1=in1, op=op), R, W)

    def stt(self, out, in0, scalar, in1, op0, op1, R, W):
        self.p.op("dve", "scalar_tensor_tensor", dict(out=out, in0=in0, scalar=scalar, in1=in1, op0=op0, op1=op1), R, W)

    def copy(self, eng, out, in_, R, W):
        if eng == "act":
            self.act(out, in_, AF.Copy, R, W)
        else:
            self.p.op(eng, "tensor_copy", dict(out=out, in_=in_), R, W)

    def memset(self, eng, ap, val, W):
        self.p.op(eng, "memset", dict(ap=ap, constant=val), (), W)

    def mm(self, out, lhsT, rhs, start, stop, R, W):
        self.p.op("pe", "matmul", dict(out=out, lhsT=lhsT, rhs=rhs, start=start, stop=stop), R, W)

    def tr(self, out, in_, R, W):
        self.p.op("pe", "transpose", dict(out=out, in_=in_, identity=self.ident), list(R) + [self.b_const], W)

    def cut(self):
        self.ncut = getattr(self, "ncut", 0) + 1
        return self.ncut >= int(os.environ.get("KCUT2", "999"))

    def flush_pe(self):
        for f in self.pe_def:
            f()
        self.pe_def = []

    def build(self):
        c = self.c
        nc = bass.Bass("TRN2", target_bir_lowering=False)
        self.nc = nc
        p = Prog()
        self.p = p
        self.pe_def = []
        IN = lambda name, shape: nc.dram_tensor(name, list(shape), F32, kind="ExternalInput").ap()
        skind = "ExternalOutput" if self.debug else "Internal"
        SC = lambda name, shape, dt: nc.dram_tensor(name, list(shape), dt, kind=skind).ap()
        d = {}
        d["hp"] = IN("hp", (c.NT, c.D))
        d["kvalid"] = IN("kvalid", (128, c.NCTX))
        d["ohc"] = IN("ohc", (32, FL))
        d["negmask"] = IN("negmask", (c.H, FL))
        d["rel_bias"] = IN("rel_bias", (32, c.H))
        d["ln1_g"] = IN("ln1_g", (c.D,))
        d["ln2_g"] = IN("ln2_g", (c.D,))
        d["w_in"] = IN("w_in", (c.D, c.DIN))
        for n in ("q_norm_g", "k_norm_g", "lam_q1", "lam_k1", "lam_q2", "lam_k2"):
            d[n] = IN(n, (64,))
        d["subln_g"] = IN("subln_g", (128,))
        d["lam_re"] = IN("lam_re", (c.G, 64))
        d["lam_im"] = IN("lam_im", (c.G, 64))
        d["log_dt"] = IN("log_dt", (c.G,))
        d["b_re"] = IN("b_re", (c.G, 64, 16))
        d["b_im"] = IN("b_im", (c.G, 64, 16))
        d["c_re"] = IN("c_re", (c.G * 16, 64))
        d["c_im"] = IN("c_im", (c.G * 16, 64))
        d["d_skip"] = IN("d_skip", (c.DS,))
        d["w_glu"] = IN("w_glu", (c.DS, c.DS))
        d["b_glu"] = IN("b_glu", (c.DS,))
        d["w_branch"] = IN("w_branch", (c.DS + c.DATT, c.D))
        d["w_o"] = IN("w_o", (c.D, c.D))
        d["w_gate_up"] = IN("w_gate_up", (c.D, 2 * c.DFF))
        d["w_down"] = IN("w_down", (c.DFF, c.D))
        self.d = d
        s = {}
        s["uT"] = SC("uT", (c.DS, c.NT), F32)
        s["qT"] = SC("qT", (c.H, 128, c.NO), BF16)
        s["kT"] = SC("kT", (c.H, 128, c.NT), BF16)
        s["v"] = SC("v", (c.NT, c.DATT), BF16)
        s["sgs"] = SC("sgs", (c.D, c.NO), BF16)
        s["sga"] = SC("sga", (c.D, c.NO), BF16)
        s["hT"] = SC("hT", (c.D, c.NO), F32)
        s["yssmT"] = SC("yssmT", (c.DS, c.NO), BF16)
        s["yattT"] = SC("yattT", (c.DATT, c.NO), BF16)
        s["h2T"] = SC("h2T", (c.D, c.NO), F32)
        s["a2T"] = SC("a2T", (c.D, c.NO), BF16)
        s["rstd2"] = SC("rstd2", (128, c.NO), F32)
        s["act3T"] = SC("act3T", (c.DFF, c.NO), BF16)
        s["rrep"] = nc.dram_tensor("rrep", [c.H, 128, FL], F32, kind="Internal").ap()
        self.s = s
        self.sb = {k: p.buf("dram_" + k) for k in s}
        self.out = nc.dram_tensor("out", [c.NO, c.D], F32, kind="ExternalOutput").ap()
        self.b_out = p.buf("dram_out")

        NW = 47616
        with ExitStack() as es:
            big = es.enter_context(nc.sbuf_tensor("arena", [128, NW], F32))
            self.A = Arena(big, NW)
            self.ps = [es.enter_context(nc.psum_tensor("ps%d" % i, [128, 512], F32)) for i in range(8)]
            self.psb = [p.buf("ps%d" % i) for i in range(8)]
            self.phase0()
            phases = [self.phase1, self.phase2, self.phase3, self.phase4, self.phase5, self.phase6]
            for i, ph in enumerate(phases):
                if i + 1 > self.stop_after:
                    break
                self.A.off = self.A0
                ph()
                self.flush_pe()
                p.barrier()
            if self.stop_after < 6:
                t = self.A.f32(128)
                b = p.buf("dummy")
                self.memset("dve", t, 0.0, [b])
                p.dma("sync", self.out[0:128, 0:128], t, b, self.b_out, store=True)
            p.barrier()
            print("ops:", {e: len(p.ops[e]) for e in p.ENG}, "sems:", len(p.semkeys), flush=True)
            p.emit(nc, es)
        return nc

    def phase0(self):
        c, p, A, d = self.c, self.p, self.A, self.d
        bc = p.buf("const")
        self.b_const = bc
        self.ident = A.f32(128)
        self.onesf = A.f32(128)
        self.onesb = A.bf16(128)
        self.blk = A.bf16(128)
        p.op("pool", "iota", dict(out=self.ident, pattern=[[1, 128]], base=0, channel_multiplier=-1,
                                  allow_small_or_imprecise_dtypes=True), (), [bc])
        p.op("dve", "tensor_single_scalar", dict(out=self.ident, in_=self.ident, scalar=0.0, op=ALU.is_equal), [bc], [bc])
        self.memset("dve", self.onesf, 1.0, [bc])
        self.memset("dve", self.onesb, 1.0, [bc])
        self.memset("dve", self.blk, 0.0, [bc])
        self.memset("dve", self.blk[0:64, 0:64], 1.0, [bc])
        self.memset("dve", self.blk[64:128, 64:128], 1.0, [bc])
        self.s1 = A.f32(1)
        self.s2 = A.f32(1)
        self.negpi = A.f32(1)
        for (ap, top, bot) in ((self.s1, -1.0, 1.0), (self.s2, 1.0, -1.0)):
            self.memset("dve", ap[0:64, :], top, [bc])
            self.memset("dve", ap[64:128, :], bot, [bc])
        self.memset("dve", self.negpi, -PI, [bc])
        self.epsc = A.f32(1)
        self.halfpi = A.f32(1)
        self.memset("dve", self.epsc, EPS, [bc])
        self.memset("dve", self.halfpi, PI / 2, [bc])
        self.g1col = A.f32(c.KD)
        self.g2col = A.f32(c.KD)
        bl = p.buf("cload")
        p.dma("sync", self.g1col, d["ln1_g"].rearrange("(c p) -> p c", p=128), bl, partial=True, allow_slow_non_contiguous=True)
        p.dma("sync", self.g2col, d["ln2_g"].rearrange("(c p) -> p c", p=128), bl, partial=True, allow_slow_non_contiguous=True)
        self.gq2 = A.f32(1)
        self.gk2 = A.f32(1)
        for (ap, nm) in ((self.gq2, "q_norm_g"), (self.gk2, "k_norm_g")):
            src = d[nm].rearrange("(p o) -> p o", o=1)
            p.dma("sync", ap[0:64, :], src, bl, partial=True, allow_slow_non_contiguous=True)
            p.dma("sync", ap[64:128, :], src, bl, partial=True, allow_slow_non_contiguous=True)
        if int(os.environ.get("KCUT", "99")) <= 1:
            self.A0 = A.off
            return
        self.slng = A.f32(1)
        p.dma("sync", self.slng, d["subln_g"].rearrange("(p o) -> p o", o=1), bl, partial=True, allow_slow_non_contiguous=True)
        self.kvalid = A.f32(c.NCTX)
        p.dma("sync", self.kvalid, d["kvalid"], bl, partial=True)
        self.rel31 = A.f32(c.H)
        p.dma("sync", self.rel31, d["rel_bias"][31, :].partition_broadcast(128), bl, partial=True)
        lam4 = A.f32(256)
        for i, nm in enumerate(("lam_q1", "lam_k1", "lam_q2", "lam_k2")):
            p.dma("sync", lam4[:, i * 64:(i + 1) * 64], d[nm].partition_broadcast(128), bl, partial=True)
        relb = A.f32(c.H)
        ohc = A.f32(FL)
        ngm = A.f32(FL)
        p.dma("sync", relb[0:32, :], d["rel_bias"], bl, partial=True)
        p.dma("sync", ohc[0:32, :], d["ohc"], bl, partial=True)
        p.dma("sync", ngm[0:c.H, :], d["negmask"], bl, partial=True)
        if int(os.environ.get("KCUT", "99")) <= 2:
            self.A0 = A.off
            return
        self.ts("dve", self.slng, self.slng, 1.0 - c.lambda_init, None, ALU.mult, None, [bl], [bl])
        lt = A.f32(128)
        l2 = A.f32(2)
        self.neglam = A.f32(1)
        self.tt("dve", lt[:, 0:64], lam4[:, 0:64], lam4[:, 64:128], ALU.mult, [bl], [bc])
        self.tt("dve", lt[:, 64:128], lam4[:, 128:192], lam4[:, 192:256], ALU.mult, [bl], [bc])
        p.op("dve", "tensor_reduce", dict(out=l2[:, 0:1], in_=lt[:, 0:64], axis=AX.X, op=ALU.add), [bc], [bc])
        p.op("dve", "tensor_reduce", dict(out=l2[:, 1:2], in_=lt[:, 64:128], axis=AX.X, op=ALU.add), [bc], [bc])
        self.act(l2, l2, AF.Exp, [bc], [bc])
        self.tt("dve", self.neglam, l2[:, 1:2], l2[:, 0:1], ALU.subtract, [bc], [bc])
        self.ts("dve", self.neglam, self.neglam, -c.lambda_init, None, ALU.add, None, [bc], [bc])
        if int(os.environ.get("KCUT", "99")) <= 3:
            self.A0 = A.off
            return
        frow = A.f32(FL)
        self.mm(self.ps[0][0:c.H, 0:FL], relb[0:32, 0:c.H], ohc[0:32, :], True, True, [bl], [self.psb[0]])
        self.tt("dve", frow[0:c.H, :], self.ps[0][0:c.H, 0:FL], ngm[0:c.H, :], ALU.add, [self.psb[0], bl], [bc])
        if int(os.environ.get("KCUT", "99")) <= 4:
            self.A0 = A.off
            return
        brr = p.buf("rrepst")
        bdr = self.p.buf("dram_rrep")
        src = frow[0:c.H, :].unsqueeze(1).broadcast_to([c.H, 128, FL])
        p.dma("sync", self.s["rrep"], src, bc, bdr, store=True)
        self.TD = A.f32(c.H * 128)
        self.TP = A.f32(c.H * 128)
        bt = p.buf("toep")
        rt = self.s["rrep"].tensor
        p.dma("sync", self.TD.rearrange("p (h q) -> p h q", q=128),
              bass.AP(tensor=rt, offset=127, ap=[[FL - 1, 128], [128 * FL, c.H], [1, 128]]), bt, bdr, partial=True)
        p.dma("sync", self.TP.rearrange("p (h q) -> p h q", q=128),
              bass.AP(tensor=rt, offset=255, ap=[[FL - 1, 128], [128 * FL, c.H], [1, 128]]), bt, bdr, partial=True)
        self.b_toep = bt
        self.b_cload = bl
        self.A0 = A.off
        p.barrier()

    def ctx_supers(self):
        c = self.c
        pre = [(t0, tw, False) for (t0, tw) in tok_tiles(c.NPRE)]
        own = [(c.O0 + t0, tw, True) for (t0, tw) in tok_tiles(c.NOWN)]
        sup = []
        for grp in super_tiles([(a, b) for (a, b, _) in pre], c.SMAX):
            sup.append((grp, False))
        for grp in super_tiles([(a, b) for (a, b, _) in own], c.SMAX):
            sup.append((grp, True))
        return sup

    def own_supers(self):
        c = self.c
        return super_tiles(tok_tiles(c.NOWN), c.SMAX)

    def phase1(self):
        c, p, A, d, s = self.c, self.p, self.A, self.d, self.s
        KD = c.KD
        for (grp, own) in self.ctx_supers():
            A.off = self.A0
            S0 = grp[0][0]
            S = sum(t[1] for t in grp)
            a1T = A.bf16(KD * S).rearrange("p (c t) -> p c t", t=S)
            b_a1 = p.buf("a1T")
            mark = A.off
            xin = [A.f32(c.D) for _ in range(2)]
            b_x = p.bufs(2, "xin")
            junk = A.bf16(c.D)
            b_j = p.buf("junk")
            hst2 = A.f32(c.D)
            hst = hst2.rearrange("p (c t) -> p c t", t=128)
            b_h = p.buf("hst")
            ssq = A.f32(2)
            rs = A.f32(2)
            b_s = p.bufs(2, "ssq")
            diag = [A.f32(128) for _ in range(2)]
            rrep = [A.f32(128) for _ in range(2)]
            b_r = p.bufs(2, "rrep")
            trb = Rot([5, 6, 7])
            for bi in range(S // 128):
                t0 = S0 + bi * 128
                j = bi % 2
                p.dma("pool", xin[j], d["hp"][t0:t0 + 128, :], b_x[j])
                self.act(junk, xin[j], AF.Square, [b_x[j]], [b_j, b_s[j]], accum_out=ssq[:, j:j + 1])
                self.act(rs[:, j:j + 1], ssq[:, j:j + 1], AF.Sqrt, [b_s[j], self.b_const], [b_r[j]], bias=self.epsc[:, 0:1], scale=1.0 / c.D)
                p.op("dve", "reciprocal", dict(out=rs[:, j:j + 1], in_=rs[:, j:j + 1]), [b_r[j]], [b_r[j]])
                self.ts("dve", diag[j], self.ident, rs[:, j:j + 1], None, ALU.mult, None, [b_r[j], self.b_const], [b_r[j]])
                self.mm(self.ps[4][:, 0:128], self.onesf, diag[j], True, True, [b_r[j], self.b_const], [self.psb[4]])
                self.copy("act", rrep[j], self.ps[4][:, 0:128], [self.psb[4]], [b_r[j]])
                for c0 in range(0, KD, 4):
                    n4 = min(4, KD - c0)
                    bk = trb.next()
                    for q in range(n4):
                        self.tr(self.ps[bk][:, q * 128:(q + 1) * 128], xin[j][:, (c0 + q) * 128:(c0 + q + 1) * 128],
                                [b_x[j]], [self.psb[bk]])
                    for q in range(n4):
                        self.stt(a1T[:, c0 + q, bi * 128:(bi + 1) * 128], self.ps[bk][:, q * 128:(q + 1) * 128],
                                 self.g1col[:, c0 + q:c0 + q + 1], rrep[j], ALU.mult, ALU.mult,
                                 [self.psb[bk], b_r[j], self.b_cload], [b_a1])
                    if own and not os.environ.get("KNOC"):
                        self.copy("dve", hst2[:, c0 * 128:(c0 + n4) * 128], self.ps[bk][:, 0:n4 * 128],
                                  [self.psb[bk]], [b_h])
                if own and not os.environ.get("KNOH"):
                    to = t0 - c.O0
                    p.dma("sync", s["hT"][:, to:to + 128].rearrange("(c p) t -> p c t", p=128), hst, b_h, self.sb["hT"], store=True)
            p.barrier()
            if self.cut():
                return
            A.off = mark
            wbuf = [A.bf16(KD * c.MB).rearrange("p (c n) -> p c n", n=c.MB) for _ in range(2)]
            b_w = p.bufs(2, "wbuf")
            NST = 3
            stf = Rot([(A.f32(512), p.buf("stf")) for _ in range(NST)])
            stb = Rot([(A.bf16(512), p.buf("stb")) for _ in range(NST)])
            sqt = Rot([(A.bf16(512), p.buf("sqt")) for _ in range(2)])
            rqt = Rot([(A.f32(512), p.buf("rqt")) for _ in range(2)])
            mainb = Rot([0, 1, 2, 3, 4])
            normb = Rot([5, 6])
            regions = [("u", c.cU, c.DS), ("q", c.cQ, c.QKW), ("k", c.cK, c.QKW), ("v", c.cV, c.DATT),
                       ("gs", c.cGS, c.D), ("ga", c.cGA, c.D)]
            blocks = []
            for (typ, cs, cw) in regions:
                if not own and typ not in ("u", "k", "v"):
                    continue
                if typ not in os.environ.get("KREG", "u,q,k,v,gs,ga").split(","):
                    continue
                for c0 in range(cs, cs + cw, c.MB):
                    blocks.append((typ, cs, c0))
            wi = 0
            for (typ, cs, c0) in blocks:
                wj = wi % 2
                wi += 1
                p.dma("pool", wbuf[wj], d["w_in"][:, c0:c0 + c.MB].rearrange("(c p) n -> p c n", p=128), b_w[wj])
                if typ == "v":
                    for tb in range(S // 128):
                        bk = mainb.next()
                        for kc in range(KD):
                            self.mm(self.ps[bk][:, 0:c.MB], a1T[:, kc, tb * 128:(tb + 1) * 128], wbuf[wj][:, kc, :],
                                    kc == 0, kc == KD - 1, [b_a1, b_w[wj]], [self.psb[bk]])
                        self.flush_pe()
                        (st, bs) = stb.next()
                        self.copy("act", st[:, 0:c.MB], self.ps[bk][:, 0:c.MB], [self.psb[bk]], [bs])
                        t0 = S0 + tb * 128
                        p.dma("sync", s["v"][t0:t0 + 128, c0 - cs:c0 - cs + c.MB], st[:, 0:c.MB], bs, self.sb["v"], store=True)
                    continue
                for sub in range(c.NSUB):
                    m = (c0 - cs) // 128 + sub
                    tl = 0
                    for (t0, tw) in grp:
                        bk = mainb.next()
                        for kc in range(KD):
                            self.mm(self.ps[bk][:, 0:tw], wbuf[wj][:, kc, sub * 128:(sub + 1) * 128], a1T[:, kc, tl:tl + tw],
                                    kc == 0, kc == KD - 1, [b_a1, b_w[wj]], [self.psb[bk]])
                        self.flush_pe()
                        pm = self.ps[bk][:, 0:tw]
                        if typ == "u":
                            (st, bs) = stf.next()
                            self.copy("act", st[:, 0:tw], pm, [self.psb[bk]], [bs])
                            p.dma("sync", s["uT"][m * 128:(m + 1) * 128, t0:t0 + tw], st[:, 0:tw], bs, self.sb["uT"], store=True)
                        elif typ in ("q", "k"):
                            (sq, bsq) = sqt.next()
                            (rq, brq) = rqt.next()
                            (st, bs) = stb.next()
                            nb = normb.next()
                            gcol = self.gq2 if typ == "q" else self.gk2
                            self.act(sq[:, 0:tw], pm, AF.Square, [self.psb[bk]], [bsq])

                            def deferred(nb=nb, sq=sq, bsq=bsq, tw=tw, rq=rq, brq=brq, st=st, bs=bs, pm=pm, bk=bk,
                                         gcol=gcol, typ=typ, m=m, t0=t0):
                                self.mm(self.ps[nb][:, 0:tw], self.blk, sq[:, 0:tw], True, True, [bsq, self.b_const], [self.psb[nb]])
                                self.act(rq[:, 0:tw], self.ps[nb][:, 0:tw], AF.Sqrt, [self.psb[nb], self.b_const], [brq],
                                         bias=self.epsc[:, 0:1], scale=1.0 / 64)
                                p.op("dve", "reciprocal", dict(out=rq[:, 0:tw], in_=rq[:, 0:tw]), [brq], [brq])
                                self.stt(st[:, 0:tw], pm, gcol[:, 0:1], rq[:, 0:tw], ALU.mult, ALU.mult,
                                         [self.psb[bk], brq, self.b_cload], [bs])
                                if typ == "q":
                                    to = t0 - c.O0
                                    p.dma("sync", s["qT"][m, :, to:to + tw], st[:, 0:tw], bs, self.sb["qT"], store=True)
                                else:
                                    p.dma("sync", s["kT"][m, :, t0:t0 + tw], st[:, 0:tw], bs, self.sb["kT"], store=True)
                            self.pe_def.append(deferred)
                        else:
                            (st, bs) = stb.next()
                            self.act(st[:, 0:tw], pm, AF.Sigmoid, [self.psb[bk]], [bs])
                            to = t0 - c.O0
                            dst = s["sgs"] if typ == "gs" else s["sga"]
                            p.dma("sync", dst[m * 128:(m + 1) * 128, to:to + tw], st[:, 0:tw], bs,
                                  self.sb["sgs" if typ == "gs" else "sga"], store=True)
                        tl += tw
            self.flush_pe()
            p.barrier()
            if self.cut():
                return

    def rsqrt_ps(self, out, in_, scale, Rb, Wb):
        self.act(out, in_, AF.Sqrt, list(Rb) + [self.b_const], [Wb], bias=self.epsc[:, 0:1], scale=scale)
        self.p.op("dve", "reciprocal", dict(out=out, in_=out), [Wb], [Wb])

    def reduce_angle(self, eng, out, a, ni, Ra, Wn, Wo):
        self.ts(eng, ni, a, 1.0 / TWO_PI, None, ALU.mult, None, Ra, [Wn])
        self.copy(eng, out, ni, [Wn], [Wo])
        self.stt(out, out, -TWO_PI, a, ALU.mult, ALU.add, list(Ra) + [Wo], [Wo])
        self.ts(eng, out, out, -PI, PI, ALU.max, ALU.min, [Wo], [Wo])

    def phase2(self):
        c, p, A, d, s = self.c, self.p, self.A, self.d, self.s
        G, KS = c.G, c.KS
        I32 = mybir.dt.int32
        bs = p.buf("ssm_setup")
        bl = p.buf("ssm_load")
        ZB = A.f32(G * 16)
        ZsB = A.f32(G * 16)
        rr = A.f32(G)
        thr = A.f32(G)
        A0thrT = A.f32(G)
        dsk = A.f32(KS)
        bgl = A.f32(KS)
        maskc = A.f32(8)
        iota = A.f32(512)
        chunks = [(t0, tw, False) for (t0, tw) in tok_tiles(c.NPRE)] + [(c.O0 + t0, tw, True) for (t0, tw) in tok_tiles(c.NOWN)]
        NCH = len(chunks)
        offs = A.f32(NCH * G)
        gT = A.bf16(KS * c.NO).rearrange("p (c t) -> p c t", t=c.NO)
        b_gT = p.buf("gT")
        Blz = A.bf16(8 * 128).rearrange("p (g m) -> p g m", m=128)
        Blzs = A.bf16(8 * 128).rearrange("p (g m) -> p g m", m=128)
        Cl1 = A.bf16(8 * 128).rearrange("p (g m) -> p g m", m=128)
        Cl2 = A.bf16(8 * 128).rearrange("p (g m) -> p g m", m=128)
        mark = A.off
        lre = A.f32(G)
        lim = A.f32(G)
        dtr = A.f32(G)
        for (ap, nm) in ((lre, "lam_re"), (lim, "lam_im")):
            src = d[nm].rearrange("g p -> p g")
            p.dma("sync", ap[0:64, :], src, bl, partial=True, allow_slow_non_contiguous=True)
            p.dma("sync", ap[64:128, :], src, bl, partial=True, allow_slow_non_contiguous=True)
        p.dma("sync", dtr, d["log_dt"].partition_broadcast(128), bl, partial=True)
        p.dma("sync", dsk, d["d_skip"].rearrange("(k p) -> p k", p=128), bl, partial=True, allow_slow_non_contiguous=True)
        p.dma("sync", bgl, d["b_glu"].rearrange("(k p) -> p k", p=128), bl, partial=True, allow_slow_non_contiguous=True)
        X1 = A.f32(G * 16)
        X2 = A.f32(G * 16)
        X13 = X1.rearrange("p (g c) -> p g c", c=16)
        X23 = X2.rearrange("p (g c) -> p g c", c=16)
        bre = d["b_re"].rearrange("g p c -> p g c")
        bim = d["b_im"].rearrange("g p c -> p g c")
        p.dma("sync", X13[0:64], bre, bl, partial=True)
        p.dma("sync", X13[64:128], bim, bl, partial=True)
        p.dma("sync", X23[0:64], bim, bl, partial=True)
        p.dma("sync", X23[64:128], bre, bl, partial=True)
        if os.environ.get("KS2") == "1":
            p.barrier()
            return
        self.act(dtr, dtr, AF.Exp, [bl], [bs])
        tmp = A.f32(G)
        self.tt("dve", tmp, lre, dtr, ALU.mult, [bl, bs], [bs])
        self.act(rr, tmp, AF.Exp, [bs], [bs])
        th = A.f32(G)
        self.tt("dve", th, lim, dtr, ALU.mult, [bl, bs], [bs])
        if os.environ.get("KS2") == "2":
            p.barrier()
            return
        ni = A.f32(G).bitcast(I32)
        self.reduce_angle("dve", thr, th, ni, [bs], bs, bs)
        if os.environ.get("KS2") == "3":
            p.barrier()
            return
        sn = A.f32(G)
        cs = A.f32(G)
        ab = A.f32(G)
        self.act(sn, thr, AF.Sin, [bs], [bs])
        self.ts("dve", ab, thr, -1.0, None, ALU.mult, None, [bs], [bs])
        self.tt("dve", ab, ab, thr, ALU.max, [bs], [bs])
        self.act(cs, ab, AF.Sin, [bs, self.b_const], [bs], bias=self.halfpi[:, 0:1], scale=-1.0)
        ar1 = A.f32(G)
        ai = A.f32(G)
        self.tt("dve", ar1, rr, cs, ALU.mult, [bs], [bs])
        self.ts("dve", ar1, ar1, -1.0, None, ALU.add, None, [bs], [bs])
        self.tt("dve", ai, rr, sn, ALU.mult, [bs], [bs])
        if os.environ.get("KS2") == "4":
            p.barrier()
            return
        den = A.f32(G)
        t2 = A.f32(G)
        self.tt("dve", den, lre, lre, ALU.mult, [bl], [bs])
        self.tt("dve", t2, lim, lim, ALU.mult, [bl], [bs])
        self.tt("dve", den, den, t2, ALU.add, [bs], [bs])
        p.op("dve", "reciprocal", dict(out=den, in_=den), [bs], [bs])
        cre = A.f32(G)
        cim = A.f32(G)
        self.tt("dve", cre, ar1, lre, ALU.mult, [bs, bl], [bs])
        self.tt("dve", t2, ai, lim, ALU.mult, [bs, bl], [bs])
        self.tt("dve", cre, cre, t2, ALU.add, [bs], [bs])
        self.tt("dve", cre, cre, den, ALU.mult, [bs], [bs])
        self.tt("dve", cim, ai, lre, ALU.mult, [bs, bl], [bs])
        self.tt("dve", t2, ar1, lim, ALU.mult, [bs, bl], [bs])
        self.tt("dve", cim, cim, t2, ALU.subtract, [bs], [bs])
        self.tt("dve", cim, cim, den, ALU.mult, [bs], [bs])
        if os.environ.get("KS2") == "5":
            p.barrier()
            return
        s1c = A.f32(G)
        s2c = A.f32(G)
        self.ts("dve", s1c, cim, self.s1[:, 0:1], None, ALU.mult, None, [bs, self.b_const], [bs])
        self.ts("dve", s2c, cre, self.s2[:, 0:1], None, ALU.mult, None, [bs, self.b_const], [bs])
        T1 = A.f32(G * 16)
        T13 = T1.rearrange("p (g c) -> p g c", c=16)
        bc3 = lambda ap: ap.unsqueeze(2).broadcast_to([128, G, 16])
        ZB3 = ZB.rearrange("p (g c) -> p g c", c=16)
        ZsB3 = ZsB.rearrange("p (g c) -> p g c", c=16)
        for cc in range(16):
            self.tt("dve", ZB3[:, :, cc], X13[:, :, cc], cre, ALU.mult, [bs, bl], [bs])
            self.tt("dve", T13[:, :, cc], X23[:, :, cc], s1c, ALU.mult, [bs, bl], [bs])
        self.tt("dve", ZB, ZB, T1, ALU.add, [bs], [bs])
        for cc in range(16):
            self.tt("dve", ZsB3[:, :, cc], X23[:, :, cc], s2c, ALU.mult, [bs, bl], [bs])
            self.tt("dve", T13[:, :, cc], X13[:, :, cc], cim, ALU.mult, [bs, bl], [bs])
        self.tt("dve", ZsB, ZsB, T1, ALU.add, [bs], [bs])
        if os.environ.get("KS2") == "6":
            p.barrier()
            return
        for ci, (t0, tw, own) in enumerate(chunks):
            o = offs[:, ci * G:(ci + 1) * G]
            self.ts("dve", tmp, thr, float(t0), None, ALU.mult, None, [bs], [bs])
            self.reduce_angle("dve", o, tmp, ni, [bs], bs, bs)
        thrT = A0thrT
        self.ts("dve", thrT, thr, 1.0 / TWO_PI, None, ALU.mult, None, [bs], [bs])
        self.ts("dve", offs, offs, 1.0 / TWO_PI, None, ALU.mult, None, [bs], [bs])
        p.op("pool", "iota", dict(out=maskc, pattern=[[16, 8]], base=0, channel_multiplier=-1,
                                  allow_small_or_imprecise_dtypes=True), (), [bs])
        mk2 = A.f32(8)
        self.ts("dve", mk2, maskc, 0.0, None, ALU.is_le, None, [bs], [bs])
        self.ts("dve", maskc, maskc, -16.0, None, ALU.is_gt, None, [bs], [bs])
        self.tt("dve", maskc, maskc, mk2, ALU.mult, [bs], [bs])
        p.op("pool", "iota", dict(out=iota, pattern=[[1, 512]], base=0, channel_multiplier=0,
                                  allow_small_or_imprecise_dtypes=True), (), [bs])
        for t in (Cl1, Cl2):
            self.memset("dve", t, 0.0, [bs])
        p.barrier()
        if self.cut():
            return
        A.off = mark
        uTk = A.bf16(c.NT)
        uTf = A.f32(c.NO)
        b_uk = p.buf("uTk")
        b_uf = p.buf("uTf")
        CC = A.f32(128)
        CC2 = A.f32(128)
        b_cc = p.buf("CC")
        b_lhs = p.buf("lhs")
        R2 = lambda n, f, nm: Rot([(f(n), p.buf(nm)) for _ in range(2)])
        r_a = R2(512, A.f32, "a")
        r_ni = Rot([(A.f32(512).bitcast(I32), p.buf("ni")) for _ in range(2)])
        r_r = R2(512, A.f32, "r")
        r_ab = Rot([(A.f32(512), p.buf("ab")) for _ in range(4)])
        r_S = Rot([(A.f32(512), p.buf("S2")) for _ in range(5)])
        r_C = Rot([(A.f32(512), p.buf("C2")) for _ in range(4)])
        r_f = Rot([(A.f32(512), p.buf("f")) for _ in range(3)])
        r_W1 = R2(512, A.f32, "W1")
        r_W2 = R2(512, A.f32, "W2")
        r_W = Rot([(A.f32(512), p.buf("W")) for _ in range(3)])
        r_w = R2(512, A.f32, "w")
        r_V1 = R2(512, A.bf16, "V1")
        r_V2 = R2(512, A.bf16, "V2")
        carry = A.f32(8)
        b_carry = p.bufs(8, "carry")
        r_yv = R2(512, A.f32, "yv")
        r_x2 = R2(512, A.f32, "x2")
        r_sg = R2(512, A.f32, "sg")
        zb = Rot([0, 1])
        zsb = Rot([2, 3])
        yb = Rot([4, 5])
        for k in range(KS):
            rows = slice(k * 128, (k + 1) * 128)
            p.dma("pool", uTk, s["uT"][rows, :], b_uk, self.sb["uT"])
            p.dma("sync", uTf, s["uT"][rows, c.O0:], b_uf, self.sb["uT"])
            p.dma("sync", CC[:, 0:64], d["c_re"][rows, :], b_cc, partial=True)
            p.dma("sync", CC[:, 64:128], d["c_im"][rows, :], b_cc, partial=True)
            p.dma("sync", CC2[:, 0:64], d["c_im"][rows, :], b_cc, partial=True)
            p.dma("sync", CC2[:, 64:128], d["c_re"][rows, :], b_cc, partial=True)
            self.tr(self.ps[6][:, 0:128], CC, [b_cc], [self.psb[6]])
            self.tr(self.ps[6][:, 128:256], CC2, [b_cc], [self.psb[6]])
            self.tr(self.ps[7][:, 0:128], ZB[:, k * 128:(k + 1) * 128], [bs], [self.psb[7]])
            self.tr(self.ps[7][:, 128:256], ZsB[:, k * 128:(k + 1) * 128], [bs], [self.psb[7]])
            for gi in range(8):
                cs_ = slice(gi * 16, (gi + 1) * 16)
                self.ts("dve", Cl1[:, gi, cs_], self.ps[6][:, gi * 16:(gi + 1) * 16], self.s2[:, 0:1], None, ALU.mult, None,
                        [self.psb[6], self.b_const], [b_lhs])
                self.ts("dve", Cl2[:, gi, cs_], self.ps[6][:, 128 + gi * 16:128 + (gi + 1) * 16], -1.0, None, ALU.mult, None,
                        [self.psb[6]], [b_lhs])
                self.ts("dve", Blz[:, gi, :], self.ps[7][:, 0:128], maskc[:, gi:gi + 1], None, ALU.mult, None,
                        [self.psb[7], bs], [b_lhs])
                self.ts("dve", Blzs[:, gi, :], self.ps[7][:, 128:256], maskc[:, gi:gi + 1], None, ALU.mult, None,
                        [self.psb[7], bs], [b_lhs])
            iters = []
            for ci, (t0, tw, own) in enumerate(chunks):
                for gi in range(8):
                    iters.append((ci, t0, tw, own, gi))
            NI = len(iters)
            ybk_of = {}
            st = [dict() for _ in range(NI)]
            MAGIC = 12582912.0

            def a1(i):
                (ci, t0, tw, own, gi) = iters[i]
                g = k * 8 + gi
                (a, b_a) = r_a.next()
                (nn, b_n) = r_r.next()
                (f, b_f) = r_f.next()
                (ab_, b_ab) = r_ab.next()
                (S2, b_S) = r_S.next()
                self.ts("dve", a[:, 0:tw], iota[:, 0:tw], thrT[:, g:g + 1], offs[:, ci * G + g:ci * G + g + 1],
                        ALU.mult, ALU.add, [bs], [b_a])
                self.ts("dve", nn[:, 0:tw], a[:, 0:tw], MAGIC, None, ALU.add, None, [b_a], [b_n])
                self.ts("dve", nn[:, 0:tw], nn[:, 0:tw], MAGIC, None, ALU.subtract, None, [b_n], [b_n])
                self.tt("dve", f[:, 0:tw], a[:, 0:tw], nn[:, 0:tw], ALU.subtract, [b_a, b_n], [b_f])
                self.act(S2[:, 0:tw], f[:, 0:tw], AF.Sin, [b_f], [b_S], scale=TWO_PI)
                self.act(ab_[:, 0:tw], f[:, 0:tw], AF.Sin, [b_f], [b_ab], scale=PI)
                self.act(ab_[:, 0:tw], ab_[:, 0:tw], AF.Square, [b_ab], [b_ab])
                st[i].update(S2=S2, b_S=b_S, ab=ab_, b_ab=b_ab)

            def zmm(i):
                (ci, t0, tw, own, gi) = iters[i]
                z1, z2 = zb.next(), zsb.next()
                self.mm(self.ps[z1][:, 0:tw], Blz[:, gi, :], uTk[:, t0:t0 + tw], True, True, [b_lhs, b_uk], [self.psb[z1]])
                self.mm(self.ps[z2][:, 0:tw], Blzs[:, gi, :], uTk[:, t0:t0 + tw], True, True, [b_lhs, b_uk], [self.psb[z2]])
                st[i].update(z1=z1, z2=z2)

            def c2(i):
                (ci, t0, tw, own, gi) = iters[i]
                (C2, b_C) = r_C.next()
                self.ts("dve", C2[:, 0:tw], st[i]["ab"][:, 0:tw], -2.0, 1.0, ALU.mult, ALU.add, [st[i]["b_ab"]], [b_C])
                st[i].update(C2=C2, b_C=b_C)

            def wstage(i):
                (ci, t0, tw, own, gi) = iters[i]
                d_ = st[i]
                (W1, b_W1) = r_W1.next()
                (W2, b_W2) = r_W2.next()
                (W, b_W) = r_W.next()
                self.tt("dve", W1[:, 0:tw], self.ps[d_["z1"]][:, 0:tw], d_["C2"][:, 0:tw], ALU.mult, [self.psb[d_["z1"]], d_["b_C"]], [b_W1])
                self.tt("dve", W2[:, 0:tw], self.ps[d_["z2"]][:, 0:tw], d_["S2"][:, 0:tw], ALU.mult, [self.psb[d_["z2"]], d_["b_S"]], [b_W2])
                self.tt("pool", W[:, 0:tw], W1[:, 0:tw], W2[:, 0:tw], ALU.add, [b_W1, b_W2], [b_W])
                d_.update(W=W, b_W=b_W)

            def scanstage(i):
                (ci, t0, tw, own, gi) = iters[i]
                d_ = st[i]
                g = k * 8 + gi
                S2, b_S, C2, b_C, W, b_W = d_["S2"], d_["b_S"], d_["C2"], d_["b_C"], d_["W"], d_["b_W"]
                if own and gi == 0:
                    ybk_of[ci] = yb.next()
                ybk = ybk_of.get(ci)
                (w, b_w) = r_w.next()
                init = 0.0 if ci == 0 else carry[:, gi:gi + 1]
                Rl = [b_W, bs] + ([] if ci == 0 else [b_carry[gi]])
                p.op("dve", "tensor_tensor_scan",
                     dict(out=w[:, 0:tw], data0=rr[:, g:g + 1].broadcast_to([128, tw]), data1=W[:, 0:tw], initial=init,
                          op0=ALU.mult, op1=ALU.add), Rl, [b_w])
                if ci < NCH - 1:
                    self.copy("act", carry[:, gi:gi + 1], w[:, tw - 1:tw], [b_w], [b_carry[gi]])
                if own:
                    (V1, b_V1) = r_V1.next()
                    (V2, b_V2) = r_V2.next()
                    self.tt("pool", V1[:, 0:tw], C2[:, 0:tw], w[:, 0:tw], ALU.mult, [b_C, b_w], [b_V1])
                    self.tt("pool", V2[:, 0:tw], S2[:, 0:tw], w[:, 0:tw], ALU.mult, [b_S, b_w], [b_V2])
                    self.mm(self.ps[ybk][:, 0:tw], Cl1[:, gi, :], V1[:, 0:tw], gi == 0, False, [b_lhs, b_V1], [self.psb[ybk]])
                    self.mm(self.ps[ybk][:, 0:tw], Cl2[:, gi, :], V2[:, 0:tw], False, gi == 7, [b_lhs, b_V2], [self.psb[ybk]])
                if own and gi == 7:
                    to = t0 - c.O0
                    (yv, b_yv) = r_yv.next()
                    (x2, b_x2) = r_x2.next()
                    (sg, b_sg) = r_sg.next()
                    self.stt(yv[:, 0:tw], uTf[:, to:to + tw], dsk[:, k:k + 1], self.ps[ybk][:, 0:tw], ALU.mult, ALU.add,
                             [b_uf, bl, self.psb[ybk]], [b_yv])
                    self.act(x2[:, 0:tw], yv[:, 0:tw], AF.Square, [b_yv], [b_x2])
                    self.ts("dve", x2[:, 0:tw], x2[:, 0:tw], 0.044715, 1.0, ALU.mult, ALU.add, [b_x2], [b_x2])
                    self.tt("pool", x2[:, 0:tw], x2[:, 0:tw], yv[:, 0:tw], ALU.mult, [b_x2, b_yv], [b_x2])
                    self.act(sg[:, 0:tw], x2[:, 0:tw], AF.Sigmoid, [b_x2], [b_sg], scale=1.5957691216057308)
                    self.tt("pool", gT[:, k, to:to + tw], yv[:, 0:tw], sg[:, 0:tw], ALU.mult, [b_yv, b_sg], [b_gT])
                st[i].clear()

            a1(0)
            if NI > 1:
                a1(1)
            zmm(0)
            c2(0)
            for i in range(NI + 1):
                if i + 2 < NI:
                    a1(i + 2)
                if i + 1 < NI:
                    zmm(i + 1)
                if i < NI:
                    wstage(i)
                if i + 1 < NI:
                    c2(i + 1)
                if 0 <= i - 1 < NI:
                    scanstage(i - 1)
        p.barrier()
        A.off = mark
        wgl = A.bf16(KS * c.DS).rearrange("p (c n) -> p c n", n=c.DS)
        b_wg = p.buf("wglu")
        p.dma("pool", wgl, d["w_glu"].rearrange("(c p) n -> p c n", p=128), b_wg)
        r_sig = Rot([(A.f32(512), p.buf("sig")) for _ in range(2)])
        r_st = Rot([(A.bf16(512), p.buf("yst")) for _ in range(3)])
        mb = Rot([0, 1, 2, 3])
        for m in range(KS):
            for (to, tw) in tok_tiles(c.NOWN):
                bk = mb.next()
                for kc in range(KS):
                    self.mm(self.ps[bk][:, 0:tw], wgl[:, kc, m * 128:(m + 1) * 128], gT[:, kc, to:to + tw], kc == 0, kc == KS - 1,
                            [b_wg, b_gT], [self.psb[bk]])
                (sg, b_sg) = r_sig.next()
                (st, b_st) = r_st.next()
                self.act(sg[:, 0:tw], self.ps[bk][:, 0:tw], AF.Sigmoid, [self.psb[bk], bl], [b_sg], bias=bgl[:, m:m + 1])
                self.tt("pool", st[:, 0:tw], sg[:, 0:tw], gT[:, m, to:to + tw], ALU.mult, [b_sg, b_gT], [b_st])
                p.dma("sync", s["yssmT"][m * 128:(m + 1) * 128, to:to + tw], st[:, 0:tw], b_st, self.sb["yssmT"], store=True)

    def phase3(self):
        c, p, A, d, s = self.c, self.p, self.A, self.d, self.s
        H = c.H
        kTh = [A.bf16(c.NT) for _ in range(2)]
        qTh = [A.bf16(c.NO) for _ in range(2)]
        Vh = [A.bf16(c.NCTX * 128).rearrange("p (b e) -> p b e", e=128) for _ in range(2)]
        cbh = [A.f32(c.NCTX) for _ in range(2)]
        b_hd = p.bufs(2, "head")
        b_cb = p.bufs(2, "cbh")
        r_pt = Rot([(A.bf16(512), p.buf("pt")) for _ in range(4)])
        r_tmp = Rot([(A.f32(128), p.buf("tmp")) for _ in range(4)])
        r_rs = Rot([(A.f32(512), p.buf("rs")) for _ in range(2)])
        r_o = Rot([(A.f32(512), p.buf("o")) for _ in range(2)])
        r_o2 = Rot([(A.f32(512), p.buf("o2")) for _ in range(2)])
        r_sq = Rot([(A.bf16(512), p.buf("sq")) for _ in range(2)])
        r_rn = Rot([(A.f32(512), p.buf("rn")) for _ in range(2)])
        r_st = Rot([(A.bf16(512), p.buf("st")) for _ in range(2)])
        sbank = Rot([0, 1, 2, 3])
        TD3 = self.TD.rearrange("p (h q) -> p h q", q=128)
        TP3 = self.TP.rearrange("p (h q) -> p h q", q=128)
        for h in range(H):
            j = h % 2
            p.dma("sync", kTh[j], s["kT"][h], b_hd[j], self.sb["kT"], partial=False)
            p.dma("sync", qTh[j], s["qT"][h], b_hd[j], self.sb["qT"], partial=True)
            p.dma("sync", Vh[j], s["v"][:, h * 128:(h + 1) * 128].rearrange("(b p) e -> p b e", p=128), b_hd[j], self.sb["v"], partial=True)
            self.ts("dve", cbh[j], self.kvalid, self.rel31[:, h:h + 1], None, ALU.add, None, [self.b_cload], [b_cb[j]])
            for (to, tw) in tok_tiles(c.NOWN):
                nqb = tw // 128
                qb0 = c.NPRE + to // 128
                nkb = qb0 + nqb
                obk = [4, 5]
                smk = [6, 7]
                for kb in range(nkb):
                    col0 = max(0, kb - qb0) * 128
                    ncols = tw - col0
                    pts = []
                    for m in range(2):
                        sb_ = sbank.next()
                        pr = slice(m * 64, (m + 1) * 64)
                        self.mm(self.ps[sb_][:, 0:ncols], kTh[j][pr, kb * 128:(kb + 1) * 128], qTh[j][pr, to + col0:to + tw],
                                True, True, [b_hd[j]], [self.psb[sb_]])
                        (pt, b_pt) = r_pt.next()
                        near = []
                        qbs = max(qb0, kb)
                        ci = 0
                        for i in range(ncols // 128):
                            qb = qbs + i
                            if qb - kb <= 1:
                                near.append((i, qb - kb))
                            else:
                                break
                        far0 = len(near) * 128
                        lastb = None
                        for (i, dd) in near:
                            (tmp, b_tmp) = r_tmp.next()
                            T3 = TD3 if dd == 0 else TP3
                            self.stt(tmp, self.ps[sb_][:, i * 128:(i + 1) * 128], 0.125, T3[:, h, :], ALU.mult, ALU.add,
                                     [self.psb[sb_], self.b_toep], [b_tmp])
                            self.act(pt[:, i * 128:(i + 1) * 128], tmp, AF.Exp, [b_tmp, self.b_cload], [b_pt],
                                     bias=self.kvalid[:, kb:kb + 1])
                            lastb = b_tmp
                        if far0 < ncols:
                            Rl = [self.psb[sb_], b_cb[j]] + ([lastb] if lastb is not None else [])
                            self.act(pt[:, far0:ncols], self.ps[sb_][:, far0:ncols], AF.Exp, Rl, [b_pt],
                                     bias=cbh[j][:, kb:kb + 1], scale=0.125)
                        pts.append((pt, b_pt))
                    self.flush_pe()

                    def pv(pts=pts, kb=kb, col0=col0, ncols=ncols, tw=tw, j=j, nkb=nkb, obk=obk, smk=smk):
                        for m in range(2):
                            (pt, b_pt) = pts[m]
                            self.mm(self.ps[obk[m]][:, col0:tw], Vh[j][:, kb, :], pt[:, 0:ncols], kb == 0, kb == nkb - 1,
                                    [b_hd[j], b_pt], [self.psb[obk[m]]])
                            self.mm(self.ps[smk[m]][:, col0:tw], self.onesb, pt[:, 0:ncols], kb == 0, kb == nkb - 1,
                                    [self.b_const, b_pt], [self.psb[smk[m]]])
                    self.pe_def.append(pv)
                self.flush_pe()
                rsl = []
                for m in range(2):
                    (rs, b_rs) = r_rs.next()
                    self.ts("dve", rs[:, 0:tw], self.ps[smk[m]][:, 0:tw], 1e-30, None, ALU.max, None, [self.psb[smk[m]]], [b_rs])
                    p.op("dve", "reciprocal", dict(out=rs[:, 0:tw], in_=rs[:, 0:tw]), [b_rs], [b_rs])
                    rsl.append((rs, b_rs))
                (o1, b_o1) = r_o.next()
                (o2, b_o2) = r_o2.next()
                self.tt("dve", o1[:, 0:tw], self.ps[obk[0]][:, 0:tw], rsl[0][0][:, 0:tw], ALU.mult, [self.psb[obk[0]], rsl[0][1]], [b_o1])
                self.tt("dve", o2[:, 0:tw], self.ps[obk[1]][:, 0:tw], rsl[1][0][:, 0:tw], ALU.mult, [self.psb[obk[1]], rsl[1][1]], [b_o2])
                self.stt(o1[:, 0:tw], o2[:, 0:tw], self.neglam[:, 0:1], o1[:, 0:tw], ALU.mult, ALU.add, [b_o2, self.b_const], [b_o1])
                (sq, b_sq) = r_sq.next()
                (rn, b_rn) = r_rn.next()
                (st, b_st) = r_st.next()
                self.act(sq[:, 0:tw], o1[:, 0:tw], AF.Square, [b_o1], [b_sq])
                nb = sbank.next()
                self.mm(self.ps[nb][:, 0:tw], self.onesb, sq[:, 0:tw], True, True, [self.b_const, b_sq], [self.psb[nb]])
                self.rsqrt_ps(rn[:, 0:tw], self.ps[nb][:, 0:tw], 1.0 / 128, [self.psb[nb]], b_rn)
                self.stt(st[:, 0:tw], o1[:, 0:tw], self.slng[:, 0:1], rn[:, 0:tw], ALU.mult, ALU.mult, [b_o1, b_rn, self.b_cload], [b_st])
                p.dma("sync", s["yattT"][h * 128:(h + 1) * 128, to:to + tw], st[:, 0:tw], b_st, self.sb["yattT"], store=True)

    def phase4(self):
        c, p, A, d, s = self.c, self.p, self.A, self.d, self.s
        KD, KS = c.KD, c.KS
        KA = c.DATT // 128
        KB = KS + KA
        base = A.off
        for grp in self.own_supers():
            A.off = base
            S0 = grp[0][0]
            S = sum(t[1] for t in grp)
            mg = A.bf16(KD * S).rearrange("p (c t) -> p c t", t=S)
            b_mg = p.buf("merged")
            mark = A.off
            ybT = A.bf16(KB * S).rearrange("p (c t) -> p c t", t=S)
            b_yb = p.buf("ybT")
            p.dma("sync", ybT[:, 0:KS, :], s["yssmT"][:, S0:S0 + S].rearrange("(c p) t -> p c t", p=128), b_yb, self.sb["yssmT"], partial=True)
            p.dma("sync", ybT[:, KS:KB, :], s["yattT"][:, S0:S0 + S].rearrange("(c p) t -> p c t", p=128), b_yb, self.sb["yattT"], partial=True)
            wbr = [A.bf16(KB * 128).rearrange("p (c n) -> p c n", n=128) for _ in range(2)]
            b_w = p.bufs(2, "wbr")
            r_g1 = Rot([(A.bf16(512), p.buf("g1")) for _ in range(3)])
            r_g2 = Rot([(A.bf16(512), p.buf("g2")) for _ in range(3)])
            r_t1 = Rot([(A.f32(512), p.buf("t1")) for _ in range(2)])
            r_t2 = Rot([(A.f32(512), p.buf("t2")) for _ in range(2)])
            ba = Rot([0, 1, 2])
            bb = Rot([3, 4, 5])
            for m in range(KD):
                wj = m % 2
                p.dma("pool", wbr[wj], d["w_branch"][:, m * 128:(m + 1) * 128].rearrange("(c p) n -> p c n", p=128), b_w[wj])
                tl = 0
                for (to, tw) in grp:
                    ka, kb_ = ba.next(), bb.next()
                    for kc in range(KS):
                        self.mm(self.ps[ka][:, 0:tw], wbr[wj][:, kc, :], ybT[:, kc, tl:tl + tw], kc == 0, kc == KS - 1,
                                [b_w[wj], b_yb], [self.psb[ka]])
                    for kc in range(KA):
                        self.mm(self.ps[kb_][:, 0:tw], wbr[wj][:, KS + kc, :], ybT[:, KS + kc, tl:tl + tw], kc == 0, kc == KA - 1,
                                [b_w[wj], b_yb], [self.psb[kb_]])
                    (g1, b_g1) = r_g1.next()
                    (g2, b_g2) = r_g2.next()
                    (t1, b_t1) = r_t1.next()
                    (t2, b_t2) = r_t2.next()
                    p.dma("sync", g1[:, 0:tw], s["sgs"][m * 128:(m + 1) * 128, to:to + tw], b_g1, self.sb["sgs"])
                    p.dma("sync", g2[:, 0:tw], s["sga"][m * 128:(m + 1) * 128, to:to + tw], b_g2, self.sb["sga"])
                    self.tt("dve", t1[:, 0:tw], self.ps[ka][:, 0:tw], g1[:, 0:tw], ALU.mult, [self.psb[ka], b_g1], [b_t1])
                    self.tt("dve", t2[:, 0:tw], self.ps[kb_][:, 0:tw], g2[:, 0:tw], ALU.mult, [self.psb[kb_], b_g2], [b_t2])
                    self.tt("pool", mg[:, m, tl:tl + tw], t1[:, 0:tw], t2[:, 0:tw], ALU.add, [b_t1, b_t2], [b_mg])
                    tl += tw
            p.barrier()
            A.off = mark
            wo = [A.bf16(KD * c.MB).rearrange("p (c n) -> p c n", n=c.MB) for _ in range(2)]
            b_wo = p.bufs(2, "wo")
            acc = A.f32(S)
            b_acc = p.buf("acc")
            self.memset("dve", acc, 0.0, [b_acc])
            r_hb = Rot([(A.f32(512), p.buf("hb")) for _ in range(3)])
            r_h2 = Rot([(A.f32(512), p.buf("h2")) for _ in range(3)])
            r_a2 = Rot([(A.bf16(512), p.buf("a2")) for _ in range(3)])
            r_sq = Rot([(A.f32(512), p.buf("sq")) for _ in range(2)])
            mb = Rot([0, 1, 2, 3, 4])
            wi = 0
            for c0 in range(0, c.D, c.MB):
                wj = wi % 2
                wi += 1
                p.dma("pool", wo[wj], d["w_o"][:, c0:c0 + c.MB].rearrange("(c p) n -> p c n", p=128), b_wo[wj])
                for sub in range(c.NSUB):
                    m = c0 // 128 + sub
                    tl = 0
                    for (to, tw) in grp:
                        bk = mb.next()
                        for kc in range(KD):
                            self.mm(self.ps[bk][:, 0:tw], wo[wj][:, kc, sub * 128:(sub + 1) * 128], mg[:, kc, tl:tl + tw],
                                    kc == 0, kc == KD - 1, [b_wo[wj], b_mg], [self.psb[bk]])
                        (hb, b_hb) = r_hb.next()
                        (h2, b_h2) = r_h2.next()
                        (a2, b_a2) = r_a2.next()
                        (sq, b_sq) = r_sq.next()
                        p.dma("sync", hb[:, 0:tw], s["hT"][m * 128:(m + 1) * 128, to:to + tw], b_hb, self.sb["hT"])
                        self.tt("dve", h2[:, 0:tw], self.ps[bk][:, 0:tw], hb[:, 0:tw], ALU.add, [self.psb[bk], b_hb], [b_h2])
                        p.dma("sync", s["h2T"][m * 128:(m + 1) * 128, to:to + tw], h2[:, 0:tw], b_h2, self.sb["h2T"], store=True)
                        self.act(a2[:, 0:tw], h2[:, 0:tw], AF.Copy, [b_h2, self.b_cload], [b_a2], scale=self.g2col[:, m:m + 1])
                        p.dma("sync", s["a2T"][m * 128:(m + 1) * 128, to:to + tw], a2[:, 0:tw], b_a2, self.sb["a2T"], store=True)
                        self.act(sq[:, 0:tw], h2[:, 0:tw], AF.Square, [b_h2], [b_sq])
                        self.tt("pool", acc[:, tl:tl + tw], acc[:, tl:tl + tw], sq[:, 0:tw], ALU.add, [b_sq], [b_acc])
                        tl += tw
            tl = 0
            for (to, tw) in grp:
                bk = mb.next()
                (h2, b_h2) = r_h2.next()
                self.mm(self.ps[bk][:, 0:tw], self.onesf, acc[:, tl:tl + tw], True, True, [self.b_const, b_acc], [self.psb[bk]])
                self.rsqrt_ps(h2[:, 0:tw], self.ps[bk][:, 0:tw], 1.0 / c.D, [self.psb[bk]], b_h2)
                p.dma("sync", s["rstd2"][:, to:to + tw], h2[:, 0:tw], b_h2, self.sb["rstd2"], store=True)
                tl += tw
            p.barrier()

    def phase5(self):
        c, p, A, d, s = self.c, self.p, self.A, self.d, self.s
        KD = c.KD
        base = A.off
        for grp in self.own_supers():
            A.off = base
            S0 = grp[0][0]
            S = sum(t[1] for t in grp)
            a2 = A.bf16(KD * S).rearrange("p (c t) -> p c t", t=S)
            b_a2 = p.buf("a2")
            rst = A.f32(S)
            b_rst = p.buf("rst")
            p.dma("sync", a2, s["a2T"][:, S0:S0 + S].rearrange("(c p) t -> p c t", p=128), b_a2, self.sb["a2T"])
            p.dma("sync", rst, s["rstd2"][:, S0:S0 + S], b_rst, self.sb["rstd2"])
            wg = [A.bf16(KD * c.MB).rearrange("p (c n) -> p c n", n=c.MB) for _ in range(2)]
            wu = [A.bf16(KD * c.MB).rearrange("p (c n) -> p c n", n=c.MB) for _ in range(2)]
            b_wg = p.bufs(2, "wg")
            b_wu = p.bufs(2, "wu")
            r_t1 = Rot([(A.f32(512), p.buf("t1")) for _ in range(2)])
            r_sg = Rot([(A.f32(512), p.buf("sg")) for _ in range(2)])
            r_t3 = Rot([(A.f32(512), p.buf("t3")) for _ in range(2)])
            r_st = Rot([(A.bf16(512), p.buf("st")) for _ in range(3)])
            bg = Rot([0, 1, 2, 3])
            bu = Rot([4, 5, 6, 7])
            wi = 0
            for c0 in range(0, c.DFF, c.MB):
                wj = wi % 2
                wi += 1
                p.dma("pool", wg[wj], d["w_gate_up"][:, c0:c0 + c.MB].rearrange("(c p) n -> p c n", p=128), b_wg[wj])
                p.dma("pool", wu[wj], d["w_gate_up"][:, c.DFF + c0:c.DFF + c0 + c.MB].rearrange("(c p) n -> p c n", p=128), b_wu[wj])
                for sub in range(c.NSUB):
                    m = c0 // 128 + sub
                    tl = 0
                    for (to, tw) in grp:
                        kg, ku = bg.next(), bu.next()
                        for kc in range(KD):
                            self.mm(self.ps[kg][:, 0:tw], wg[wj][:, kc, sub * 128:(sub + 1) * 128], a2[:, kc, tl:tl + tw],
                                    kc == 0, kc == KD - 1, [b_wg[wj], b_a2], [self.psb[kg]])
                        for kc in range(KD):
                            self.mm(self.ps[ku][:, 0:tw], wu[wj][:, kc, sub * 128:(sub + 1) * 128], a2[:, kc, tl:tl + tw],
                                    kc == 0, kc == KD - 1, [b_wu[wj], b_a2], [self.psb[ku]])
                        (t1, b_t1) = r_t1.next()
                        (sg, b_sg) = r_sg.next()
                        (t3, b_t3) = r_t3.next()
                        (st, b_st) = r_st.next()
                        self.tt("dve", t1[:, 0:tw], self.ps[kg][:, 0:tw], rst[:, tl:tl + tw], ALU.mult, [self.psb[kg], b_rst], [b_t1])
                        self.act(sg[:, 0:tw], t1[:, 0:tw], AF.Silu, [b_t1], [b_sg])
                        self.tt("dve", t3[:, 0:tw], self.ps[ku][:, 0:tw], rst[:, tl:tl + tw], ALU.mult, [self.psb[ku], b_rst], [b_t3])
                        self.tt("pool", st[:, 0:tw], sg[:, 0:tw], t3[:, 0:tw], ALU.mult, [b_sg, b_t3], [b_st])
                        p.dma("sync", s["act3T"][m * 128:(m + 1) * 128, to:to + tw], st[:, 0:tw], b_st, self.sb["act3T"], store=True)
                        tl += tw
            p.barrier()

    def phase6(self):
        c, p, A, d, s = self.c, self.p, self.A, self.d, self.s
        KF, KD = c.KF, c.KD
        PC = 32
        KH = (KF + 1) // 2
        halves = [(0, KH), (KH, KF)]
        base = A.off
        for grp in self.own_supers():
            S0 = grp[0][0]
            S = sum(t[1] for t in grp)
            for hi, (h0, h1) in enumerate(halves):
                A.off = base
                pieces = [(c0, min(h1, c0 + PC)) for c0 in range(h0, h1, PC)]
                a3src = s["act3T"][:, S0:S0 + S].rearrange("(c p) t -> p c t", p=128)
                a3p, b_a3 = [], []
                for (c0, c1) in pieces:
                    t = A.bf16((c1 - c0) * S).rearrange("p (c t) -> p c t", t=S)
                    b = p.buf("a3")
                    p.dma("sync", t, a3src[:, c0:c1, :], b, self.sb["act3T"])
                    p.fence("sync", b)
                    a3p.append(t)
                    b_a3.append(b)
                r_wd = Rot([(A.bf16(PC * 128).rearrange("p (c n) -> p c n", n=128), p.buf("wd")) for _ in range(3)])
                r_hb = Rot([(A.f32(512), p.buf("hb")) for _ in range(4)])
                r_h3 = Rot([(A.f32(512), p.buf("h3")) for _ in range(4)])
                r_os = Rot([(A.f32(512), p.buf("ost")) for _ in range(3)])
                mb = Rot([0, 1, 2, 3, 4, 5] if hi == 0 else [0, 1, 2, 3])
                tb = Rot([4, 5, 6, 7])
                src_prev = s["h2T"] if hi == 0 else s["hT"]
                sb_prev = self.sb["h2T"] if hi == 0 else self.sb["hT"]
                for m in range(KD):
                    wsrc = d["w_down"][:, m * 128:(m + 1) * 128].rearrange("(c p) n -> p c n", p=128)
                    banks = [mb.next() for _ in grp]
                    for pi, (c0, c1) in enumerate(pieces):
                        (wd, b_wd) = r_wd.next()
                        p.dma("pool", wd[:, 0:c1 - c0, :], wsrc[:, c0:c1, :], b_wd)
                        tl = 0
                        for ti, (to, tw) in enumerate(grp):
                            bk = banks[ti]
                            for kc in range(c0, c1):
                                self.mm(self.ps[bk][:, 0:tw], wd[:, kc - c0, :], a3p[pi][:, kc - c0, tl:tl + tw], kc == h0, kc == h1 - 1,
                                        [b_wd, b_a3[pi]], [self.psb[bk]])
                            tl += tw
                    self.flush_pe()
                    for ti, (to, tw) in enumerate(grp):
                        bk = banks[ti]
                        (hb, b_hb) = r_hb.next()
                        (h3, b_h3) = r_h3.next()
                        p.dma("sync", hb[:, 0:tw], src_prev[m * 128:(m + 1) * 128, to:to + tw], b_hb, sb_prev)
                        self.tt("dve", h3[:, 0:tw], self.ps[bk][:, 0:tw], hb[:, 0:tw], ALU.add, [self.psb[bk], b_hb], [b_h3])
                        if hi == 0:
                            p.dma("sync", s["hT"][m * 128:(m + 1) * 128, to:to + tw], h3[:, 0:tw], b_h3, self.sb["hT"], store=True)
                        else:
                            (os_, b_os) = r_os.next()
                            nb = tw // 128

                            def fin(h3=h3, b_h3=b_h3, os_=os_, b_os=b_os, m=m, to=to, tw=tw, nb=nb):
                                tk = tb.next()
                                for i in range(nb):
                                    self.tr(self.ps[tk][:, i * 128:(i + 1) * 128], h3[:, i * 128:(i + 1) * 128], [b_h3], [self.psb[tk]])
                                self.copy("dve", os_[:, 0:tw], self.ps[tk][:, 0:tw], [self.psb[tk]], [b_os])
                                p.dma("sync", self.out[to:to + tw, m * 128:(m + 1) * 128].rearrange("(i p) f -> p i f", p=128),
                                      os_[:, 0:tw].rearrange("p (i f) -> p i f", f=128), b_os, self.b_out, store=True)
                            self.pe_def.append(fin)
                self.flush_pe()
                p.barrier()


def t5_bucket_np(n):
    n = np.asarray(n)
    nf = np.maximum(n, 16).astype(np.float32)
    log_b = 16 + (np.log(nf / np.float32(16)).astype(np.float32) / np.float32(math.log(128 / 16)) * np.float32(16)).astype(np.int32)
    return np.where(n < 16, n, np.minimum(log_b, 31))


def static_consts(cfg):
    ohc = np.zeros((32, FL), np.float32)
    dist = np.arange(FL) - 127
    bk = t5_bucket_np(np.maximum(dist, 0))
    for j in range(FL):
        if dist[j] >= 0:
            ohc[bk[j], j] = 1.0
    negmask = np.zeros((cfg.H, FL), np.float32)
    negmask[:, :127] = NEG
    return ohc, negmask


def make_in_maps(cfg, inputs, n_batch):
    c = cfg
    f = lambda a: np.ascontiguousarray(np.asarray(a, dtype=np.float32))
    x = np.asarray(inputs["x"], dtype=np.float32)
    meta = f(inputs["meta_tokens"])
    ohc, negmask = static_consts(c)
    shared = {
        "ohc": ohc, "negmask": negmask, "rel_bias": f(inputs["rel_bias"]),
        "ln1_g": f(inputs["ln1_g"][0]), "ln2_g": f(inputs["ln2_g"][0]), "w_in": f(inputs["w_in"][0]),
        "subln_g": f(inputs["subln_g"][0]),
        "lam_re": f(inputs["lam_re"][0]), "lam_im": f(inputs["lam_im"][0]), "log_dt": f(inputs["log_dt"][0]),
        "b_re": f(inputs["b_re"][0]), "b_im": f(inputs["b_im"][0]),
        "c_re": f(inputs["c_re"][0]).reshape(c.G * 16, 64), "c_im": f(inputs["c_im"][0]).reshape(c.G * 16, 64),
        "d_skip": f(inputs["d_skip"][0]), "w_glu": f(inputs["w_glu"][0]), "b_glu": f(inputs["b_glu"][0]),
        "w_branch": f(inputs["w_branch"][0]), "w_o": f(inputs["w_o"][0]),
        "w_gate_up": f(inputs["w_gate_up"][0]), "w_down": f(inputs["w_down"][0]),
    }
    for n in ("q_norm_g", "k_norm_g", "lam_q1", "lam_k1", "lam_q2", "lam_k2"):
        shared[n] = f(inputs[n][0])
    NB = c.NCTX
    maps = []
    for b in range(n_batch):
        hpad = np.zeros((NB * 128, c.D), np.float32)
        hpad[PADF:PADF + N_META] = meta
        hpad[PADF + N_META:] = x[b]
        for half in range(2):
            hp = np.zeros((c.NT, c.D), np.float32)
            kv = np.full((c.NT,), NEG, np.float32)
            if half == 0:
                hp[c.O0:] = hpad[:c.NO]
                kv[c.O0 + PADF:] = 0.0
            else:
                hp[:] = hpad
                kv[PADF:] = 0.0
            m = dict(shared)
            m["hp"] = hp
            m["kvalid"] = np.ascontiguousarray(kv.reshape(c.NCTX, 128).T)
            maps.append(m)
    return maps


_NC_CACHE = {}


def run_cfg(cfg, inputs, n_batch, debug=False, stop_after=99, trace=False):
    key = (id(cfg), debug, stop_after)
    if key not in _NC_CACHE:
        _NC_CACHE[key] = Builder(cfg, debug=debug, stop_after=stop_after).build()
    nc = _NC_CACHE[key]
    maps = make_in_maps(cfg, inputs, n_batch)
    res = run_bass_kernel_spmd(nc, maps, core_ids=list(range(2 * n_batch)), trace=trace)
    return res


def assemble(cfg, res, n_batch, seq):
    c = cfg
    out = np.zeros((n_batch, seq, c.D), np.float32)
    n0 = c.NO - 128
    for b in range(n_batch):
        o0 = res.results[2 * b]["out"]
        o1 = res.results[2 * b + 1]["out"]
        out[b, :n0] = o0[128:]
        out[b, n0:] = o1[c.NO - (seq - n0):]
    return out


FULL = Cfg()


def kernel(**inputs):
    res = run_cfg(FULL, inputs, 4)
    return assemble(FULL, res, 4, 4096)
```

```python
import math
import os
from contextlib import ExitStack

import numpy as np
import concourse.bass as bass
import concourse.mybir as mybir
from concourse.bass_utils import run_bass_kernel_spmd

F32 = mybir.dt.float32
BF16 = mybir.dt.bfloat16
AF = mybir.ActivationFunctionType
ALU = mybir.AluOpType
AX = mybir.AxisListType
NEG = -30000.0
EPS = 1e-6
PI = math.pi
TWO_PI = 2.0 * math.pi
N_META = 16
PADF = 112
FL = 384


class Cfg:
    def __init__(self, D=4096, DS=1024, H=16, DFF=11008, NPRE=16, NOWN=17, NSUB=2, SMAX=1152, depth_l=0):
        self.D, self.DS, self.H, self.DFF = D, DS, H, DFF
        self.NPRE, self.NOWN, self.NSUB, self.SMAX = NPRE, NOWN, NSUB, SMAX
        self.G = DS // 16
        self.QKW = H * 128
        self.DATT = H * 128
        self.DIN = DS + 2 * self.QKW + self.DATT + 2 * D
        self.cU, self.cQ = 0, DS
        self.cK = DS + self.QKW
        self.cV = DS + 2 * self.QKW
        self.cGS = self.cV + self.DATT
        self.cGA = self.cGS + D
        self.NCTX = NPRE + NOWN
        self.NT = self.NCTX * 128
        self.NO = NOWN * 128
        self.O0 = NPRE * 128
        self.KD = D // 128
        self.KS = DS // 128
        self.KF = DFF // 128
        self.MB = NSUB * 128
        self.lambda_init = 0.8 - 0.6 * math.exp(-0.3 * depth_l)


def tok_tiles(nblocks, maxb=4):
    nt = -(-nblocks // maxb)
    base, rem = divmod(nblocks, nt)
    out, s = [], 0
    for i in range(nt):
        w = base + (1 if i < rem else 0)
        out.append((s * 128, w * 128))
        s += w
    return out


def super_tiles(tiles, smax):
    out, cur, tot = [], [], 0
    for t in tiles:
        if cur and tot + t[1] > smax:
            out.append(cur)
            cur, tot = [], 0
        cur.append(t)
        tot += t[1]
    if cur:
        out.append(cur)
    return out


class Buf:
    __slots__ = ("name", "w", "r", "slot", "gen")

    def __init__(self, name):
        self.name, self.w, self.r, self.slot, self.gen = name, [], [], None, -1


class Prog:
    ENG = ("sync", "act", "pool", "dve", "pe")
    EPOCH = 30000
    DLIM = 3000

    def __init__(self):
        self.ops = {e: [] for e in self.ENG}
        self.cnt = {e: 0 for e in self.ENG}
        self.seen = {e: {} for e in self.ENG}
        self.nb = 0
        self.semkeys, self.semset = [], set()
        self.dlast = {}
        self.gen = 0
        self.slots = []
        self.nfree = 0

    def buf(self, name="b"):
        self.nb += 1
        return Buf("%s#%d" % (name, self.nb))

    def bufs(self, n, name="b"):
        return [self.buf(name) for _ in range(n)]

    def _sem(self, sk):
        if sk not in self.semset:
            self.semset.add(sk)
            self.semkeys.append(sk)

    def _need(self, eng, ev, waits):
        k, v = ev
        if k[0] == "e" and k[1] == "pe" and eng == "pe":
            return
        if self.seen[eng].get(k, 0) >= v:
            return
        if waits.get(k, 0) < v:
            waits[k] = v

    def _commit(self, eng, waits):
        wl = []
        for k, v in waits.items():
            self.seen[eng][k] = v
            if k[0] == "e":
                sk = ("e", k[1], (v - 1) // self.EPOCH)
                val = (v - 1) % self.EPOCH + 1
            else:
                sk, val = k, v
            self._sem(sk)
            wl.append((sk, val))
        return wl

    @staticmethod
    def _add(lst, ev):
        return [x for x in lst if x[0] != ev[0]] + [ev]

    def op(self, eng, name, kw, R=(), W=()):
        waits = {}
        for b in R:
            for ev in b.w:
                self._need(eng, ev, waits)
        for b in W:
            for ev in b.w:
                self._need(eng, ev, waits)
            for ev in b.r:
                self._need(eng, ev, waits)
        wl = self._commit(eng, waits)
        self.cnt[eng] += 1
        c = self.cnt[eng]
        ev = (("e", eng), c)
        sk = ("e", eng, (c - 1) // self.EPOCH)
        self._sem(sk)
        self.ops[eng].append((wl, name, kw, sk, 1))
        for b in W:
            b.w = [ev]
            b.r = []
        for b in R:
            if b not in W:
                b.r = self._add(b.r, ev)

    def dma(self, q, out, in_, sb, dr=None, store=False, partial=False, **kw):
        waits = {}
        if sb.gen != self.gen:
            if self.nfree >= len(self.slots):
                self.slots.append([0, 0])
            sb.slot = self.nfree
            self.nfree += 1
            sb.gen = self.gen
            sl = self.slots[sb.slot]
            if sl[1] >= self.DLIM:
                sl[0] += 1
                sl[1] = 0
        sl = self.slots[sb.slot]
        key = ("d", sb.slot, sl[0])
        if store:
            for ev in sb.w:
                self._need(q, ev, waits)
        else:
            if dr is not None:
                for ev in dr.w:
                    self._need(q, ev, waits)
            for ev in sb.w:
                if partial and ev[0] == key:
                    continue
                self._need(q, ev, waits)
            for ev in sb.r:
                self._need(q, ev, waits)
        wl = self._commit(q, waits)
        sl[1] += 1
        ev = (key, 16 * sl[1])
        self._sem(key)
        self.dlast[key] = 16 * sl[1]
        d = dict(out=out, in_=in_)
        d.update(kw)
        self.ops[q].append((wl, "dma_start", d, key, 16))
        if store:
            sb.r = self._add(sb.r, ev)
            if dr is not None:
                dr.w = self._add(dr.w, ev)
        else:
            if partial:
                sb.w = self._add(sb.w, ev)
            else:
                sb.w = [ev]
                sb.r = []
            if dr is not None:
                dr.r = self._add(dr.r, ev)

    def fence(self, q, b):
        waits = {}
        for ev in b.w:
            self._need(q, ev, waits)
        wl = self._commit(q, waits)
        if wl:
            self.ops[q].append((wl, None, None, None, 0))

    def barrier(self, allbufs=()):
        evs = [(("e", e), self.cnt[e]) for e in self.ENG if self.cnt[e] > 0]
        evs += [(k, v) for k, v in self.dlast.items()]
        for eng in self.ENG:
            waits = {}
            for ev in evs:
                self._need(eng, ev, waits)
            wl = self._commit(eng, waits)
            if wl:
                self.ops[eng].append((wl, None, None, None, 0))
        self.gen += 1
        self.nfree = 0

    def emit(self, nc, es):
        sems = {}
        for i, sk in enumerate(self.semkeys):
            sems[sk] = es.enter_context(nc.semaphore("s%d" % i))
        block = es.enter_context(nc.Block())
        if os.environ.get("KDUMP"):
            for e in self.ENG:
                ops = self.ops[e]
                idx = [i for i, o in enumerate(ops) if o[1] is None]
                st = idx[-int(os.environ.get("KDB", "3"))]
                print("ENGINE", e, "total", len(ops), "from", st)
                for o in ops[st:st + int(os.environ["KDUMP"])]:
                    print("   waits", o[0], "op", o[1], "inc", o[3], o[4])

        def run(engname):
            def f(e):
                for (wl, name, kw, sk, inc) in self.ops[engname]:
                    for (wk, val) in wl:
                        e.wait_ge(sems[wk], val)
                    if name is None:
                        continue
                    ins = getattr(e, name)(**kw)
                    ins.then_inc(sems[sk], inc)
            return f

        block.sync(run("sync"))
        block.scalar(run("act"))
        block.gpsimd(run("pool"))
        block.vector(run("dve"))
        block.tensor(run("pe"))


class Arena:
    def __init__(self, big, nwords):
        self.big, self.n, self.off = big, nwords, 0

    def f32(self, n):
        a = self.off
        self.off += n
        assert self.off <= self.n, ("arena overflow", self.off, self.n)
        return self.big[:, a:a + n]

    def bf16(self, n):
        w = (n + 1) // 2
        a = self.off
        self.off += w
        assert self.off <= self.n, ("arena overflow", self.off, self.n)
        return self.big[:, a:a + w].bitcast(BF16)[:, 0:n]


class Rot:
    def __init__(self, items):
        self.items, self.i = items, 0

    def next(self):
        it = self.items[self.i % len(self.items)]
        self.i += 1
        return it


class Builder:
    def __init__(self, cfg, debug=False, stop_after=99):
        self.c = cfg
        self.debug = debug
        self.stop_after = stop_after

    def act(self, out, in_, func, R, W, bias=None, scale=None, accum_out=None):
        kw = dict(out=out, in_=in_, func=func)
        if bias is not None:
            kw["bias"] = bias
        if scale is not None:
            kw["scale"] = scale
        if accum_out is not None:
            kw["accum_out"] = accum_out
        self.p.op("act", "activation", kw, R, W)

    def ts(self, eng, out, in0, s1, s2, op0, op1, R, W):
        kw = dict(out=out, in0=in0, scalar1=s1, scalar2=s2, op0=op0)
        if op1 is not None:
            kw["op1"] = op1
        self.p.op(eng, "tensor_scalar", kw, R, W)

    def tt(self, eng, out, in0, in1, op, R, W):
        self.p.op(eng, "tensor_tensor", dict(out=out, in0=in0, in1=in1, op=op), R, W)

    def stt(self, out, in0, scalar, in1, op0, op1, R, W):
        self.p.op("dve", "scalar_tensor_tensor", dict(out=out, in0=in0, scalar=scalar, in1=in1, op0=op0, op1=op1), R, W)

    def copy(self, eng, out, in_, R, W):
        if eng == "act":
            self.act(out, in_, AF.Copy, R, W)
        else:
            self.p.op(eng, "tensor_copy", dict(out=out, in_=in_), R, W)

    def memset(self, eng, ap, val, W):
        self.p.op(eng, "memset", dict(ap=ap, constant=val), (), W)

    def mm(self, out, lhsT, rhs, start, stop, R, W):
        self.p.op("pe", "matmul", dict(out=out, lhsT=lhsT, rhs=rhs, start=start, stop=stop), R, W)

    def tr(self, out, in_, R, W):
        self.p.op("pe", "transpose", dict(out=out, in_=in_, identity=self.ident), list(R) + [self.b_const], W)

    def cut(self):
        self.ncut = getattr(self, "ncut", 0) + 1
        return self.ncut >= int(os.environ.get("KCUT2", "999"))

    def flush_pe(self):
        for f in self.pe_def:
            f()
        self.pe_def = []

    def build(self):
        c = self.c
        nc = bass.Bass("TRN2", target_bir_lowering=False)
        self.nc = nc
        p = Prog()
        self.p = p
        self.pe_def = []
        IN = lambda name, shape: nc.dram_tensor(name, list(shape), F32, kind="ExternalInput").ap()
        skind = "ExternalOutput" if self.debug else "Internal"
        SC = lambda name, shape, dt: nc.dram_tensor(name, list(shape), dt, kind=skind).ap()
        d = {}
        d["hp"] = IN("hp", (c.NT, c.D))
        d["kvalid"] = IN("kvalid", (128, c.NCTX))
        d["ohc"] = IN("ohc", (32, FL))
        d["negmask"] = IN("negmask", (c.H, FL))
        d["rel_bias"] = IN("rel_bias", (32, c.H))
        d["ln1_g"] = IN("ln1_g", (c.D,))
        d["ln2_g"] = IN("ln2_g", (c.D,))
        d["w_in"] = IN("w_in", (c.D, c.DIN))
        for n in ("q_norm_g", "k_norm_g", "lam_q1", "lam_k1", "lam_q2", "lam_k2"):
            d[n] = IN(n, (64,))
        d["subln_g"] = IN("subln_g", (128,))
        d["lam_re"] = IN("lam_re", (c.G, 64))
        d["lam_im"] = IN("lam_im", (c.G, 64))
        d["log_dt"] = IN("log_dt", (c.G,))
        d["b_re"] = IN("b_re", (c.G, 64, 16))
        d["b_im"] = IN("b_im", (c.G, 64, 16))
        d["c_re"] = IN("c_re", (c.G * 16, 64))
        d["c_im"] = IN("c_im", (c.G * 16, 64))
        d["d_skip"] = IN("d_skip", (c.DS,))
        d["w_glu"] = IN("w_glu", (c.DS, c.DS))
        d["b_glu"] = IN("b_glu", (c.DS,))
        d["w_branch"] = IN("w_branch", (c.DS + c.DATT, c.D))
        d["w_o"] = IN("w_o", (c.D, c.D))
        d["w_gate_up"] = IN("w_gate_up", (c.D, 2 * c.DFF))
        d["w_down"] = IN("w_down", (c.DFF, c.D))
        self.d = d
        s = {}
        s["uT"] = SC("uT", (c.DS, c.NT), F32)
        s["qT"] = SC("qT", (c.H, 128, c.NO), BF16)
        s["kT"] = SC("kT", (c.H, 128, c.NT), BF16)
        s["v"] = SC("v", (c.NT, c.DATT), BF16)
        s["sgs"] = SC("sgs", (c.D, c.NO), BF16)
        s["sga"] = SC("sga", (c.D, c.NO), BF16)
        s["hT"] = SC("hT", (c.D, c.NO), F32)
        s["yssmT"] = SC("yssmT", (c.DS, c.NO), BF16)
        s["yattT"] = SC("yattT", (c.DATT, c.NO), BF16)
        s["h2T"] = SC("h2T", (c.D, c.NO), F32)
        s["a2T"] = SC("a2T", (c.D, c.NO), BF16)
        s["rstd2"] = SC("rstd2", (128, c.NO), F32)
        s["act3T"] = SC("act3T", (c.DFF, c.NO), BF16)
        s["rrep"] = nc.dram_tensor("rrep", [c.H, 128, FL], F32, kind="Internal").ap()
        self.s = s
        self.sb = {k: p.buf("dram_" + k) for k in s}
        self.out = nc.dram_tensor("out", [c.NO, c.D], F32, kind="ExternalOutput").ap()
        self.b_out = p.buf("dram_out")

        NW = 47616
        with ExitStack() as es:
            big = es.enter_context(nc.sbuf_tensor("arena", [128, NW], F32))
            self.A = Arena(big, NW)
            self.ps = [es.enter_context(nc.psum_tensor("ps%d" % i, [128, 512], F32)) for i in range(8)]
            self.psb = [p.buf("ps%d" % i) for i in range(8)]
            self.phase0()
            phases = [self.phase1, self.phase2, self.phase3, self.phase4, self.phase5, self.phase6]
            for i, ph in enumerate(phases):
                if i + 1 > self.stop_after:
                    break
                self.A.off = self.A0
                ph()
                self.flush_pe()
                p.barrier()
            if self.stop_after < 6:
                t = self.A.f32(128)
                b = p.buf("dummy")
                self.memset("dve", t, 0.0, [b])
                p.dma("sync", self.out[0:128, 0:128], t, b, self.b_out, store=True)
            p.barrier()
            print("ops:", {e: len(p.ops[e]) for e in p.ENG}, "sems:", len(p.semkeys), flush=True)
            p.emit(nc, es)
        return nc

    def phase0(self):
        c, p, A, d = self.c, self.p, self.A, self.d
        bc = p.buf("const")
        self.b_const = bc
        self.ident = A.f32(128)
        self.onesf = A.f32(128)
        self.onesb = A.bf16(128)
        self.blk = A.bf16(128)
        p.op("pool", "iota", dict(out=self.ident, pattern=[[1, 128]], base=0, channel_multiplier=-1,
                                  allow_small_or_imprecise_dtypes=True), (), [bc])
        p.op("dve", "tensor_single_scalar", dict(out=self.ident, in_=self.ident, scalar=0.0, op=ALU.is_equal), [bc], [bc])
        self.memset("dve", self.onesf, 1.0, [bc])
        self.memset("dve", self.onesb, 1.0, [bc])
        self.memset("dve", self.blk, 0.0, [bc])
        self.memset("dve", self.blk[0:64, 0:64], 1.0, [bc])
        self.memset("dve", self.blk[64:128, 64:128], 1.0, [bc])
        self.s1 = A.f32(1)
        self.s2 = A.f32(1)
        self.negpi = A.f32(1)
        for (ap, top, bot) in ((self.s1, -1.0, 1.0), (self.s2, 1.0, -1.0)):
            self.memset("dve", ap[0:64, :], top, [bc])
            self.memset("dve", ap[64:128, :], bot, [bc])
        self.memset("dve", self.negpi, -PI, [bc])
        self.epsc = A.f32(1)
        self.halfpi = A.f32(1)
        self.memset("dve", self.epsc, EPS, [bc])
        self.memset("dve", self.halfpi, PI / 2, [bc])
        self.g1col = A.f32(c.KD)
        self.g2col = A.f32(c.KD)
        bl = p.buf("cload")
        p.dma("sync", self.g1col, d["ln1_g"].rearrange("(c p) -> p c", p=128), bl, partial=True, allow_slow_non_contiguous=True)
        p.dma("sync", self.g2col, d["ln2_g"].rearrange("(c p) -> p c", p=128), bl, partial=True, allow_slow_non_contiguous=True)
        self.gq2 = A.f32(1)
        self.gk2 = A.f32(1)
        for (ap, nm) in ((self.gq2, "q_norm_g"), (self.gk2, "k_norm_g")):
            src = d[nm].rearrange("(p o) -> p o", o=1)
            p.dma("sync", ap[0:64, :], src, bl, partial=True, allow_slow_non_contiguous=True)
            p.dma("sync", ap[64:128, :], src, bl, partial=True, allow_slow_non_contiguous=True)
        if int(os.environ.get("KCUT", "99")) <= 1:
            self.A0 = A.off
            return
        self.slng = A.f32(1)
        p.dma("sync", self.slng, d["subln_g"].rearrange("(p o) -> p o", o=1), bl, partial=True, allow_slow_non_contiguous=True)
        self.kvalid = A.f32(c.NCTX)
        p.dma("sync", self.kvalid, d["kvalid"], bl, partial=True)
        self.rel31 = A.f32(c.H)
        p.dma("sync", self.rel31, d["rel_bias"][31, :].partition_broadcast(128), bl, partial=True)
        lam4 = A.f32(256)
        for i, nm in enumerate(("lam_q1", "lam_k1", "lam_q2", "lam_k2")):
            p.dma("sync", lam4[:, i * 64:(i + 1) * 64], d[nm].partition_broadcast(128), bl, partial=True)
        relb = A.f32(c.H)
        ohc = A.f32(FL)
        ngm = A.f32(FL)
        p.dma("sync", relb[0:32, :], d["rel_bias"], bl, partial=True)
        p.dma("sync", ohc[0:32, :], d["ohc"], bl, partial=True)
        p.dma("sync", ngm[0:c.H, :], d["negmask"], bl, partial=True)
        if int(os.environ.get("KCUT", "99")) <= 2:
            self.A0 = A.off
            return
        self.ts("dve", self.slng, self.slng, 1.0 - c.lambda_init, None, ALU.mult, None, [bl], [bl])
        lt = A.f32(128)
        l2 = A.f32(2)
        self.neglam = A.f32(1)
        self.tt("dve", lt[:, 0:64], lam4[:, 0:64], lam4[:, 64:128], ALU.mult, [bl], [bc])
        self.tt("dve", lt[:, 64:128], lam4[:, 128:192], lam4[:, 192:256], ALU.mult, [bl], [bc])
        p.op("dve", "tensor_reduce", dict(out=l2[:, 0:1], in_=lt[:, 0:64], axis=AX.X, op=ALU.add), [bc], [bc])
        p.op("dve", "tensor_reduce", dict(out=l2[:, 1:2], in_=lt[:, 64:128], axis=AX.X, op=ALU.add), [bc], [bc])
        self.act(l2, l2, AF.Exp, [bc], [bc])
        self.tt("dve", self.neglam, l2[:, 1:2], l2[:, 0:1], ALU.subtract, [bc], [bc])
        self.ts("dve", self.neglam, self.neglam, -c.lambda_init, None, ALU.add, None, [bc], [bc])
        if int(os.environ.get("KCUT", "99")) <= 3:
            self.A0 = A.off
            return
        frow = A.f32(FL)
        self.mm(self.ps[0][0:c.H, 0:FL], relb[0:32, 0:c.H], ohc[0:32, :], True, True, [bl], [self.psb[0]])
        self.tt("dve", frow[0:c.H, :], self.ps[0][0:c.H, 0:FL], ngm[0:c.H, :], ALU.add, [self.psb[0], bl], [bc])
        if int(os.environ.get("KCUT", "99")) <= 4:
            self.A0 = A.off
            return
        brr = p.buf("rrepst")
        bdr = self.p.buf("dram_rrep")
        src = frow[0:c.H, :].unsqueeze(1).broadcast_to([c.H, 128, FL])
        p.dma("sync", self.s["rrep"], src, bc, bdr, store=True)
        self.TD = A.f32(c.H * 128)
        self.TP = A.f32(c.H * 128)
        bt = p.buf("toep")
        rt = self.s["rrep"].tensor
        p.dma("sync", self.TD.rearrange("p (h q) -> p h q", q=128),
              bass.AP(tensor=rt, offset=127, ap=[[FL - 1, 128], [128 * FL, c.H], [1, 128]]), bt, bdr, partial=True)
        p.dma("sync", self.TP.rearrange("p (h q) -> p h q", q=128),
              bass.AP(tensor=rt, offset=255, ap=[[FL - 1, 128], [128 * FL, c.H], [1, 128]]), bt, bdr, partial=True)
        self.b_toep = bt
        self.b_cload = bl
        self.A0 = A.off
        p.barrier()

    def ctx_supers(self):
        c = self.c
        pre = [(t0, tw, False) for (t0, tw) in tok_tiles(c.NPRE)]
        own = [(c.O0 + t0, tw, True) for (t0, tw) in tok_tiles(c.NOWN)]
        sup = []
        for grp in super_tiles([(a, b) for (a, b, _) in pre], c.SMAX):
            sup.append((grp, False))
        for grp in super_tiles([(a, b) for (a, b, _) in own], c.SMAX):
            sup.append((grp, True))
        return sup

    def own_supers(self):
        c = self.c
        return super_tiles(tok_tiles(c.NOWN), c.SMAX)

    def phase1(self):
        c, p, A, d, s = self.c, self.p, self.A, self.d, self.s
        KD = c.KD
        for (grp, own) in self.ctx_supers():
            A.off = self.A0
            S0 = grp[0][0]
            S = sum(t[1] for t in grp)
            a1T = A.bf16(KD * S).rearrange("p (c t) -> p c t", t=S)
            b_a1 = p.buf("a1T")
            mark = A.off
            xin = [A.f32(c.D) for _ in range(2)]
            b_x = p.bufs(2, "xin")
            junk = A.bf16(c.D)
            b_j = p.buf("junk")
            hst2 = A.f32(c.D)
            hst = hst2.rearrange("p (c t) -> p c t", t=128)
            b_h = p.buf("hst")
            ssq = A.f32(2)
            rs = A.f32(2)
            b_s = p.bufs(2, "ssq")
            diag = [A.f32(128) for _ in range(2)]
            rrep = [A.f32(128) for _ in range(2)]
            b_r = p.bufs(2, "rrep")
            trb = Rot([5, 6, 7])
            for bi in range(S // 128):
                t0 = S0 + bi * 128
                j = bi % 2
                p.dma("pool", xin[j], d["hp"][t0:t0 + 128, :], b_x[j])
                self.act(junk, xin[j], AF.Square, [b_x[j]], [b_j, b_s[j]], accum_out=ssq[:, j:j + 1])
                self.act(rs[:, j:j + 1], ssq[:, j:j + 1], AF.Sqrt, [b_s[j], self.b_const], [b_r[j]], bias=self.epsc[:, 0:1], scale=1.0 / c.D)
                p.op("dve", "reciprocal", dict(out=rs[:, j:j + 1], in_=rs[:, j:j + 1]), [b_r[j]], [b_r[j]])
                self.ts("dve", diag[j], self.ident, rs[:, j:j + 1], None, ALU.mult, None, [b_r[j], self.b_const], [b_r[j]])
                self.mm(self.ps[4][:, 0:128], self.onesf, diag[j], True, True, [b_r[j], self.b_const], [self.psb[4]])
                self.copy("act", rrep[j], self.ps[4][:, 0:128], [self.psb[4]], [b_r[j]])
                for c0 in range(0, KD, 4):
                    n4 = min(4, KD - c0)
                    bk = trb.next()
                    for q in range(n4):
                        self.tr(self.ps[bk][:, q * 128:(q + 1) * 128], xin[j][:, (c0 + q) * 128:(c0 + q + 1) * 128],
                                [b_x[j]], [self.psb[bk]])
                    for q in range(n4):
                        self.stt(a1T[:, c0 + q, bi * 128:(bi + 1) * 128], self.ps[bk][:, q * 128:(q + 1) * 128],
                                 self.g1col[:, c0 + q:c0 + q + 1], rrep[j], ALU.mult, ALU.mult,
                                 [self.psb[bk], b_r[j], self.b_cload], [b_a1])
                    if own and not os.environ.get("KNOC"):
                        self.copy("dve", hst2[:, c0 * 128:(c0 + n4) * 128], self.ps[bk][:, 0:n4 * 128],
                                  [self.psb[bk]], [b_h])
                if own and not os.environ.get("KNOH"):
                    to = t0 - c.O0
                    p.dma("sync", s["hT"][:, to:to + 128].rearrange("(c p) t -> p c t", p=128), hst, b_h, self.sb["hT"], store=True)
            p.barrier()
            if self.cut():
                return
            A.off = mark
            wbuf = [A.bf16(KD * c.MB).rearrange("p (c n) -> p c n", n=c.MB) for _ in range(2)]
            b_w = p.bufs(2, "wbuf")
            NST = 3
            stf = Rot([(A.f32(512), p.buf("stf")) for _ in range(NST)])
            stb = Rot([(A.bf16(512), p.buf("stb")) for _ in range(NST)])
            sqt = Rot([(A.bf16(512), p.buf("sqt")) for _ in range(2)])
            rqt = Rot([(A.f32(512), p.buf("rqt")) for _ in range(2)])
            mainb = Rot([0, 1, 2, 3, 4])
            normb = Rot([5, 6])
            regions = [("u", c.cU, c.DS), ("q", c.cQ, c.QKW), ("k", c.cK, c.QKW), ("v", c.cV, c.DATT),
                       ("gs", c.cGS, c.D), ("ga", c.cGA, c.D)]
            blocks = []
            for (typ, cs, cw) in regions:
                if not own and typ not in ("u", "k", "v"):
                    continue
                if typ not in os.environ.get("KREG", "u,q,k,v,gs,ga").split(","):
                    continue
                for c0 in range(cs, cs + cw, c.MB):
                    blocks.append((typ, cs, c0))
            wi = 0
            for (typ, cs, c0) in blocks:
                wj = wi % 2
                wi += 1
                p.dma("pool", wbuf[wj], d["w_in"][:, c0:c0 + c.MB].rearrange("(c p) n -> p c n", p=128), b_w[wj])
                if typ == "v":
                    for tb in range(S // 128):
                        bk = mainb.next()
                        for kc in range(KD):
                            self.mm(self.ps[bk][:, 0:c.MB], a1T[:, kc, tb * 128:(tb + 1) * 128], wbuf[wj][:, kc, :],
                                    kc == 0, kc == KD - 1, [b_a1, b_w[wj]], [self.psb[bk]])
                        self.flush_pe()
                        (st, bs) = stb.next()
                        self.copy("act", st[:, 0:c.MB], self.ps[bk][:, 0:c.MB], [self.psb[bk]], [bs])
                        t0 = S0 + tb * 128
                        p.dma("sync", s["v"][t0:t0 + 128, c0 - cs:c0 - cs + c.MB], st[:, 0:c.MB], bs, self.sb["v"], store=True)
                    continue
                for sub in range(c.NSUB):
                    m = (c0 - cs) // 128 + sub
                    tl = 0
                    for (t0, tw) in grp:
                        bk = mainb.next()
                        for kc in range(KD):
                            self.mm(self.ps[bk][:, 0:tw], wbuf[wj][:, kc, sub * 128:(sub + 1) * 128], a1T[:, kc, tl:tl + tw],
                                    kc == 0, kc == KD - 1, [b_a1, b_w[wj]], [self.psb[bk]])
                        self.flush_pe()
                        pm = self.ps[bk][:, 0:tw]
                        if typ == "u":
                            (st, bs) = stf.next()
                            self.copy("act", st[:, 0:tw], pm, [self.psb[bk]], [bs])
                            p.dma("sync", s["uT"][m * 128:(m + 1) * 128, t0:t0 + tw], st[:, 0:tw], bs, self.sb["uT"], store=True)
                        elif typ in ("q", "k"):
                            (sq, bsq) = sqt.next()
                            (rq, brq) = rqt.next()
                            (st, bs) = stb.next()
                            nb = normb.next()
                            gcol = self.gq2 if typ == "q" else self.gk2
                            self.act(sq[:, 0:tw], pm, AF.Square, [self.psb[bk]], [bsq])

                            def deferred(nb=nb, sq=sq, bsq=bsq, tw=tw, rq=rq, brq=brq, st=st, bs=bs, pm=pm, bk=bk,
                                         gcol=gcol, typ=typ, m=m, t0=t0):
                                self.mm(self.ps[nb][:, 0:tw], self.blk, sq[:, 0:tw], True, True, [bsq, self.b_const], [self.psb[nb]])
                                self.act(rq[:, 0:tw], self.ps[nb][:, 0:tw], AF.Sqrt, [self.psb[nb], self.b_const], [brq],
                                         bias=self.epsc[:, 0:1], scale=1.0 / 64)
                                p.op("dve", "reciprocal", dict(out=rq[:, 0:tw], in_=rq[:, 0:tw]), [brq], [brq])
                                self.stt(st[:, 0:tw], pm, gcol[:, 0:1], rq[:, 0:tw], ALU.mult, ALU.mult,
                                         [self.psb[bk], brq, self.b_cload], [bs])
                                if typ == "q":
                                    to = t0 - c.O0
                                    p.dma("sync", s["qT"][m, :, to:to + tw], st[:, 0:tw], bs, self.sb["qT"], store=True)
                                else:
                                    p.dma("sync", s["kT"][m, :, t0:t0 + tw], st[:, 0:tw], bs, self.sb["kT"], store=True)
                            self.pe_def.append(deferred)
                        else:
                            (st, bs) = stb.next()
                            self.act(st[:, 0:tw], pm, AF.Sigmoid, [self.psb[bk]], [bs])
                            to = t0 - c.O0
                            dst = s["sgs"] if typ == "gs" else s["sga"]
                            p.dma("sync", dst[m * 128:(m + 1) * 128, to:to + tw], st[:, 0:tw], bs,
                                  self.sb["sgs" if typ == "gs" else "sga"], store=True)
                        tl += tw
            self.flush_pe()
            p.barrier()
            if self.cut():
                return

    def rsqrt_ps(self, out, in_, scale, Rb, Wb):
        self.act(out, in_, AF.Sqrt, list(Rb) + [self.b_const], [Wb], bias=self.epsc[:, 0:1], scale=scale)
        self.p.op("dve", "reciprocal", dict(out=out, in_=out), [Wb], [Wb])

    def reduce_angle(self, eng, out, a, ni, Ra, Wn, Wo):
        self.ts(eng, ni, a, 1.0 / TWO_PI, None, ALU.mult, None, Ra, [Wn])
        self.copy(eng, out, ni, [Wn], [Wo])
        self.stt(out, out, -TWO_PI, a, ALU.mult, ALU.add, list(Ra) + [Wo], [Wo])
        self.ts(eng, out, out, -PI, PI, ALU.max, ALU.min, [Wo], [Wo])

    def phase2(self):
        c, p, A, d, s = self.c, self.p, self.A, self.d, self.s
        G, KS = c.G, c.KS
        I32 = mybir.dt.int32
        bs = p.buf("ssm_setup")
        bl = p.buf("ssm_load")
        ZB = A.f32(G * 16)
        ZsB = A.f32(G * 16)
        rr = A.f32(G)
        thr = A.f32(G)
        A0thrT = A.f32(G)
        dsk = A.f32(KS)
        bgl = A.f32(KS)
        maskc = A.f32(8)
        iota = A.f32(512)
        chunks = [(t0, tw, False) for (t0, tw) in tok_tiles(c.NPRE)] + [(c.O0 + t0, tw, True) for (t0, tw) in tok_tiles(c.NOWN)]
        NCH = len(chunks)
        offs = A.f32(NCH * G)
        gT = A.bf16(KS * c.NO).rearrange("p (c t) -> p c t", t=c.NO)
        b_gT = p.buf("gT")
        Blz = A.bf16(8 * 128).rearrange("p (g m) -> p g m", m=128)
        Blzs = A.bf16(8 * 128).rearrange("p (g m) -> p g m", m=128)
        Cl1 = A.bf16(8 * 128).rearrange("p (g m) -> p g m", m=128)
        Cl2 = A.bf16(8 * 128).rearrange("p (g m) -> p g m", m=128)
        mark = A.off
        lre = A.f32(G)
        lim = A.f32(G)
        dtr = A.f32(G)
        for (ap, nm) in ((lre, "lam_re"), (lim, "lam_im")):
            src = d[nm].rearrange("g p -> p g")
            p.dma("sync", ap[0:64, :], src, bl, partial=True, allow_slow_non_contiguous=True)
            p.dma("sync", ap[64:128, :], src, bl, partial=True, allow_slow_non_contiguous=True)
        p.dma("sync", dtr, d["log_dt"].partition_broadcast(128), bl, partial=True)
        p.dma("sync", dsk, d["d_skip"].rearrange("(k p) -> p k", p=128), bl, partial=True, allow_slow_non_contiguous=True)
        p.dma("sync", bgl, d["b_glu"].rearrange("(k p) -> p k", p=128), bl, partial=True, allow_slow_non_contiguous=True)
        X1 = A.f32(G * 16)
        X2 = A.f32(G * 16)
        X13 = X1.rearrange("p (g c) -> p g c", c=16)
        X23 = X2.rearrange("p (g c) -> p g c", c=16)
        bre = d["b_re"].rearrange("g p c -> p g c")
        bim = d["b_im"].rearrange("g p c -> p g c")
        p.dma("sync", X13[0:64], bre, bl, partial=True)
        p.dma("sync", X13[64:128], bim, bl, partial=True)
        p.dma("sync", X23[0:64], bim, bl, partial=True)
        p.dma("sync", X23[64:128], bre, bl, partial=True)
        if os.environ.get("KS2") == "1":
            p.barrier()
            return
        self.act(dtr, dtr, AF.Exp, [bl], [bs])
        tmp = A.f32(G)
        self.tt("dve", tmp, lre, dtr, ALU.mult, [bl, bs], [bs])
        self.act(rr, tmp, AF.Exp, [bs], [bs])
        th = A.f32(G)
        self.tt("dve", th, lim, dtr, ALU.mult, [bl, bs], [bs])
        if os.environ.get("KS2") == "2":
            p.barrier()
            return
        ni = A.f32(G).bitcast(I32)
        self.reduce_angle("dve", thr, th, ni, [bs], bs, bs)
        if os.environ.get("KS2") == "3":
            p.barrier()
            return
        sn = A.f32(G)
        cs = A.f32(G)
        ab = A.f32(G)
        self.act(sn, thr, AF.Sin, [bs], [bs])
        self.ts("dve", ab, thr, -1.0, None, ALU.mult, None, [bs], [bs])
        self.tt("dve", ab, ab, thr, ALU.max, [bs], [bs])
        self.act(cs, ab, AF.Sin, [bs, self.b_const], [bs], bias=self.halfpi[:, 0:1], scale=-1.0)
        ar1 = A.f32(G)
        ai = A.f32(G)
        self.tt("dve", ar1, rr, cs, ALU.mult, [bs], [bs])
        self.ts("dve", ar1, ar1, -1.0, None, ALU.add, None, [bs], [bs])
        self.tt("dve", ai, rr, sn, ALU.mult, [bs], [bs])
        if os.environ.get("KS2") == "4":
            p.barrier()
            return
        den = A.f32(G)
        t2 = A.f32(G)
        self.tt("dve", den, lre, lre, ALU.mult, [bl], [bs])
        self.tt("dve", t2, lim, lim, ALU.mult, [bl], [bs])
        self.tt("dve", den, den, t2, ALU.add, [bs], [bs])
        p.op("dve", "reciprocal", dict(out=den, in_=den), [bs], [bs])
        cre = A.f32(G)
        cim = A.f32(G)
        self.tt("dve", cre, ar1, lre, ALU.mult, [bs, bl], [bs])
        self.tt("dve", t2, ai, lim, ALU.mult, [bs, bl], [bs])
        self.tt("dve", cre, cre, t2, ALU.add, [bs], [bs])
        self.tt("dve", cre, cre, den, ALU.mult, [bs], [bs])
        self.tt("dve", cim, ai, lre, ALU.mult, [bs, bl], [bs])
        self.tt("dve", t2, ar1, lim, ALU.mult, [bs, bl], [bs])
        self.tt("dve", cim, cim, t2, ALU.subtract, [bs], [bs])
        self.tt("dve", cim, cim, den, ALU.mult, [bs], [bs])
        if os.environ.get("KS2") == "5":
            p.barrier()
            return
        s1c = A.f32(G)
        s2c = A.f32(G)
        self.ts("dve", s1c, cim, self.s1[:, 0:1], None, ALU.mult, None, [bs, self.b_const], [bs])
        self.ts("dve", s2c, cre, self.s2[:, 0:1], None, ALU.mult, None, [bs, self.b_const], [bs])
        T1 = A.f32(G * 16)
        T13 = T1.rearrange("p (g c) -> p g c", c=16)
        bc3 = lambda ap: ap.unsqueeze(2).broadcast_to([128, G, 16])
        ZB3 = ZB.rearrange("p (g c) -> p g c", c=16)
        ZsB3 = ZsB.rearrange("p (g c) -> p g c", c=16)
        for cc in range(16):
            self.tt("dve", ZB3[:, :, cc], X13[:, :, cc], cre, ALU.mult, [bs, bl], [bs])
            self.tt("dve", T13[:, :, cc], X23[:, :, cc], s1c, ALU.mult, [bs, bl], [bs])
        self.tt("dve", ZB, ZB, T1, ALU.add, [bs], [bs])
        for cc in range(16):
            self.tt("dve", ZsB3[:, :, cc], X23[:, :, cc], s2c, ALU.mult, [bs, bl], [bs])
            self.tt("dve", T13[:, :, cc], X13[:, :, cc], cim, ALU.mult, [bs, bl], [bs])
        self.tt("dve", ZsB, ZsB, T1, ALU.add, [bs], [bs])
        if os.environ.get("KS2") == "6":
            p.barrier()
            return
        for ci, (t0, tw, own) in enumerate(chunks):
            o = offs[:, ci * G:(ci + 1) * G]
            self.ts("dve", tmp, thr, float(t0), None, ALU.mult, None, [bs], [bs])
            self.reduce_angle("dve", o, tmp, ni, [bs], bs, bs)
        thrT = A0thrT
        self.ts("dve", thrT, thr, 1.0 / TWO_PI, None, ALU.mult, None, [bs], [bs])
        self.ts("dve", offs, offs, 1.0 / TWO_PI, None, ALU.mult, None, [bs], [bs])
        p.op("pool", "iota", dict(out=maskc, pattern=[[16, 8]], base=0, channel_multiplier=-1,
                                  allow_small_or_imprecise_dtypes=True), (), [bs])
        mk2 = A.f32(8)
        self.ts("dve", mk2, maskc, 0.0, None, ALU.is_le, None, [bs], [bs])
        self.ts("dve", maskc, maskc, -16.0, None, ALU.is_gt, None, [bs], [bs])
        self.tt("dve", maskc, maskc, mk2, ALU.mult, [bs], [bs])
        p.op("pool", "iota", dict(out=iota, pattern=[[1, 512]], base=0, channel_multiplier=0,
                                  allow_small_or_imprecise_dtypes=True), (), [bs])
        for t in (Cl1, Cl2):
            self.memset("dve", t, 0.0, [bs])
        p.barrier()
        if self.cut():
            return
        A.off = mark
        uTk = A.bf16(c.NT)
        uTf = A.f32(c.NO)
        b_uk = p.buf("uTk")
        b_uf = p.buf("uTf")
        CC = A.f32(128)
        CC2 = A.f32(128)
        b_cc = p.buf("CC")
        b_lhs = p.buf("lhs")
        R2 = lambda n, f, nm: Rot([(f(n), p.buf(nm)) for _ in range(2)])
        r_a = R2(512, A.f32, "a")
        r_ni = Rot([(A.f32(512).bitcast(I32), p.buf("ni")) for _ in range(2)])
        r_r = R2(512, A.f32, "r")
        r_ab = Rot([(A.f32(512), p.buf("ab")) for _ in range(4)])
        r_S = Rot([(A.f32(512), p.buf("S2")) for _ in range(5)])
        r_C = Rot([(A.f32(512), p.buf("C2")) for _ in range(4)])
        r_f = Rot([(A.f32(512), p.buf("f")) for _ in range(3)])
        r_W1 = R2(512, A.f32, "W1")
        r_W2 = R2(512, A.f32, "W2")
        r_W = Rot([(A.f32(512), p.buf("W")) for _ in range(3)])
        r_w = R2(512, A.f32, "w")
        r_V1 = R2(512, A.bf16, "V1")
        r_V2 = R2(512, A.bf16, "V2")
        carry = A.f32(8)
        b_carry = p.bufs(8, "carry")
        r_yv = R2(512, A.f32, "yv")
        r_x2 = R2(512, A.f32, "x2")
        r_sg = R2(512, A.f32, "sg")
        zb = Rot([0, 1])
        zsb = Rot([2, 3])
        yb = Rot([4, 5])
        for k in range(KS):
            rows = slice(k * 128, (k + 1) * 128)
            p.dma("pool", uTk, s["uT"][rows, :], b_uk, self.sb["uT"])
            p.dma("sync", uTf, s["uT"][rows, c.O0:], b_uf, self.sb["uT"])
            p.dma("sync", CC[:, 0:64], d["c_re"][rows, :], b_cc, partial=True)
            p.dma("sync", CC[:, 64:128], d["c_im"][rows, :], b_cc, partial=True)
            p.dma("sync", CC2[:, 0:64], d["c_im"][rows, :], b_cc, partial=True)
            p.dma("sync", CC2[:, 64:128], d["c_re"][rows, :], b_cc, partial=True)
            self.tr(self.ps[6][:, 0:128], CC, [b_cc], [self.psb[6]])
            self.tr(self.ps[6][:, 128:256], CC2, [b_cc], [self.psb[6]])
            self.tr(self.ps[7][:, 0:128], ZB[:, k * 128:(k + 1) * 128], [bs], [self.psb[7]])
            self.tr(self.ps[7][:, 128:256], ZsB[:, k * 128:(k + 1) * 128], [bs], [self.psb[7]])
            for gi in range(8):
                cs_ = slice(gi * 16, (gi + 1) * 16)
                self.ts("dve", Cl1[:, gi, cs_], self.ps[6][:, gi * 16:(gi + 1) * 16], self.s2[:, 0:1], None, ALU.mult, None,
                        [self.psb[6], self.b_const], [b_lhs])
                self.ts("dve", Cl2[:, gi, cs_], self.ps[6][:, 128 + gi * 16:128 + (gi + 1) * 16], -1.0, None, ALU.mult, None,
                        [self.psb[6]], [b_lhs])
                self.ts("dve", Blz[:, gi, :], self.ps[7][:, 0:128], maskc[:, gi:gi + 1], None, ALU.mult, None,
                        [self.psb[7], bs], [b_lhs])
                self.ts("dve", Blzs[:, gi, :], self.ps[7][:, 128:256], maskc[:, gi:gi + 1], None, ALU.mult, None,
                        [self.psb[7], bs], [b_lhs])
            iters = []
            for ci, (t0, tw, own) in enumerate(chunks):
                for gi in range(8):
                    iters.append((ci, t0, tw, own, gi))
            NI = len(iters)
            ybk_of = {}
            st = [dict() for _ in range(NI)]
            MAGIC = 12582912.0

            def a1(i):
                (ci, t0, tw, own, gi) = iters[i]
                g = k * 8 + gi
                (a, b_a) = r_a.next()
                (nn, b_n) = r_r.next()
                (f, b_f) = r_f.next()
                (ab_, b_ab) = r_ab.next()
                (S2, b_S) = r_S.next()
                self.ts("dve", a[:, 0:tw], iota[:, 0:tw], thrT[:, g:g + 1], offs[:, ci * G + g:ci * G + g + 1],
                        ALU.mult, ALU.add, [bs], [b_a])
                self.ts("dve", nn[:, 0:tw], a[:, 0:tw], MAGIC, None, ALU.add, None, [b_a], [b_n])
                self.ts("dve", nn[:, 0:tw], nn[:, 0:tw], MAGIC, None, ALU.subtract, None, [b_n], [b_n])
                self.tt("dve", f[:, 0:tw], a[:, 0:tw], nn[:, 0:tw], ALU.subtract, [b_a, b_n], [b_f])
                self.act(S2[:, 0:tw], f[:, 0:tw], AF.Sin, [b_f], [b_S], scale=TWO_PI)
                self.act(ab_[:, 0:tw], f[:, 0:tw], AF.Sin, [b_f], [b_ab], scale=PI)
                self.act(ab_[:, 0:tw], ab_[:, 0:tw], AF.Square, [b_ab], [b_ab])
                st[i].update(S2=S2, b_S=b_S, ab=ab_, b_ab=b_ab)

            def zmm(i):
                (ci, t0, tw, own, gi) = iters[i]
                z1, z2 = zb.next(), zsb.next()
                self.mm(self.ps[z1][:, 0:tw], Blz[:, gi, :], uTk[:, t0:t0 + tw], True, True, [b_lhs, b_uk], [self.psb[z1]])
                self.mm(self.ps[z2][:, 0:tw], Blzs[:, gi, :], uTk[:, t0:t0 + tw], True, True, [b_lhs, b_uk], [self.psb[z2]])
                st[i].update(z1=z1, z2=z2)

            def c2(i):
                (ci, t0, tw, own, gi) = iters[i]
                (C2, b_C) = r_C.next()
                self.ts("dve", C2[:, 0:tw], st[i]["ab"][:, 0:tw], -2.0, 1.0, ALU.mult, ALU.add, [st[i]["b_ab"]], [b_C])
                st[i].update(C2=C2, b_C=b_C)

            def wstage(i):
                (ci, t0, tw, own, gi) = iters[i]
                d_ = st[i]
                (W1, b_W1) = r_W1.next()
                (W2, b_W2) = r_W2.next()
                (W, b_W) = r_W.next()
                self.tt("dve", W1[:, 0:tw], self.ps[d_["z1"]][:, 0:tw], d_["C2"][:, 0:tw], ALU.mult, [self.psb[d_["z1"]], d_["b_C"]], [b_W1])
                self.tt("dve", W2[:, 0:tw], self.ps[d_["z2"]][:, 0:tw], d_["S2"][:, 0:tw], ALU.mult, [self.psb[d_["z2"]], d_["b_S"]], [b_W2])
                self.tt("pool", W[:, 0:tw], W1[:, 0:tw], W2[:, 0:tw], ALU.add, [b_W1, b_W2], [b_W])
                d_.update(W=W, b_W=b_W)

            def scanstage(i):
                (ci, t0, tw, own, gi) = iters[i]
                d_ = st[i]
                g = k * 8 + gi
                S2, b_S, C2, b_C, W, b_W = d_["S2"], d_["b_S"], d_["C2"], d_["b_C"], d_["W"], d_["b_W"]
                if own and gi == 0:
                    ybk_of[ci] = yb.next()
                ybk = ybk_of.get(ci)
                (w, b_w) = r_w.next()
                init = 0.0 if ci == 0 else carry[:, gi:gi + 1]
                Rl = [b_W, bs] + ([] if ci == 0 else [b_carry[gi]])
                p.op("dve", "tensor_tensor_scan",
                     dict(out=w[:, 0:tw], data0=rr[:, g:g + 1].broadcast_to([128, tw]), data1=W[:, 0:tw], initial=init,
                          op0=ALU.mult, op1=ALU.add), Rl, [b_w])
                if ci < NCH - 1:
                    self.copy("act", carry[:, gi:gi + 1], w[:, tw - 1:tw], [b_w], [b_carry[gi]])
                if own:
                    (V1, b_V1) = r_V1.next()
                    (V2, b_V2) = r_V2.next()
                    self.tt("pool", V1[:, 0:tw], C2[:, 0:tw], w[:, 0:tw], ALU.mult, [b_C, b_w], [b_V1])
                    self.tt("pool", V2[:, 0:tw], S2[:, 0:tw], w[:, 0:tw], ALU.mult, [b_S, b_w], [b_V2])
                    self.mm(self.ps[ybk][:, 0:tw], Cl1[:, gi, :], V1[:, 0:tw], gi == 0, False, [b_lhs, b_V1], [self.psb[ybk]])
                    self.mm(self.ps[ybk][:, 0:tw], Cl2[:, gi, :], V2[:, 0:tw], False, gi == 7, [b_lhs, b_V2], [self.psb[ybk]])
                if own and gi == 7:
                    to = t0 - c.O0
                    (yv, b_yv) = r_yv.next()
                    (x2, b_x2) = r_x2.next()
                    (sg, b_sg) = r_sg.next()
                    self.stt(yv[:, 0:tw], uTf[:, to:to + tw], dsk[:, k:k + 1], self.ps[ybk][:, 0:tw], ALU.mult, ALU.add,
                             [b_uf, bl, self.psb[ybk]], [b_yv])
                    self.act(x2[:, 0:tw], yv[:, 0:tw], AF.Square, [b_yv], [b_x2])
                    self.ts("dve", x2[:, 0:tw], x2[:, 0:tw], 0.044715, 1.0, ALU.mult, ALU.add, [b_x2], [b_x2])
                    self.tt("pool", x2[:, 0:tw], x2[:, 0:tw], yv[:, 0:tw], ALU.mult, [b_x2, b_yv], [b_x2])
                    self.act(sg[:, 0:tw], x2[:, 0:tw], AF.Sigmoid, [b_x2], [b_sg], scale=1.5957691216057308)
                    self.tt("pool", gT[:, k, to:to + tw], yv[:, 0:tw], sg[:, 0:tw], ALU.mult, [b_yv, b_sg], [b_gT])
                st[i].clear()

            a1(0)
            if NI > 1:
                a1(1)
            zmm(0)
            c2(0)
            for i in range(NI + 1):
                if i + 2 < NI:
                    a1(i + 2)
                if i + 1 < NI:
                    zmm(i + 1)
                if i < NI:
                    wstage(i)
                if i + 1 < NI:
                    c2(i + 1)
                if 0 <= i - 1 < NI:
                    scanstage(i - 1)
        p.barrier()
        A.off = mark
        wgl = A.bf16(KS * c.DS).rearrange("p (c n) -> p c n", n=c.DS)
        b_wg = p.buf("wglu")
        p.dma("pool", wgl, d["w_glu"].rearrange("(c p) n -> p c n", p=128), b_wg)
        r_sig = Rot([(A.f32(512), p.buf("sig")) for _ in range(2)])
        r_st = Rot([(A.bf16(512), p.buf("yst")) for _ in range(3)])
        mb = Rot([0, 1, 2, 3])
        for m in range(KS):
            for (to, tw) in tok_tiles(c.NOWN):
                bk = mb.next()
                for kc in range(KS):
                    self.mm(self.ps[bk][:, 0:tw], wgl[:, kc, m * 128:(m + 1) * 128], gT[:, kc, to:to + tw], kc == 0, kc == KS - 1,
                            [b_wg, b_gT], [self.psb[bk]])
                (sg, b_sg) = r_sig.next()
                (st, b_st) = r_st.next()
                self.act(sg[:, 0:tw], self.ps[bk][:, 0:tw], AF.Sigmoid, [self.psb[bk], bl], [b_sg], bias=bgl[:, m:m + 1])
                self.tt("pool", st[:, 0:tw], sg[:, 0:tw], gT[:, m, to:to + tw], ALU.mult, [b_sg, b_gT], [b_st])
                p.dma("sync", s["yssmT"][m * 128:(m + 1) * 128, to:to + tw], st[:, 0:tw], b_st, self.sb["yssmT"], store=True)

    def phase3(self):
        c, p, A, d, s = self.c, self.p, self.A, self.d, self.s
        H = c.H
        kTh = [A.bf16(c.NT) for _ in range(2)]
        qTh = [A.bf16(c.NO) for _ in range(2)]
        Vh = [A.bf16(c.NCTX * 128).rearrange("p (b e) -> p b e", e=128) for _ in range(2)]
        cbh = [A.f32(c.NCTX) for _ in range(2)]
        b_hd = p.bufs(2, "head")
        b_cb = p.bufs(2, "cbh")
        r_pt = Rot([(A.bf16(512), p.buf("pt")) for _ in range(4)])
        r_tmp = Rot([(A.f32(128), p.buf("tmp")) for _ in range(4)])
        r_rs = Rot([(A.f32(512), p.buf("rs")) for _ in range(2)])
        r_o = Rot([(A.f32(512), p.buf("o")) for _ in range(2)])
        r_o2 = Rot([(A.f32(512), p.buf("o2")) for _ in range(2)])
        r_sq = Rot([(A.bf16(512), p.buf("sq")) for _ in range(2)])
        r_rn = Rot([(A.f32(512), p.buf("rn")) for _ in range(2)])
        r_st = Rot([(A.bf16(512), p.buf("st")) for _ in range(2)])
        sbank = Rot([0, 1, 2, 3])
        TD3 = self.TD.rearrange("p (h q) -> p h q", q=128)
        TP3 = self.TP.rearrange("p (h q) -> p h q", q=128)
        for h in range(H):
            j = h % 2
            p.dma("sync", kTh[j], s["kT"][h], b_hd[j], self.sb["kT"], partial=False)
            p.dma("sync", qTh[j], s["qT"][h], b_hd[j], self.sb["qT"], partial=True)
            p.dma("sync", Vh[j], s["v"][:, h * 128:(h + 1) * 128].rearrange("(b p) e -> p b e", p=128), b_hd[j], self.sb["v"], partial=True)
            self.ts("dve", cbh[j], self.kvalid, self.rel31[:, h:h + 1], None, ALU.add, None, [self.b_cload], [b_cb[j]])
            for (to, tw) in tok_tiles(c.NOWN):
                nqb = tw // 128
                qb0 = c.NPRE + to // 128
                nkb = qb0 + nqb
                obk = [4, 5]
                smk = [6, 7]
                for kb in range(nkb):
                    col0 = max(0, kb - qb0) * 128
                    ncols = tw - col0
                    pts = []
                    for m in range(2):
                        sb_ = sbank.next()
                        pr = slice(m * 64, (m + 1) * 64)
                        self.mm(self.ps[sb_][:, 0:ncols], kTh[j][pr, kb * 128:(kb + 1) * 128], qTh[j][pr, to + col0:to + tw],
                                True, True, [b_hd[j]], [self.psb[sb_]])
                        (pt, b_pt) = r_pt.next()
                        near = []
                        qbs = max(qb0, kb)
                        ci = 0
                        for i in range(ncols // 128):
                            qb = qbs + i
                            if qb - kb <= 1:
                                near.append((i, qb - kb))
                            else:
                                break
                        far0 = len(near) * 128
                        lastb = None
                        for (i, dd) in near:
                            (tmp, b_tmp) = r_tmp.next()
                            T3 = TD3 if dd == 0 else TP3
                            self.stt(tmp, self.ps[sb_][:, i * 128:(i + 1) * 128], 0.125, T3[:, h, :], ALU.mult, ALU.add,
                                     [self.psb[sb_], self.b_toep], [b_tmp])
                            self.act(pt[:, i * 128:(i + 1) * 128], tmp, AF.Exp, [b_tmp, self.b_cload], [b_pt],
                                     bias=self.kvalid[:, kb:kb + 1])
                            lastb = b_tmp
                        if far0 < ncols:
                            Rl = [self.psb[sb_], b_cb[j]] + ([lastb] if lastb is not None else [])
                            self.act(pt[:, far0:ncols], self.ps[sb_][:, far0:ncols], AF.Exp, Rl, [b_pt],
                                     bias=cbh[j][:, kb:kb + 1], scale=0.125)
                        pts.append((pt, b_pt))
                    self.flush_pe()

                    def pv(pts=pts, kb=kb, col0=col0, ncols=ncols, tw=tw, j=j, nkb=nkb, obk=obk, smk=smk):
                        for m in range(2):
                            (pt, b_pt) = pts[m]
                            self.mm(self.ps[obk[m]][:, col0:tw], Vh[j][:, kb, :], pt[:, 0:ncols], kb == 0, kb == nkb - 1,
                                    [b_hd[j], b_pt], [self.psb[obk[m]]])
                            self.mm(self.ps[smk[m]][:, col0:tw], self.onesb, pt[:, 0:ncols], kb == 0, kb == nkb - 1,
                                    [self.b_const, b_pt], [self.psb[smk[m]]])
                    self.pe_def.append(pv)
                self.flush_pe()
                rsl = []
                for m in range(2):
                    (rs, b_rs) = r_rs.next()
                    self.ts("dve", rs[:, 0:tw], self.ps[smk[m]][:, 0:tw], 1e-30, None, ALU.max, None, [self.psb[smk[m]]], [b_rs])
                    p.op("dve", "reciprocal", dict(out=rs[:, 0:tw], in_=rs[:, 0:tw]), [b_rs], [b_rs])
                    rsl.append((rs, b_rs))
                (o1, b_o1) = r_o.next()
                (o2, b_o2) = r_o2.next()
                self.tt("dve", o1[:, 0:tw], self.ps[obk[0]][:, 0:tw], rsl[0][0][:, 0:tw], ALU.mult, [self.psb[obk[0]], rsl[0][1]], [b_o1])
                self.tt("dve", o2[:, 0:tw], self.ps[obk[1]][:, 0:tw], rsl[1][0][:, 0:tw], ALU.mult, [self.psb[obk[1]], rsl[1][1]], [b_o2])
                self.stt(o1[:, 0:tw], o2[:, 0:tw], self.neglam[:, 0:1], o1[:, 0:tw], ALU.mult, ALU.add, [b_o2, self.b_const], [b_o1])
                (sq, b_sq) = r_sq.next()
                (rn, b_rn) = r_rn.next()
                (st, b_st) = r_st.next()
                self.act(sq[:, 0:tw], o1[:, 0:tw], AF.Square, [b_o1], [b_sq])
                nb = sbank.next()
                self.mm(self.ps[nb][:, 0:tw], self.onesb, sq[:, 0:tw], True, True, [self.b_const, b_sq], [self.psb[nb]])
                self.rsqrt_ps(rn[:, 0:tw], self.ps[nb][:, 0:tw], 1.0 / 128, [self.psb[nb]], b_rn)
                self.stt(st[:, 0:tw], o1[:, 0:tw], self.slng[:, 0:1], rn[:, 0:tw], ALU.mult, ALU.mult, [b_o1, b_rn, self.b_cload], [b_st])
                p.dma("sync", s["yattT"][h * 128:(h + 1) * 128, to:to + tw], st[:, 0:tw], b_st, self.sb["yattT"], store=True)

    def phase4(self):
        c, p, A, d, s = self.c, self.p, self.A, self.d, self.s
        KD, KS = c.KD, c.KS
        KA = c.DATT // 128
        KB = KS + KA
        base = A.off
        for grp in self.own_supers():
            A.off = base
            S0 = grp[0][0]
            S = sum(t[1] for t in grp)
            mg = A.bf16(KD * S).rearrange("p (c t) -> p c t", t=S)
            b_mg = p.buf("merged")
            mark = A.off
            ybT = A.bf16(KB * S).rearrange("p (c t) -> p c t", t=S)
            b_yb = p.buf("ybT")
            p.dma("sync", ybT[:, 0:KS, :], s["yssmT"][:, S0:S0 + S].rearrange("(c p) t -> p c t", p=128), b_yb, self.sb["yssmT"], partial=True)
            p.dma("sync", ybT[:, KS:KB, :], s["yattT"][:, S0:S0 + S].rearrange("(c p) t -> p c t", p=128), b_yb, self.sb["yattT"], partial=True)
            wbr = [A.bf16(KB * 128).rearrange("p (c n) -> p c n", n=128) for _ in range(2)]
            b_w = p.bufs(2, "wbr")
            r_g1 = Rot([(A.bf16(512), p.buf("g1")) for _ in range(3)])
            r_g2 = Rot([(A.bf16(512), p.buf("g2")) for _ in range(3)])
            r_t1 = Rot([(A.f32(512), p.buf("t1")) for _ in range(2)])
            r_t2 = Rot([(A.f32(512), p.buf("t2")) for _ in range(2)])
            ba = Rot([0, 1, 2])
            bb = Rot([3, 4, 5])
            for m in range(KD):
                wj = m % 2
                p.dma("pool", wbr[wj], d["w_branch"][:, m * 128:(m + 1) * 128].rearrange("(c p) n -> p c n", p=128), b_w[wj])
                tl = 0
                for (to, tw) in grp:
                    ka, kb_ = ba.next(), bb.next()
                    for kc in range(KS):
                        self.mm(self.ps[ka][:, 0:tw], wbr[wj][:, kc, :], ybT[:, kc, tl:tl + tw], kc == 0, kc == KS - 1,
                                [b_w[wj], b_yb], [self.psb[ka]])
                    for kc in range(KA):
                        self.mm(self.ps[kb_][:, 0:tw], wbr[wj][:, KS + kc, :], ybT[:, KS + kc, tl:tl + tw], kc == 0, kc == KA - 1,
                                [b_w[wj], b_yb], [self.psb[kb_]])
                    (g1, b_g1) = r_g1.next()
                    (g2, b_g2) = r_g2.next()
                    (t1, b_t1) = r_t1.next()
                    (t2, b_t2) = r_t2.next()
                    p.dma("sync", g1[:, 0:tw], s["sgs"][m * 128:(m + 1) * 128, to:to + tw], b_g1, self.sb["sgs"])
                    p.dma("sync", g2[:, 0:tw], s["sga"][m * 128:(m + 1) * 128, to:to + tw], b_g2, self.sb["sga"])
                    self.tt("dve", t1[:, 0:tw], self.ps[ka][:, 0:tw], g1[:, 0:tw], ALU.mult, [self.psb[ka], b_g1], [b_t1])
                    self.tt("dve", t2[:, 0:tw], self.ps[kb_][:, 0:tw], g2[:, 0:tw], ALU.mult, [self.psb[kb_], b_g2], [b_t2])
                    self.tt("dve", mg[:, m, tl:tl + tw], t1[:, 0:tw], t2[:, 0:tw], ALU.add, [b_t1, b_t2], [b_mg])
                    tl += tw
            p.barrier()
            A.off = mark
            wo = [A.bf16(KD * c.MB).rearrange("p (c n) -> p c n", n=c.MB) for _ in range(2)]
            b_wo = p.bufs(2, "wo")
            acc = A.f32(S)
            b_acc = p.buf("acc")
            self.memset("dve", acc, 0.0, [b_acc])
            r_hb = Rot([(A.f32(512), p.buf("hb")) for _ in range(3)])
            r_h2 = Rot([(A.f32(512), p.buf("h2")) for _ in range(3)])
            r_a2 = Rot([(A.bf16(512), p.buf("a2")) for _ in range(3)])
            r_sq = Rot([(A.f32(512), p.buf("sq")) for _ in range(2)])
            mb = Rot([0, 1, 2, 3, 4])
            wi = 0
            for c0 in range(0, c.D, c.MB):
                wj = wi % 2
                wi += 1
                p.dma("pool", wo[wj], d["w_o"][:, c0:c0 + c.MB].rearrange("(c p) n -> p c n", p=128), b_wo[wj])
                for sub in range(c.NSUB):
                    m = c0 // 128 + sub
                    tl = 0
                    for (to, tw) in grp:
                        bk = mb.next()
                        for kc in range(KD):
                            self.mm(self.ps[bk][:, 0:tw], wo[wj][:, kc, sub * 128:(sub + 1) * 128], mg[:, kc, tl:tl + tw],
                                    kc == 0, kc == KD - 1, [b_wo[wj], b_mg], [self.psb[bk]])
                        (hb, b_hb) = r_hb.next()
                        (h2, b_h2) = r_h2.next()
                        (a2, b_a2) = r_a2.next()
                        (sq, b_sq) = r_sq.next()
                        p.dma("sync", hb[:, 0:tw], s["hT"][m * 128:(m + 1) * 128, to:to + tw], b_hb, self.sb["hT"])
                        self.tt("dve", h2[:, 0:tw], self.ps[bk][:, 0:tw], hb[:, 0:tw], ALU.add, [self.psb[bk], b_hb], [b_h2])
                        p.dma("sync", s["h2T"][m * 128:(m + 1) * 128, to:to + tw], h2[:, 0:tw], b_h2, self.sb["h2T"], store=True)
                        self.act(a2[:, 0:tw], h2[:, 0:tw], AF.Copy, [b_h2, self.b_cload], [b_a2], scale=self.g2col[:, m:m + 1])
                        p.dma("sync", s["a2T"][m * 128:(m + 1) * 128, to:to + tw], a2[:, 0:tw], b_a2, self.sb["a2T"], store=True)
                        self.act(sq[:, 0:tw], h2[:, 0:tw], AF.Square, [b_h2], [b_sq])
                        self.tt("dve", acc[:, tl:tl + tw], acc[:, tl:tl + tw], sq[:, 0:tw], ALU.add, [b_sq], [b_acc])
                        tl += tw
            tl = 0
            for (to, tw) in grp:
                bk = mb.next()
                (h2, b_h2) = r_h2.next()
                self.mm(self.ps[bk][:, 0:tw], self.onesf, acc[:, tl:tl + tw], True, True, [self.b_const, b_acc], [self.psb[bk]])
                self.rsqrt_ps(h2[:, 0:tw], self.ps[bk][:, 0:tw], 1.0 / c.D, [self.psb[bk]], b_h2)
                p.dma("sync", s["rstd2"][:, to:to + tw], h2[:, 0:tw], b_h2, self.sb["rstd2"], store=True)
                tl += tw
            p.barrier()

    def phase5(self):
        c, p, A, d, s = self.c, self.p, self.A, self.d, self.s
        KD = c.KD
        base = A.off
        for grp in self.own_supers():
            A.off = base
            S0 = grp[0][0]
            S = sum(t[1] for t in grp)
            a2 = A.bf16(KD * S).rearrange("p (c t) -> p c t", t=S)
            b_a2 = p.buf("a2")
            rst = A.f32(S)
            b_rst = p.buf("rst")
            p.dma("sync", a2, s["a2T"][:, S0:S0 + S].rearrange("(c p) t -> p c t", p=128), b_a2, self.sb["a2T"])
            p.dma("sync", rst, s["rstd2"][:, S0:S0 + S], b_rst, self.sb["rstd2"])
            wg = [A.bf16(KD * c.MB).rearrange("p (c n) -> p c n", n=c.MB) for _ in range(2)]
            wu = [A.bf16(KD * c.MB).rearrange("p (c n) -> p c n", n=c.MB) for _ in range(2)]
            b_wg = p.bufs(2, "wg")
            b_wu = p.bufs(2, "wu")
            r_t1 = Rot([(A.f32(512), p.buf("t1")) for _ in range(2)])
            r_sg = Rot([(A.f32(512), p.buf("sg")) for _ in range(2)])
            r_t3 = Rot([(A.f32(512), p.buf("t3")) for _ in range(2)])
            r_st = Rot([(A.bf16(512), p.buf("st")) for _ in range(3)])
            bg = Rot([0, 1, 2, 3])
            bu = Rot([4, 5, 6, 7])
            wi = 0
            for c0 in range(0, c.DFF, c.MB):
                wj = wi % 2
                wi += 1
                p.dma("pool", wg[wj], d["w_gate_up"][:, c0:c0 + c.MB].rearrange("(c p) n -> p c n", p=128), b_wg[wj])
                p.dma("pool", wu[wj], d["w_gate_up"][:, c.DFF + c0:c.DFF + c0 + c.MB].rearrange("(c p) n -> p c n", p=128), b_wu[wj])
                for sub in range(c.NSUB):
                    m = c0 // 128 + sub
                    tl = 0
                    for (to, tw) in grp:
                        kg, ku = bg.next(), bu.next()
                        for kc in range(KD):
                            self.mm(self.ps[kg][:, 0:tw], wg[wj][:, kc, sub * 128:(sub + 1) * 128], a2[:, kc, tl:tl + tw],
                                    kc == 0, kc == KD - 1, [b_wg[wj], b_a2], [self.psb[kg]])
                        for kc in range(KD):
                            self.mm(self.ps[ku][:, 0:tw], wu[wj][:, kc, sub * 128:(sub + 1) * 128], a2[:, kc, tl:tl + tw],
                                    kc == 0, kc == KD - 1, [b_wu[wj], b_a2], [self.psb[ku]])
                        (t1, b_t1) = r_t1.next()
                        (sg, b_sg) = r_sg.next()
                        (t3, b_t3) = r_t3.next()
                        (st, b_st) = r_st.next()
                        self.tt("dve", t1[:, 0:tw], self.ps[kg][:, 0:tw], rst[:, tl:tl + tw], ALU.mult, [self.psb[kg], b_rst], [b_t1])
                        self.act(sg[:, 0:tw], t1[:, 0:tw], AF.Silu, [b_t1], [b_sg])
                        self.tt("dve", t3[:, 0:tw], self.ps[ku][:, 0:tw], rst[:, tl:tl + tw], ALU.mult, [self.psb[ku], b_rst], [b_t3])
                        self.tt("dve", st[:, 0:tw], sg[:, 0:tw], t3[:, 0:tw], ALU.mult, [b_sg, b_t3], [b_st])
                        p.dma("sync", s["act3T"][m * 128:(m + 1) * 128, to:to + tw], st[:, 0:tw], b_st, self.sb["act3T"], store=True)
                        tl += tw
            p.barrier()

    def phase6(self):
        c, p, A, d, s = self.c, self.p, self.A, self.d, self.s
        KF, KD = c.KF, c.KD
        PC = 32
        pieces = [(c0, min(KF, c0 + PC)) for c0 in range(0, KF, PC)]
        base = A.off
        for (to, tw) in tok_tiles(c.NOWN):
            A.off = base
            a3src = s["act3T"][:, to:to + tw].rearrange("(c p) t -> p c t", p=128)
            a3p, b_a3 = [], []
            for (c0, c1) in pieces:
                t = A.bf16((c1 - c0) * tw).rearrange("p (c t) -> p c t", t=tw)
                b = p.buf("a3")
                p.dma("sync", t, a3src[:, c0:c1, :], b, self.sb["act3T"])
                p.fence("sync", b)
                a3p.append(t)
                b_a3.append(b)
            r_wd = Rot([(A.bf16(PC * 128).rearrange("p (c n) -> p c n", n=128), p.buf("wd")) for _ in range(3)])
            r_hb = Rot([(A.f32(512), p.buf("hb")) for _ in range(3)])
            r_h3 = Rot([(A.f32(512), p.buf("h3")) for _ in range(2)])
            r_os = Rot([(A.f32(512), p.buf("ost")) for _ in range(3)])
            mb = Rot([0, 1, 2, 3])
            tb = Rot([4, 5, 6, 7])
            nb = tw // 128
            for m in range(KD):
                wsrc = d["w_down"][:, m * 128:(m + 1) * 128].rearrange("(c p) n -> p c n", p=128)
                bk = mb.next()
                for pi, (c0, c1) in enumerate(pieces):
                    (wd, b_wd) = r_wd.next()
                    p.dma("pool", wd[:, 0:c1 - c0, :], wsrc[:, c0:c1, :], b_wd)
                    for kc in range(c0, c1):
                        self.mm(self.ps[bk][:, 0:tw], wd[:, kc - c0, :], a3p[pi][:, kc - c0, :], kc == 0, kc == KF - 1,
                                [b_wd, b_a3[pi]], [self.psb[bk]])
                self.flush_pe()
                (hb, b_hb) = r_hb.next()
                (h3, b_h3) = r_h3.next()
                (os_, b_os) = r_os.next()
                p.dma("sync", hb[:, 0:tw], s["h2T"][m * 128:(m + 1) * 128, to:to + tw], b_hb, self.sb["h2T"])
                self.tt("dve", h3[:, 0:tw], self.ps[bk][:, 0:tw], hb[:, 0:tw], ALU.add, [self.psb[bk], b_hb], [b_h3])

                def fin(h3=h3, b_h3=b_h3, os_=os_, b_os=b_os, m=m, to=to, tw=tw, nb=nb):
                    tk = tb.next()
                    for i in range(nb):
                        self.tr(self.ps[tk][:, i * 128:(i + 1) * 128], h3[:, i * 128:(i + 1) * 128], [b_h3], [self.psb[tk]])
                    self.copy("dve", os_[:, 0:tw], self.ps[tk][:, 0:tw], [self.psb[tk]], [b_os])
                    p.dma("sync", self.out[to:to + tw, m * 128:(m + 1) * 128].rearrange("(i p) f -> p i f", p=128),
                          os_[:, 0:tw].rearrange("p (i f) -> p i f", f=128), b_os, self.b_out, store=True)
                self.pe_def.append(fin)
            self.flush_pe()
            p.barrier()


def t5_bucket_np(n):
    n = np.asarray(n)
    nf = np.maximum(n, 16).astype(np.float32)
    log_b = 16 + (np.log(nf / np.float32(16)).astype(np.float32) / np.float32(math.log(128 / 16)) * np.float32(16)).astype(np.int32)
    return np.where(n < 16, n, np.minimum(log_b, 31))


def static_consts(cfg):
    ohc = np.zeros((32, FL), np.float32)
    dist = np.arange(FL) - 127
    bk = t5_bucket_np(np.maximum(dist, 0))
    for j in range(FL):
        if dist[j] >= 0:
            ohc[bk[j], j] = 1.0
    negmask = np.zeros((cfg.H, FL), np.float32)
    negmask[:, :127] = NEG
    return ohc, negmask


def make_in_maps(cfg, inputs, n_batch):
    c = cfg
    f = lambda a: np.ascontiguousarray(np.asarray(a, dtype=np.float32))
    x = np.asarray(inputs["x"], dtype=np.float32)
    meta = f(inputs["meta_tokens"])
    ohc, negmask = static_consts(c)
    shared = {
        "ohc": ohc, "negmask": negmask, "rel_bias": f(inputs["rel_bias"]),
        "ln1_g": f(inputs["ln1_g"][0]), "ln2_g": f(inputs["ln2_g"][0]), "w_in": f(inputs["w_in"][0]),
        "subln_g": f(inputs["subln_g"][0]),
        "lam_re": f(inputs["lam_re"][0]), "lam_im": f(inputs["lam_im"][0]), "log_dt": f(inputs["log_dt"][0]),
        "b_re": f(inputs["b_re"][0]), "b_im": f(inputs["b_im"][0]),
        "c_re": f(inputs["c_re"][0]).reshape(c.G * 16, 64), "c_im": f(inputs["c_im"][0]).reshape(c.G * 16, 64),
        "d_skip": f(inputs["d_skip"][0]), "w_glu": f(inputs["w_glu"][0]), "b_glu": f(inputs["b_glu"][0]),
        "w_branch": f(inputs["w_branch"][0]), "w_o": f(inputs["w_o"][0]),
        "w_gate_up": f(inputs["w_gate_up"][0]), "w_down": f(inputs["w_down"][0]),
    }
    for n in ("q_norm_g", "k_norm_g", "lam_q1", "lam_k1", "lam_q2", "lam_k2"):
        shared[n] = f(inputs[n][0])
    NB = c.NCTX
    maps = []
    for b in range(n_batch):
        hpad = np.zeros((NB * 128, c.D), np.float32)
        hpad[PADF:PADF + N_META] = meta
        hpad[PADF + N_META:] = x[b]
        for half in range(2):
            hp = np.zeros((c.NT, c.D), np.float32)
            kv = np.full((c.NT,), NEG, np.float32)
            if half == 0:
                hp[c.O0:] = hpad[:c.NO]
                kv[c.O0 + PADF:] = 0.0
            else:
                hp[:] = hpad
                kv[PADF:] = 0.0
            m = dict(shared)
            m["hp"] = hp
            m["kvalid"] = np.ascontiguousarray(kv.reshape(c.NCTX, 128).T)
            maps.append(m)
    return maps


_NC_CACHE = {}


def run_cfg(cfg, inputs, n_batch, debug=False, stop_after=99, trace=False):
    key = (id(cfg), debug, stop_after)
    if key not in _NC_CACHE:
        _NC_CACHE[key] = Builder(cfg, debug=debug, stop_after=stop_after).build()
    nc = _NC_CACHE[key]
    maps = make_in_maps(cfg, inputs, n_batch)
    res = run_bass_kernel_spmd(nc, maps, core_ids=list(range(2 * n_batch)), trace=trace)
    return res


def assemble(cfg, res, n_batch, seq):
    c = cfg
    out = np.zeros((n_batch, seq, c.D), np.float32)
    n0 = c.NO - 128
    for b in range(n_batch):
        o0 = res.results[2 * b]["out"]
        o1 = res.results[2 * b + 1]["out"]
        out[b, :n0] = o0[128:]
        out[b, n0:] = o1[c.NO - (seq - n0):]
    return out


FULL = Cfg()


def kernel(**inputs):
    res = run_cfg(FULL, inputs, 4)
    return assemble(FULL, res, 4, 4096)
```
